# Optimizing a Trainium2 kernel written in Bass

```python
import jax, jax.numpy as jnp
from jax import lax
import numpy as np

D_MODEL = 4096
BATCH = 4
SEQ = 2048
DEPTH = 2
DEC_BATCH = 32
DEC_SEQ = 1
PAST_LEN = 16384
PAGE_SIZE = 128

N_MIXERS = 2
N_RET_LAYERS = (DEPTH + 1) // 2
N_ATTN_LAYERS = DEPTH // 2
RET_HEADS = 16
RET_DK = D_MODEL // RET_HEADS
RET_DV = 2 * D_MODEL // RET_HEADS
RET_CHUNK = 128
ATT_HEAD_DIM = 64
ATT_Q_HEADS = D_MODEL // ATT_HEAD_DIM
ATT_KV_HEADS = 8
ATT_GROUP = ATT_Q_HEADS // ATT_KV_HEADS
WINDOW = 128
ATT_BLOCK = 128
D_FF = 4 * D_MODEL
PLE_DIM = 256
RMS_EPS = 1e-6
GN_EPS = 1e-6

kernel_name = 'retention_swa_sink_hybrid_step'


def rms_norm(x, gain):
    xf = x.astype(jnp.float32)
    y = xf * lax.rsqrt(jnp.mean(xf * xf, axis=-1, keepdims=True) + RMS_EPS)
    return (y * gain.astype(jnp.float32)).astype(x.dtype)


def retention_log_decay():
    return jnp.log1p(-jnp.exp2(-5.0 - jnp.arange(RET_HEADS, dtype=jnp.float32)))


def alibi_slopes():
    h = jnp.arange(1, ATT_Q_HEADS + 1, dtype=jnp.float32)
    return jnp.exp2(-8.0 * h / ATT_Q_HEADS).reshape(ATT_KV_HEADS, ATT_GROUP)


def retention_chunk(state, q, k, v, log_g):
    L = q.shape[1]
    idx = jnp.arange(L, dtype=jnp.float32)
    diff = idx[:, None] - idx[None, :]
    causal = diff >= 0
    decay = jnp.where(causal[None], jnp.exp(jnp.where(causal, diff, 0.0)[None] * log_g[:, None, None]), 0.0)
    scores = jnp.einsum('blhd,bmhd->bhlm', q, k) * decay[None]
    out = jnp.einsum('bhlm,bmhe->blhe', scores, v)
    q_decay = jnp.exp((idx + 1.0)[:, None] * log_g[None, :])
    out = out + jnp.einsum('blhd,bhde->blhe', q * q_decay[None, :, :, None], state)
    k_decay = jnp.exp((L - 1.0 - idx)[:, None] * log_g[None, :])
    new_state = jnp.exp(L * log_g)[None, :, None, None] * state + jnp.einsum(
        'blhd,blhe->bhde', k * k_decay[None, :, :, None], v)
    return new_state, out


def retention_mixer(h, w_in, gn_gain, w_out, state):
    B, L, _ = h.shape
    qk_w = RET_HEADS * RET_DK
    v_w = RET_HEADS * RET_DV
    proj = h @ w_in
    q, k, v, g = jnp.split(proj, [qk_w, 2 * qk_w, 2 * qk_w + v_w], axis=-1)
    q = q.reshape(B, L, RET_HEADS, RET_DK).astype(jnp.float32)
    k = k.reshape(B, L, RET_HEADS, RET_DK).astype(jnp.float32) * (RET_DK ** -0.5)
    v = v.reshape(B, L, RET_HEADS, RET_DV).astype(jnp.float32)
    log_g = retention_log_decay()
    if state is None:
        nc = L // RET_CHUNK
        def to_chunks(t):
            return t.reshape(B, nc, RET_CHUNK, *t.shape[2:]).swapaxes(0, 1)
        s0 = jnp.zeros((B, RET_HEADS, RET_DK, RET_DV), jnp.float32)
        def step(s, qkv):
            return retention_chunk(s, qkv[0], qkv[1], qkv[2], log_g)
        s_new, o = lax.scan(step, s0, (to_chunks(q), to_chunks(k), to_chunks(v)))
        o = o.swapaxes(0, 1).reshape(B, L, RET_HEADS, RET_DV)
    else:
        s_new, o = retention_chunk(state.astype(jnp.float32), q, k, v, log_g)
    mu = jnp.mean(o, axis=-1, keepdims=True)
    var = jnp.mean(jnp.square(o - mu), axis=-1, keepdims=True)
    o = ((o - mu) * lax.rsqrt(var + GN_EPS)).reshape(B, L, v_w) * gn_gain.astype(jnp.float32)
    y = (jax.nn.silu(g.astype(jnp.float32)) * o).astype(h.dtype) @ w_out
    return y, s_new.astype(h.dtype)


def attn_project(h, w_qkv, b_qkv, q_norm, k_norm):
    B, L, _ = h.shape
    q_w = ATT_Q_HEADS * ATT_HEAD_DIM
    kv_w = ATT_KV_HEADS * ATT_HEAD_DIM
    proj = h @ w_qkv + b_qkv
    q, k, v = jnp.split(proj, [q_w, q_w + kv_w], axis=-1)
    q = rms_norm(q.reshape(B, L, ATT_KV_HEADS, ATT_GROUP, ATT_HEAD_DIM), q_norm) * (ATT_HEAD_DIM ** -0.5)
    k = rms_norm(k.reshape(B, L, ATT_KV_HEADS, ATT_HEAD_DIM), k_norm)
    v = v.reshape(B, L, ATT_KV_HEADS, ATT_HEAD_DIM)
    return q, k, v


def sink_softmax(scores, dist, valid, slopes, sinks):
    logits = scores - slopes[:, :, None, None] * dist
    logits = jnp.where(valid, logits, -jnp.inf)
    sink = sinks.reshape(ATT_KV_HEADS, ATT_GROUP)[:, :, None, None].astype(jnp.float32)
    m = jnp.maximum(jnp.max(logits, axis=-1, keepdims=True), sink)
    e = jnp.exp(logits - m)
    denom = jnp.sum(e, axis=-1, keepdims=True) + jnp.exp(sink - m)
    return e / denom


def swa_prompt(q, k, v, slopes, sinks):
    B, S = q.shape[:2]
    nb = S // ATT_BLOCK
    qb = q.reshape(B, nb, ATT_BLOCK, ATT_KV_HEADS, ATT_GROUP, ATT_HEAD_DIM)
    kb = k.reshape(B, nb, ATT_BLOCK, ATT_KV_HEADS, ATT_HEAD_DIM)
    vb = v.reshape(B, nb, ATT_BLOCK, ATT_KV_HEADS, ATT_HEAD_DIM)
    def prev(t):
        return jnp.concatenate([jnp.zeros_like(t[:, :1]), t[:, :-1]], axis=1)
    kk = jnp.concatenate([prev(kb), kb], axis=2)
    vv = jnp.concatenate([prev(vb), vb], axis=2)
    scores = jnp.einsum('bnqhgd,bnkhd->bnhgqk', qb, kk).astype(jnp.float32)
    qpos = ATT_BLOCK + jnp.arange(ATT_BLOCK)
    kpos = jnp.arange(2 * ATT_BLOCK)
    dist = qpos[:, None] - kpos[None, :]
    in_window = (dist >= 0) & (dist <= WINDOW)
    has_prev = (jnp.arange(nb) > 0)[:, None, None] | (kpos >= ATT_BLOCK)[None, None, :]
    valid = (in_window[None] & has_prev)[:, None, None]
    probs = sink_softmax(scores, dist.astype(jnp.float32), valid, slopes, sinks)
    out = jnp.einsum('bnhgqk,bnkhd->bnqhgd', probs.astype(v.dtype), vv)
    return out.reshape(B, S, ATT_Q_HEADS * ATT_HEAD_DIM)


def swa_sample(q, k, v, cache_k, cache_v, slopes, sinks):
    B, L = q.shape[:2]
    wb = cache_k.shape[1]
    kk = jnp.concatenate([cache_k, k], axis=1)
    vv = jnp.concatenate([cache_v, v], axis=1)
    qpos = PAST_LEN + jnp.arange(L)
    kpos = jnp.concatenate([PAST_LEN - wb + jnp.arange(wb), PAST_LEN + jnp.arange(L)])
    dist = qpos[:, None] - kpos[None, :]
    valid = (dist >= 0) & (dist <= WINDOW)
    scores = jnp.einsum('blhgd,bkhd->bhglk', q, kk).astype(jnp.float32)
    probs = sink_softmax(scores, dist.astype(jnp.float32), valid, slopes, sinks)
    out = jnp.einsum('bhglk,bkhd->blhgd', probs.astype(v.dtype), vv)
    return out.reshape(B, L, ATT_Q_HEADS * ATT_HEAD_DIM), kk[:, -wb:], vv[:, -wb:]


def channel_mixer(h, w_up, w_down):
    return jnp.square(jax.nn.relu(h @ w_up)) @ w_down


def per_layer_embed(r, p_i, gain, w_gate, w_proj):
    h = rms_norm(r, gain)
    gate = jax.nn.sigmoid((h @ w_gate).astype(jnp.float32))
    return r + (gate * (p_i @ w_proj).astype(jnp.float32)).astype(r.dtype)


def setup_inputs(seed: int = 0) -> dict:
    key = jax.random.key(seed)
    ks = jax.random.split(key, 24)
    f32 = jnp.float32
    w_buf = min(WINDOW, PAST_LEN)
    ret_in = RET_HEADS * (2 * RET_DK + 2 * RET_DV)
    att_in = (ATT_Q_HEADS + 2 * ATT_KV_HEADS) * ATT_HEAD_DIM
    def nrm(k, shape, scale=1.0):
        return jax.random.normal(k, shape, f32) * scale
    def gain(k, shape):
        return 1.0 + 0.02 * jax.random.normal(k, shape, f32)
    return {
        'x_prompt': nrm(ks[0], (BATCH, SEQ, D_MODEL)),
        'x_sample': nrm(ks[1], (DEC_BATCH, DEC_SEQ, D_MODEL)),
        'p_prompt': nrm(ks[2], (DEPTH, BATCH, SEQ, PLE_DIM)),
        'p_sample': nrm(ks[3], (DEPTH, DEC_BATCH, DEC_SEQ, PLE_DIM)),
        'state_ret': nrm(ks[4], (N_RET_LAYERS, DEC_BATCH, RET_HEADS, RET_DK, RET_DV), 0.5),
        'cache_k': nrm(ks[5], (N_ATTN_LAYERS, DEC_BATCH, w_buf, ATT_KV_HEADS, ATT_HEAD_DIM)),
        'cache_v': nrm(ks[6], (N_ATTN_LAYERS, DEC_BATCH, w_buf, ATT_KV_HEADS, ATT_HEAD_DIM)),
        'norm_mix': gain(ks[7], (DEPTH, D_MODEL)),
        'norm_mlp': gain(ks[8], (DEPTH, D_MODEL)),
        'norm_ple': gain(ks[9], (DEPTH, D_MODEL)),
        'ret_w_in': nrm(ks[10], (N_RET_LAYERS, D_MODEL, ret_in), D_MODEL ** -0.5),
        'ret_gn_gain': gain(ks[11], (N_RET_LAYERS, RET_HEADS * RET_DV)),
        'ret_w_out': nrm(ks[12], (N_RET_LAYERS, RET_HEADS * RET_DV, D_MODEL), (RET_HEADS * RET_DV) ** -0.5),
        'attn_w_qkv': nrm(ks[13], (N_ATTN_LAYERS, D_MODEL, att_in), D_MODEL ** -0.5),
        'attn_b_qkv': nrm(ks[14], (N_ATTN_LAYERS, att_in), 0.02),
        'attn_q_norm': gain(ks[15], (N_ATTN_LAYERS, ATT_HEAD_DIM)),
        'attn_k_norm': gain(ks[16], (N_ATTN_LAYERS, ATT_HEAD_DIM)),
        'attn_sinks': nrm(ks[17], (N_ATTN_LAYERS, ATT_Q_HEADS), 0.5),
        'attn_w_out': nrm(ks[18], (N_ATTN_LAYERS, ATT_Q_HEADS * ATT_HEAD_DIM, D_MODEL), (ATT_Q_HEADS * ATT_HEAD_DIM) ** -0.5),
        'mlp_w_up': nrm(ks[19], (DEPTH, D_MODEL, D_FF), D_MODEL ** -0.5),
        'mlp_w_down': nrm(ks[20], (DEPTH, D_FF, D_MODEL), D_FF ** -0.5),
        'ple_w_gate': nrm(ks[21], (DEPTH, D_MODEL, D_MODEL), D_MODEL ** -0.5),
        'ple_w_proj': nrm(ks[22], (DEPTH, PLE_DIM, D_MODEL), PLE_DIM ** -0.5),
    }


def reference(x_prompt, x_sample, p_prompt, p_sample, state_ret, cache_k, cache_v,
              norm_mix, norm_mlp, norm_ple, ret_w_in, ret_gn_gain, ret_w_out,
              attn_w_qkv, attn_b_qkv, attn_q_norm, attn_k_norm, attn_sinks, attn_w_out,
              mlp_w_up, mlp_w_down, ple_w_gate, ple_w_proj):
    slopes = alibi_slopes()
    rp, rs = x_prompt, x_sample
    ret_p, ret_s, kbuf_p, vbuf_p, kbuf_s, vbuf_s = [], [], [], [], [], []
    for i in range(DEPTH):
        j = i // N_MIXERS
        hp = rms_norm(rp, norm_mix[i])
        hs = rms_norm(rs, norm_mix[i])
        if i % N_MIXERS == 0:
            yp, sp = retention_mixer(hp, ret_w_in[j], ret_gn_gain[j], ret_w_out[j], None)
            ys, ss = retention_mixer(hs, ret_w_in[j], ret_gn_gain[j], ret_w_out[j], state_ret[j])
            ret_p.append(sp)
            ret_s.append(ss)
        else:
            qp, kp, vp = attn_project(hp, attn_w_qkv[j], attn_b_qkv[j], attn_q_norm[j], attn_k_norm[j])
            yp = swa_prompt(qp, kp, vp, slopes, attn_sinks[j]) @ attn_w_out[j]
            kbuf_p.append(kp[:, -WINDOW:])
            vbuf_p.append(vp[:, -WINDOW:])
            qs, ks_, vs_ = attn_project(hs, attn_w_qkv[j], attn_b_qkv[j], attn_q_norm[j], attn_k_norm[j])
            os_, kb, vb = swa_sample(qs, ks_, vs_, cache_k[j], cache_v[j], slopes, attn_sinks[j])
            ys = os_ @ attn_w_out[j]
            kbuf_s.append(kb)
            vbuf_s.append(vb)
        rp = rp + yp
        rs = rs + ys
        rp = rp + channel_mixer(rms_norm(rp, norm_mlp[i]), mlp_w_up[i], mlp_w_down[i])
        rs = rs + channel_mixer(rms_norm(rs, norm_mlp[i]), mlp_w_up[i], mlp_w_down[i])
        rp = per_layer_embed(rp, p_prompt[i], norm_ple[i], ple_w_gate[i], ple_w_proj[i])
        rs = per_layer_embed(rs, p_sample[i], norm_ple[i], ple_w_gate[i], ple_w_proj[i])
    return (rp, rs, jnp.stack(ret_p), jnp.stack(ret_s), jnp.stack(kbuf_p), jnp.stack(vbuf_p), jnp.stack(kbuf_s), jnp.stack(vbuf_s))
```

```python
import math
from contextlib import ExitStack

import numpy as np
import concourse.bass as bass
import concourse.mybir as mybir
from concourse.bass_utils import run_bass_kernel_spmd

F32 = mybir.dt.float32
BF16 = mybir.dt.bfloat16
AF = mybir.ActivationFunctionType
ALU = mybir.AluOpType
AX = mybir.AxisListType

RMS_EPS = 1e-6
GN_EPS = 1e-6
NEG = -30000.0


class Cfg:
    def __init__(self, D=4096, NH=16, NKV=8, DFF=16384, SEQ=2048, B=4, DECB=32, PAST=16384):
        self.D = D
        self.KC = D // 128
        self.NH = NH
        assert NH * 256 == D
        self.NQ = D // 64
        self.NKV = NKV
        self.G = self.NQ // NKV
        self.KVW = NKV * 64
        assert self.KVW % 128 == 0
        self.KVC = self.KVW // 128
        self.DFF = DFF
        self.FC = DFF // 128
        self.PLE = 256
        self.SEQ = SEQ
        self.B = B
        self.DECB = DECB
        self.OWN = SEQ // 2
        self.NTO = self.OWN // 128
        self.TM = 128 + self.OWN
        self.NTM = 1 + self.NTO
        self.NS = DECB // 8
        self.T = self.TM + self.NS
        self.CTX = self.OWN - 128
        self.NTC = self.CTX // 128
        self.KH = min(8, self.KC)
        self.WIN = D * 6

    def cgs(self, T):
        out = []
        c = 0
        while c < T:
            out.append((c, min(T, c + 512)))
            c += 512
        return out


class Res:
    __slots__ = ("w", "rs")

    def __init__(self):
        self.w = None
        self.rs = {}


def RL(n):
    return [Res() for _ in range(n)]


class Prog:
    NDS = 56

    def __init__(self, nc, es):
        self.nc = nc
        self.es = es
        self.E = {"pe": nc.tensor, "act": nc.scalar, "dve": nc.vector, "pool": nc.gpsimd, "sp": nc.sync}
        self.cur = {}
        self.known = {e: {} for e in self.E}
        self.dsems = []
        self.dptr = 0
        self.nsem = 0
        self.allsems = []
        self.last = {}
        self.pending = []

    def _newsem(self):
        self.nsem += 1
        s = self.es.enter_context(self.nc.semaphore(f"s{self.nsem}"))
        self.allsems.append(s)
        return s

    def _ticket(self, eng, instr):
        c = self.cur.get(eng)
        if c is None or c[1] >= 30000:
            c = [self._newsem(), 0]
            self.cur[eng] = c
        c[1] += 1
        instr.then_inc(c[0], 1)
        t = (c[0], c[1], eng)
        self.last[id(c[0])] = t
        return t

    def _wait(self, eng, deps):
        k = self.known[eng]
        best = {}
        for d in deps:
            sem, val, src = d
            if src == "pe" and eng == "pe":
                continue
            sid = id(sem)
            if k.get(sid, 0) >= val:
                continue
            if sid not in best or best[sid][1] < val:
                best[sid] = (sem, val)
        for sid, (sem, val) in best.items():
            self.E[eng].wait_ge(sem, val)
            k[sid] = val

    @staticmethod
    def _deps(reads, writes):
        d = []
        for r in reads:
            if r.w is not None:
                d.append(r.w)
        for w in writes:
            if w.w is not None:
                d.append(w.w)
            d.extend(w.rs.values())
        return d

    @staticmethod
    def _reg(t, reads, writes):
        key = id(t[0])
        for r in reads:
            old = r.rs.get(key)
            if old is None or old[1] < t[1]:
                r.rs[key] = t
        for w in writes:
            w.w = t
            w.rs = {}

    def op(self, eng, fn, reads=(), writes=()):
        self._wait(eng, self._deps(reads, writes))
        instr = fn(self.E[eng])
        t = self._ticket(eng, instr)
        self._reg(t, reads, writes)
        return t

    def dma(self, out, in_, reads=(), writes=(), q="sp", **kw):
        deps = self._deps(reads, writes)
        if len(self.dsems) < self.NDS:
            self.dsems.append([self._newsem(), 0])
            slot = self.dsems[-1]
        else:
            slot = self.dsems[self.dptr % self.NDS]
        self.dptr += 1
        if slot[1] > 0:
            deps.append((slot[0], slot[1], "dma"))
        self._wait(q, deps)
        slot[1] += 16
        self.E[q].dma_start(out=out, in_=in_, **kw).then_inc(slot[0], 16)
        t = (slot[0], slot[1], "dma")
        self.last[id(slot[0])] = t
        self._reg(t, reads, writes)
        return t

    def defer_dma(self, out, in_, reads=(), writes=(), **kw):
        self.pending.append((out, in_, list(reads), list(writes), kw))

    def flush(self):
        pend, self.pending = self.pending, []
        for (out, in_, reads, writes, kw) in pend:
            self.dma(out, in_, reads=reads, writes=writes, **kw)

    def barrier(self, engines=("pe", "act", "dve", "pool", "sp")):
        self.flush()
        deps = list(self.last.values())
        for e in engines:
            self._wait(e, deps)


class Rot:
    def __init__(self, items):
        self.items = items
        self.i = 0

    def next(self):
        it = self.items[self.i % len(self.items)]
        self.i += 1
        return it


def build(cfg, dbg=None):
    nc = bass.Bass("TRN2", target_bir_lowering=False)
    es = ExitStack()
    P = Prog(nc, es)
    D, KC, NH, T, TM, NTM, NS, KH = cfg.D, cfg.KC, cfg.NH, cfg.T, cfg.TM, cfg.NTM, cfg.NS, cfg.KH
    OWN, NTO, CTX, NTC, NQ, NKV, G, KVW, KVC = cfg.OWN, cfg.NTO, cfg.CTX, cfg.NTC, cfg.NQ, cfg.NKV, cfg.G, cfg.KVW, cfg.KVC
    DFF, FC = cfg.DFF, cfg.FC
    cgsT = cfg.cgs(T)
    NCG = len(cgsT)

    def din(name, shape, dt=F32):
        return nc.dram_tensor(name, list(shape), dt, kind="ExternalInput").ap()

    def dout(name, shape, dt=F32):
        return nc.dram_tensor(name, list(shape), dt, kind="ExternalOutput").ap()

    def dscr(name, shape, dt=F32):
        return nc.dram_tensor(name, list(shape), dt, kind="Internal").ap()

    x_main = din("x_main", [TM, D])
    x_ctx = din("x_ctx", [max(CTX, 1), D])
    x_smp = din("x_smp", [NS, D])
    p_main = din("p_main", [2, TM, 256])
    p_smp = din("p_smp", [2, NS, 256])
    state_in = din("state_in", [NS, NH, 256, 512])
    cache_k = din("cache_k", [NS, 128, KVW])
    cache_v = din("cache_v", [NS, 128, KVW])
    norm_mix = din("norm_mix", [2, D])
    norm_mlp = din("norm_mlp", [2, D])
    norm_ple = din("norm_ple", [2, D])
    ret_w_in = din("ret_w_in", [D, 6 * D])
    ret_gn = din("ret_gn", [2 * D])
    ret_w_out = din("ret_w_out", [2 * D, D])
    att_w_qkv = din("att_w_qkv", [D, D + 2 * KVW])
    att_b = din("att_b", [D + 2 * KVW])
    att_qn = din("att_qn", [64])
    att_kn = din("att_kn", [64])
    att_sinks = din("att_sinks", [NQ])
    att_w_out = din("att_w_out", [D, D])
    mlp_up = din("mlp_up", [2, D, DFF])
    mlp_down = din("mlp_down", [2, DFF, D])
    ple_gate = din("ple_gate", [2, D, D])
    ple_proj = din("ple_proj", [2, 256, D])
    c_ident = din("c_ident", [128, 128])
    c_decT = din("c_decT", [NH, 128, 128])
    c_qdec = din("c_qdec", [NH, 128])
    c_kdec = din("c_kdec", [128, NH])
    c_negdist = din("c_negdist", [128, 256])
    c_mask = din("c_mask", [128, 256])
    c_first = din("c_first", [128, 1])
    c_sdist = din("c_sdist", [128, 1])

    y_main = dout("y_main", [OWN, D])
    y_smp = dout("y_smp", [NS, D])
    st_p = dout("st_p", [NH, 256, 512])
    st_s = dout("st_s", [NS, NH, 256, 512])
    ck_p = dout("ck_p", [128, KVW])
    cv_p = dout("cv_p", [128, KVW])
    ck_s = dout("ck_s", [NS, 128, KVW])
    cv_s = dout("cv_s", [NS, 128, KVW])

    R = dscr("R", [KC, 128, T])
    O = dscr("O", [2 * KC, 128, T], BF16)
    H = dscr("H", [FC, 128, T], BF16)
    PP = dscr("PP", [KC, 128, T])
    S7 = dscr("S7", [NH, 256, 512])
    R_res = RL(KC)
    O_res = RL(2 * KC)
    H_res = RL(FC)
    PP_res = RL(KC)
    S7_res = RL(NH)
    dbg_out = {}
    if dbg:
        for name in dbg:
            if name.startswith("R"):
                dbg_out[name] = dout("dbg_" + name, [KC, 128, T])

    sbn = [0]

    def sb(name, shape, dt=F32, stack=es):
        sbn[0] += 1
        return stack.enter_context(nc.sbuf_tensor(f"{name}_{sbn[0]}", list(shape), dt))

    A = sb("A", [128, KC, T], BF16)
    A_res = RL(KC)
    NWS, NWB = 2, 3
    wst = [sb(f"wst{i}", [128, KH, 256]) for i in range(NWS)]
    wst_res = RL(NWS)
    wbf = [sb(f"wbf{i}", [128, KH, 256], BF16) for i in range(NWB)]
    wbf_res = RL(NWB)
    rstd = sb("rstd", [128, T])
    rstd_res = Res()
    ident_f = sb("ident_f", [128, 128])
    ident_b = sb("ident_b", [128, 128], BF16)
    ones_f = sb("ones_f", [128, 128])
    cres = Res()
    ps = [es.enter_context(nc.psum_tensor(f"ps{i}", [128, 512], F32)) for i in range(8)]
    ps_res = RL(8)
    auxrot = [0]

    def aux_bank():
        auxrot[0] += 1
        return 6 + (auxrot[0] % 2)

    P.dma(ident_f[:, :], c_ident[:, :], writes=[cres])
    P.op("dve", lambda e: e.tensor_copy(out=ident_b[:, :], in_=ident_f[:, :]), writes=[cres])
    P.op("dve", lambda e: e.memset(ones_f[:, :], 1.0), writes=[cres])

    class Job:
        def __init__(self, W, row0, panels, KCj):
            self.W, self.row0, self.panels, self.KCj = W, row0, panels, KCj

    def full_panels(N):
        return [(c, 2) for c in range(0, N, 256)]

    jobs = []

    def ret_job(h, full):
        pn = []
        if full:
            pn.append((h * 256, 2))
        pn.append((D + h * 256, 2))
        pn += [(2 * D + h * 512, 2), (2 * D + h * 512 + 256, 2)]
        if full:
            pn += [(4 * D + h * 512, 2), (4 * D + h * 512 + 256, 2)]
        return Job(ret_w_in, 0, pn, KC)

    def layer_tail_jobs(i):
        js = []
        js.append(Job(mlp_up[i], 0, full_panels(DFF), KC))
        for p_ in range(DFF // D):
            js.append(Job(mlp_down[i], p_ * D, full_panels(D), KC))
        js.append(Job(ple_proj[i], 0, full_panels(D), 2))
        js.append(Job(ple_gate[i], 0, full_panels(D), KC))
        return js

    if NTC > 0:
        for h in range(NH):
            jobs.append(ret_job(h, False))
    for h in range(NH):
        jobs.append(ret_job(h, True))
    for p_ in range(2):
        jobs.append(Job(ret_w_out, p_ * D, full_panels(D), KC))
    jobs += layer_tail_jobs(0)
    qkv_panels = []
    for c in range(D, D + 2 * KVW, 256):
        qkv_panels.append((c, min(2, (D + 2 * KVW - c) // 128)))
    qkv_panels += full_panels(D)
    jobs.append(Job(att_w_qkv, 0, qkv_panels, KC))
    jobs.append(Job(att_w_out, 0, full_panels(D), KC))
    jobs += layer_tail_jobs(1)

    tiles = []
    for jb in jobs:
        khj = min(KH, jb.KCj)
        for (c0, nch) in jb.panels:
            for hk in range(jb.KCj // khj):
                r0 = jb.row0 + hk * khj * 128
                tiles.append((jb.W[r0:r0 + khj * 128, c0:c0 + nch * 128], khj, nch * 128))

    class WS:
        def __init__(self):
            self.emitted = 0
            self.ptr = 0

        def _load(self, i):
            ap, kh, ncol = tiles[i]
            s = i % NWS
            P.dma(wst[s][:, 0:kh, 0:ncol], ap.rearrange("(k p) n -> p k n", p=128), writes=[wst_res[s]])

        def _conv(self, i):
            ap, kh, ncol = tiles[i]
            s, b = i % NWS, i % NWB
            eng = "pool"
            P.op(eng, lambda e: e.tensor_copy(out=wbf[b][:, 0:kh, 0:ncol], in_=wst[s][:, 0:kh, 0:ncol]),
                 reads=[wst_res[s]], writes=[wbf_res[b]])

        def get(self):
            i = self.ptr
            while self.emitted < min(len(tiles), i + 2):
                j = self.emitted
                self._load(j)
                self.emitted += 1
            self.ptr += 1
            return i

    ws_state = {"ld": 0, "cv": 0, "ptr": 0}

    def ws_emit_load():
        j = ws_state["ld"]
        ap, kh, ncol = tiles[j]
        s = j % NWS
        P.dma(wst[s][:, 0:kh, 0:ncol], ap.rearrange("(k p) n -> p k n", p=128), writes=[wst_res[s]])
        ws_state["ld"] += 1

    def ws_emit_conv():
        jj = ws_state["cv"]
        _, kh2, nc2 = tiles[jj]
        s2, b2 = jj % NWS, jj % NWB
        if jj % 2 == 0:
            P.op("act", lambda e: e.activation(out=wbf[b2][:, 0:kh2, 0:nc2], in_=wst[s2][:, 0:kh2, 0:nc2], func=AF.Copy),
                 reads=[wst_res[s2]], writes=[wbf_res[b2]])
        else:
            P.op("pool", lambda e: e.tensor_copy(out=wbf[b2][:, 0:kh2, 0:nc2], in_=wst[s2][:, 0:kh2, 0:nc2]),
                 reads=[wst_res[s2]], writes=[wbf_res[b2]])
        ws_state["cv"] += 1

    def ws_advance(upto):
        n = len(tiles)
        while ws_state["ld"] < min(n, upto) or ws_state["cv"] < min(n, upto):
            if ws_state["ld"] < min(n, upto) and ws_state["ld"] <= ws_state["cv"] + 1 - 1 + 1 and ws_state["ld"] - ws_state["cv"] < NWS:
                ws_emit_load()
            elif ws_state["cv"] < ws_state["ld"]:
                ws_emit_conv()
            else:
                break

    def ws_get():
        i = ws_state["ptr"]
        ws_advance(i + 2)
        P.flush()
        ws_state["ptr"] += 1
        return wbf[i % NWB], wbf_res[i % NWB]

    def ws_prefetch():
        ws_advance(ws_state["ptr"] + 3)

    job_ptr = [0]

    def gemm(At, Ares, cgs, epi, expect_W=None):
        jb = jobs[job_ptr[0]]
        job_ptr[0] += 1
        if expect_W is not None:
            assert jb.W.tensor.name == expect_W.tensor.name, (jb.W.tensor.name, expect_W.tensor.name)
        KCj = jb.KCj
        khj = min(KH, KCj)
        nh = KCj // khj
        ncg = len(cgs)
        jglob = 0
        for (c0, nch) in jb.panels:
            for hk in range(nh):
                wt, wres = ws_get()
                for ch in range(nch):
                    banks = [ch * 3 + ci for ci in range(ncg)]

                    def fn(e, wt=wt, ch=ch, hk=hk, banks=banks):
                        last = None
                        for k in range(khj):
                            kc = hk * khj + k
                            for ci, (a0, a1) in enumerate(cgs):
                                last = e.matmul(ps[banks[ci]][:, 0:a1 - a0], lhsT=wt[:, k, ch * 128:(ch + 1) * 128],
                                                rhs=At[:, kc, a0:a1], start=(kc == 0), stop=(kc == KCj - 1))
                        return last
                    P.op("pe", fn, reads=[wres] + [Ares[hk * khj + k] for k in range(khj)],
                         writes=[ps_res[b] for b in banks])
                    if hk == nh - 1:
                        epi(jglob + ch, c0 + ch * 128, banks)
            jglob += nch
        P.flush()
        ws_prefetch()

    def load_xT(xap, nrows_total, col0, stack):
        xt = [sb(f"xt{i}", [128, D], stack=stack) for i in range(2)]
        xt_res = RL(2)
        xs = [sb(f"xs{i}", [128, 4, 128], stack=stack) for i in range(2)]
        xs_res = RL(2)
        it = 0
        r = 0
        while r < nrows_total:
            n = min(128, nrows_total - r)
            s = it % 2
            P.dma(xt[s][0:n, :], xap[r:r + n, :], writes=[xt_res[s]])
            for g0 in range(0, KC, 4):
                gn = min(4, KC - g0)
                bk = aux_bank()
                u = (it * ((KC + 3) // 4) + g0 // 4) % 2

                def fn(e, s=s, n=n, g0=g0, gn=gn, bk=bk):
                    last = None
                    for i in range(gn):
                        last = e.transpose(out=ps[bk][:, i * 128:i * 128 + n], in_=xt[s][0:n, (g0 + i) * 128:(g0 + i + 1) * 128],
                                           identity=ident_f[0:n, 0:n])
                    return last
                P.op("pe", fn, reads=[xt_res[s], cres], writes=[ps_res[bk]])
                P.op("act", lambda e, u=u, n=n, gn=gn, bk=bk: e.activation(
                    out=xs[u][:, 0:gn, 0:n], in_=ps[bk][:, 0:gn * 128].rearrange("p (g c) -> p g c", g=gn)[:, :, 0:n], func=AF.Copy),
                    reads=[ps_res[bk]], writes=[xs_res[u]])
                P.dma(R[g0:g0 + gn, :, col0 + r:col0 + r + n].rearrange("k p c -> p k c"), xs[u][:, 0:gn, 0:n],
                      reads=[xs_res[u]], writes=R_res[g0:g0 + gn])
            r += n
            it += 1

    def norm_phase(gain_ap, Tn, cgs, stack, Rsrc=R, Rres=R_res):
        gcol = sb("gcol", [128, KC], stack=stack)
        gres = Res()
        P.dma(gcol[:, :], gain_ap.rearrange("(k p) -> p k", p=128), writes=[gres], allow_slow_non_contiguous=True)
        rb = [sb(f"rb{i}", [128, T], stack=stack) for i in range(2)]
        rb_res = RL(2)
        sq = [sb(f"sq{i}", [128, T], stack=stack) for i in range(2)]
        sq_res = RL(2)
        for kc in range(KC):
            s = kc % 2
            P.dma(rb[s][:, 0:Tn], Rsrc[kc, :, 0:Tn], reads=[Rres[kc]], writes=[rb_res[s]])
            P.op("act", lambda e, s=s, kc=kc: e.activation(out=A[:, kc, 0:Tn], in_=rb[s][:, 0:Tn], func=AF.Copy,
                                                           scale=gcol[:, kc:kc + 1]),
                 reads=[rb_res[s], gres], writes=[A_res[kc]])
            P.op("dve", lambda e, s=s: e.tensor_tensor(out=sq[s][:, 0:Tn], in0=rb[s][:, 0:Tn], in1=rb[s][:, 0:Tn], op=ALU.mult),
                 reads=[rb_res[s]], writes=[sq_res[s]])

            def fn(e, s=s, kc=kc):
                last = None
                for ci, (a0, a1) in enumerate(cgs):
                    last = e.matmul(ps[ci][:, 0:a1 - a0], lhsT=ones_f[:, :], rhs=sq[s][:, a0:a1], start=(kc == 0), stop=(kc == KC - 1))
                return last
            P.op("pe", fn, reads=[sq_res[s], cres], writes=[ps_res[ci] for ci in range(len(cgs))])
        for ci, (a0, a1) in enumerate(cgs):
            P.op("act", lambda e, ci=ci, a0=a0, a1=a1: e.activation(out=rstd[:, a0:a1], in_=ps[ci][:, 0:a1 - a0], func=AF.Ln,
                                                                    scale=1.0 / D, bias=eps_col[:, 0:1]),
                 reads=[ps_res[ci], cres], writes=[rstd_res])
        P.op("act", lambda e: e.activation(out=rstd[:, 0:Tn], in_=rstd[:, 0:Tn], func=AF.Exp, scale=-0.5), writes=[rstd_res])

    eps_col = sb("eps_col", [128, 1])
    P.op("dve", lambda e: e.memset(eps_col[:, :], RMS_EPS), writes=[cres])

    def resid_epi_factory(stack, tag, scale_rstd=False):
        rr = [sb(f"rr{tag}{i}", [128, T], stack=stack) for i in range(2)]
        rr_res = RL(2)
        cnt = [0]

        def epi(j, col, banks):
            s = cnt[0] % 2
            cnt[0] += 1
            P.dma(rr[s][:, :], R[j, :, :], reads=[R_res[j]], writes=[rr_res[s]])
            for ci, (a0, a1) in enumerate(cgsT):
                eng = "dve"
                P.op(eng, lambda e, s=s, ci=ci, a0=a0, a1=a1, b=banks[ci]: e.tensor_tensor(
                    out=rr[s][:, a0:a1], in0=ps[b][:, 0:a1 - a0], in1=rr[s][:, a0:a1], op=ALU.add),
                    reads=[ps_res[banks[ci]]], writes=[rr_res[s]])
            P.defer_dma(R[j, :, :], rr[s][:, :], reads=[rr_res[s]], writes=[R_res[j]])
        return epi

    def dump_R(name):
        if name in dbg_out:
            with ExitStack() as st:
                tb = sb("dbgt", [128, T], stack=st)
                tr = Res()
                for kc in range(KC):
                    P.dma(tb[:, :], R[kc, :, :], reads=[R_res[kc]], writes=[tr])
                    P.dma(dbg_out[name][kc, :, :], tb[:, :], reads=[tr], writes=[Res()])
                P.barrier()

    loggam = [math.log1p(-2.0 ** (-5 - h)) for h in range(NH)]

    def retention_pass(full, Tn, ntiles, stack):
        cgs = cfg.cgs(Tn)
        gm = lambda h: math.exp(loggam[h])
        g128 = lambda h: math.exp(128.0 * loggam[h])
        nt_all = ntiles + (1 if full else 0)
        kT = sb("kT", [128, 2, Tn], BF16, stack=stack)
        kT_res = Res()
        kd = sb("kd", [128, nt_all, 256], BF16, stack=stack)
        kd_res = Res()
        vtm = sb("vtm", [128, nt_all, 512], BF16, stack=stack)
        vtm_res = Res()
        fT = [sb(f"fT{i}", [128, Tn], BF16, stack=stack) for i in range(2)]
        fT_res = RL(2)
        Sf = sb("Sf", [128, 2, 512], stack=stack)
        Sf_res = Res()
        Sb = sb("Sb", [128, 2, 512], BF16, stack=stack)
        Sb_res = Res()
        kdec = sb("kdec", [128, NH], stack=stack)
        kdec_res = Res()
        P.dma(kdec[:, :], c_kdec[:, :], writes=[kdec_res])
        if full:
            qT = sb("qT", [128, 2, Tn], BF16, stack=stack)
            qT_res = Res()
            qdT = sb("qdT", [128, 2, TM], BF16, stack=stack)
            qdT_res = Res()
            sg = sb("sg", [128, nt_all, 512], BF16, stack=stack)
            sg_res = Res()
            oT = sb("oT", [128, 4, Tn], BF16, stack=stack)
            oT_res = Res()
            decT = [sb(f"decT{i}", [128, 128], stack=stack) for i in range(2)]
            decT_res = RL(2)
            qdec = [sb(f"qdec{i}", [128, 128], stack=stack) for i in range(2)]
            qdec_res = RL(2)
            gncol = sb("gncol", [128, 2 * KC], stack=stack)
            P.dma(gncol[:, :], ret_gn.rearrange("(k p) -> p k", p=128), writes=[kdec_res], allow_slow_non_contiguous=True)
            sT = [sb(f"sT{i}", [128, 128], BF16, stack=stack) for i in range(2)]
            sT_res = RL(2)
            o1 = [sb(f"o1{i}", [128, 512], stack=stack) for i in range(2)]
            o1_res = RL(2)
            og = [sb(f"og{i}", [128, 512], BF16, stack=stack) for i in range(2)]
            og_res = RL(2)
            st6 = [sb(f"st6{i}", [128, 6], stack=stack) for i in range(2)]
            mv = [sb(f"mv{i}", [128, 2], stack=stack) for i in range(2)]
            rs_o = [sb(f"rso{i}", [128, 1], stack=stack) for i in range(2)]
            stat_res = RL(2)
            rs_res = RL(2)
            nbias = [sb(f"nbias{i}", [128, 1], stack=stack) for i in range(2)]
            nb_res = RL(2)
            Ss = [sb(f"Ss{i}", [128, 2, 512], stack=stack) for i in range(2)]
            Ss_res = RL(2)
            Ssb = [sb(f"Ssb{i}", [128, 2, 512], BF16, stack=stack) for i in range(2)]
            Ssb_res = RL(2)
            Sso = [sb(f"Sso{i}", [128, 2, 512], stack=stack) for i in range(2)]
            Sso_res = RL(2)
            qm = [sb(f"qm{i}", [128, 2, NS], BF16, stack=stack) for i in range(2)]
            qm_res = RL(2)
            km = [sb(f"km{i}", [NS, 256], BF16, stack=stack) for i in range(2)]
            km_res = RL(2)
            geps = sb("geps", [128, 1], stack=stack)
            P.op("dve", lambda e: e.memset(geps[:, :], GN_EPS), writes=[kdec_res])
        fcnt = [0]
        ccnt = [0]

        def tile_rows(t):
            return NS if (full and t == ntiles) else 128

        def tile_cols(t):
            if full and t == ntiles:
                return (TM, T)
            return (t * 128, (t + 1) * 128)

        def transposes_to(srcT, src_res, dst, dst_res, dst_c0, evac):
            t = 0
            while t < nt_all:
                grp = []
                while t < nt_all and len(grp) < 8 and tile_rows(t) == 128:
                    grp.append(t)
                    t += 1
                if not grp:
                    grp = [t]
                    t += 1
                bk = aux_bank()
                pb = ps[bk].bitcast(BF16)

                def fn(e, grp=grp, pb=pb):
                    last = None
                    for i, tt in enumerate(grp):
                        a0, a1 = tile_cols(tt)
                        n = a1 - a0
                        last = e.transpose(out=pb[0:n, i * 128:(i + 1) * 128], in_=srcT(a0, a1), identity=ident_b[:, :])
                    return last
                P.op("pe", fn, reads=[src_res, cres], writes=[ps_res[bk]])
                n = tile_rows(grp[0])
                evac(grp, pb, bk, n)

        for h in range(NH):
            hb = h % 2
            if full:
                P.dma(decT[hb][:, :], c_decT[h, :, :], writes=[decT_res[hb]])
                P.dma(qdec[hb][:, :], c_qdec[h, :].partition_broadcast(128), writes=[qdec_res[hb]])
                P.dma(Sf[:, :, :], S7[h, :, :].rearrange("(c p) e -> p c e", p=128), reads=[S7_res[h]], writes=[Sf_res])
                P.op("act", lambda e: e.activation(out=Sb[:, :, :], in_=Sf[:, :, :], func=AF.Copy), reads=[Sf_res], writes=[Sb_res])
            else:
                P.op("dve", lambda e: e.memset(Sf[:, :, :], 0.0), writes=[Sf_res])

            late = []

            def run_late():
                pend = list(late)
                del late[:]
                for f in pend:
                    f()

            def epi(j, col, banks, h=h, hb=hb):
                run_late()
                kind = col // D
                if kind == 0:
                    dc = (col - h * 256) // 128
                    for ci, (a0, a1) in enumerate(cgs):
                        P.op("dve", lambda e, ci=ci, a0=a0, a1=a1, b=banks[ci]: e.tensor_tensor(
                            out=qT[:, dc, a0:a1], in0=ps[b][:, 0:a1 - a0], in1=rstd[:, a0:a1], op=ALU.mult),
                            reads=[ps_res[banks[ci]], rstd_res], writes=[qT_res])
                    P.op("pool", lambda e: e.tensor_tensor(
                        out=qdT[:, dc, :].rearrange("p (t c) -> p t c", c=128),
                        in0=qT[:, dc, 0:TM].rearrange("p (t c) -> p t c", c=128),
                        in1=qdec[hb][:, :].unsqueeze(1).broadcast_to([128, NTM, 128]), op=ALU.mult),
                        reads=[qT_res, qdec_res[hb]], writes=[qdT_res])
                elif kind == 1:
                    dc = (col - D - h * 256) // 128
                    for ci, (a0, a1) in enumerate(cgs):
                        P.op("dve", lambda e, ci=ci, a0=a0, a1=a1, b=banks[ci]: e.scalar_tensor_tensor(
                            out=kT[:, dc, a0:a1], in0=ps[b][:, 0:a1 - a0], scalar=1.0 / 16.0, in1=rstd[:, a0:a1],
                            op0=ALU.mult, op1=ALU.mult),
                            reads=[ps_res[banks[ci]], rstd_res], writes=[kT_res])

                    def evac(grp, pb, bk, n):
                        t0 = grp[0]
                        if n == 128:
                            P.op("act", lambda e: e.activation(
                                out=kd[:, t0:t0 + len(grp), dc * 128:(dc + 1) * 128],
                                in_=pb[:, 0:len(grp) * 128].rearrange("p (t c) -> p t c", c=128),
                                func=AF.Copy, scale=kdec[:, h:h + 1]),
                                reads=[ps_res[bk], kdec_res], writes=[kd_res])
                        else:
                            P.op("act", lambda e: e.activation(out=kd[0:n, t0, dc * 128:(dc + 1) * 128], in_=pb[0:n, 0:128], func=AF.Copy),
                                 reads=[ps_res[bk]], writes=[kd_res])
                    late.append(lambda: transposes_to(lambda a0, a1: kT[:, dc, a0:a1], kT_res, kd, kd_res, dc * 128, evac))
                else:
                    isv = kind in (2, 3)
                    base = (2 * D if isv else 4 * D) + h * 512
                    ec = (col - base) // 128
                    s = fcnt[0] % 2
                    fcnt[0] += 1
                    for ci, (a0, a1) in enumerate(cgs):
                        P.op("dve", lambda e, ci=ci, a0=a0, a1=a1, b=banks[ci]: e.tensor_tensor(
                            out=fT[s][:, a0:a1], in0=ps[b][:, 0:a1 - a0], in1=rstd[:, a0:a1], op=ALU.mult),
                            reads=[ps_res[banks[ci]], rstd_res], writes=[fT_res[s]])
                    dst, dres = (vtm, vtm_res) if isv else (sg, sg_res)
                    func = AF.Copy if isv else AF.Silu

                    def evac(grp, pb, bk, n):
                        t0 = grp[0]
                        if n == 128:
                            P.op("act", lambda e: e.activation(
                                out=dst[:, t0:t0 + len(grp), ec * 128:(ec + 1) * 128],
                                in_=pb[:, 0:len(grp) * 128].rearrange("p (t c) -> p t c", c=128), func=func),
                                reads=[ps_res[bk]], writes=[dres])
                        else:
                            P.op("act", lambda e: e.activation(out=dst[0:n, t0, ec * 128:(ec + 1) * 128], in_=pb[0:n, 0:128], func=func),
                                 reads=[ps_res[bk]], writes=[dres])
                    late.append(lambda: transposes_to(lambda a0, a1: fT[s][:, a0:a1], fT_res[s], dst, dres, ec * 128, evac))

            gemm(A, A_res, cgs, epi, expect_W=ret_w_in)
            run_late()

            def sample_load(s_):
                u_ = s_ % 2
                P.dma(Ss[u_][:, :, :], state_in[s_, h, :, :].rearrange("(c p) e -> p c e", p=128), writes=[Ss_res[u_]])
                P.op("act", lambda e: e.activation(out=Ssb[u_][:, :, :], in_=Ss[u_][:, :, :], func=AF.Copy), reads=[Ss_res[u_]], writes=[Ssb_res[u_]])

            def sample_prefetch():
                for s_ in range(min(2, NS)):
                    sample_load(s_)

            def s2_gn(n, obank, t, u):
                P.op("dve", lambda e: e.bn_stats(out=st6[u][0:n, :], in_=ps[obank][0:n, :]), reads=[ps_res[obank]], writes=[stat_res[u]])
                P.op("dve", lambda e: e.bn_aggr(out=mv[u][0:n, :], in_=st6[u][0:n, :]), writes=[stat_res[u]])
                P.op("act", lambda e: e.activation(out=rs_o[u][0:n, :], in_=mv[u][0:n, 1:2], func=AF.Ln, bias=geps[0:n, :], scale=1.0),
                     reads=[kdec_res, stat_res[u]], writes=[rs_res[u]])
                P.op("act", lambda e: e.activation(out=rs_o[u][0:n, :], in_=rs_o[u][0:n, :], func=AF.Exp, scale=-0.5), writes=[rs_res[u]])
                P.op("dve", lambda e: e.scalar_tensor_tensor(out=nbias[u][0:n, :], in0=mv[u][0:n, 0:1], scalar=-1.0, in1=rs_o[u][0:n, :],
                                                             op0=ALU.mult, op1=ALU.mult),
                     reads=[stat_res[u], rs_res[u]], writes=[nb_res[u]])
                P.op("act", lambda e: e.activation(out=o1[u][0:n, :], in_=ps[obank][0:n, :], func=AF.Identity, scale=rs_o[u][0:n, :], bias=nbias[u][0:n, :]),
                     reads=[ps_res[obank], nb_res[u], rs_res[u]], writes=[o1_res[u]])

            def s2b(n, t, u):
                P.op("dve", lambda e: e.tensor_tensor(out=og[u][0:n, :], in0=o1[u][0:n, :], in1=sg[0:n, t, :], op=ALU.mult),
                     reads=[o1_res[u], sg_res], writes=[og_res[u]])

            def s3_tr(n, u, ocols):
                bk = aux_bank()
                pb = ps[bk].bitcast(BF16)

                def fn(e):
                    last = None
                    for ec in range(4):
                        last = e.transpose(out=pb[:, ec * 128:ec * 128 + n], in_=og[u][0:n, ec * 128:(ec + 1) * 128], identity=ident_b[0:n, 0:n])
                    return last
                P.op("pe", fn, reads=[og_res[u], cres], writes=[ps_res[bk]])
                for ec in range(4):
                    P.op("act", lambda e, ec=ec: e.activation(out=oT[:, ec, ocols[0]:ocols[1]], in_=pb[:, ec * 128:ec * 128 + n], func=AF.Copy,
                                                              scale=gncol[:, 4 * h + ec:4 * h + ec + 1]),
                         reads=[ps_res[bk], kdec_res], writes=[oT_res])

            def s1(c):
                a0, a1 = c * 128, (c + 1) * 128
                if full:
                    u = c % 2
                    ob = 1 if c % 2 == 0 else 5
                    P.op("pe", lambda e: [e.matmul(ps[0][:, 0:128], lhsT=kT[:, dc, a0:a1], rhs=qT[:, dc, a0:a1], start=(dc == 0), stop=(dc == 1))
                                          for dc in range(2)][-1],
                         reads=[kT_res, qT_res], writes=[ps_res[0]])
                    P.op("dve", lambda e: e.tensor_tensor(out=sT[u][:, :], in0=ps[0][:, 0:128], in1=decT[hb][:, :], op=ALU.mult),
                         reads=[ps_res[0], decT_res[hb]], writes=[sT_res[u]])

                    def fn(e):
                        e.matmul(ps[ob][:, :], lhsT=sT[u][:, :], rhs=vtm[:, c, :], start=True, stop=False)
                        last = None
                        for dc in range(2):
                            last = e.matmul(ps[ob][:, :], lhsT=qdT[:, dc, a0:a1], rhs=Sb[:, dc, :], start=False, stop=(dc == 1))
                        return last
                    P.op("pe", fn, reads=[sT_res[u], vtm_res, qdT_res, Sb_res], writes=[ps_res[ob]])
                P.op("pe", lambda e: [e.matmul(ps[2 + dc][:, :], lhsT=kd[:, c, dc * 128:(dc + 1) * 128], rhs=vtm[:, c, :], start=True, stop=True)
                                      for dc in range(2)][-1],
                     reads=[kd_res, vtm_res], writes=[ps_res[2], ps_res[3]])

            def s1u(c):
                for dc in range(2):
                    P.op("dve", lambda e, dc=dc: e.scalar_tensor_tensor(out=Sf[:, dc, :], in0=Sf[:, dc, :], scalar=g128(h), in1=ps[2 + dc][:, :],
                                                                        op0=ALU.mult, op1=ALU.add),
                         reads=[ps_res[2 + dc]], writes=[Sf_res])
                if full and c < ntiles - 1:
                    P.op("act", lambda e: e.activation(out=Sb[:, :, :], in_=Sf[:, :, :], func=AF.Copy), reads=[Sf_res], writes=[Sb_res])

            if full:
                sample_prefetch()
            s1(0)
            s1u(0)
            for c in range(ntiles):
                if c + 1 < ntiles:
                    s1(c + 1)
                if full:
                    if c >= 2:
                        s3_tr(128, (c - 2) % 2, ((c - 2) * 128, (c - 1) * 128))
                    s2_gn(128, 1 if c % 2 == 0 else 5, c, c % 2)
                if c + 1 < ntiles:
                    s1u(c + 1)
                if full and c >= 1:
                    s2b(128, c - 1, (c - 1) % 2)
            if full:
                if ntiles >= 2:
                    s3_tr(128, (ntiles - 2) % 2, ((ntiles - 2) * 128, (ntiles - 1) * 128))
                s2b(128, ntiles - 1, (ntiles - 1) % 2)
                s3_tr(128, (ntiles - 1) % 2, ((ntiles - 1) * 128, ntiles * 128))
            if not full:
                P.dma(S7[h, :, :].rearrange("(c p) e -> p c e", p=128), Sf[:, :, :], reads=[Sf_res], writes=[S7_res[h]])
                continue
            P.dma(st_p[h, :, :].rearrange("(c p) e -> p c e", p=128), Sf[:, :, :], reads=[Sf_res], writes=[Res()])
            ts = ntiles
            P.op("pe", lambda e: [e.matmul(ps[0][0:NS, 0:NS], lhsT=kT[:, dc, TM:T], rhs=qT[:, dc, TM:T], start=(dc == 0), stop=(dc == 1))
                                  for dc in range(2)][-1],
                 reads=[kT_res, qT_res], writes=[ps_res[0]])
            P.op("dve", lambda e: e.tensor_tensor(out=sT[0][0:NS, 0:NS], in0=ps[0][0:NS, 0:NS], in1=ident_f[0:NS, 0:NS], op=ALU.mult),
                 reads=[ps_res[0], cres], writes=[sT_res[0]])
            for s in range(NS):
                u = s % 2
                if s >= 2:
                    sample_load(s)
                P.op("dve", lambda e: e.scalar_tensor_tensor(
                    out=qm[u][:, :, :], in0=qT[:, :, TM:T], scalar=gm(h),
                    in1=selrow[:, s, :].unsqueeze(1).broadcast_to([128, 2, NS]),
                    op0=ALU.mult, op1=ALU.mult),
                    reads=[qT_res, cres], writes=[qm_res[u]])

                def fn(e, s=s, u=u):
                    if s == 0:
                        e.matmul(ps[1][0:NS, :], lhsT=sT[0][0:NS, 0:NS], rhs=vtm[0:NS, ts, :], start=True, stop=False)
                    last = None
                    for dc in range(2):
                        last = e.matmul(ps[1][0:NS, :], lhsT=qm[u][:, dc, :], rhs=Ssb[u][:, dc, :], start=False, stop=(s == NS - 1 and dc == 1))
                    return last
                P.op("pe", fn, reads=[sT_res[0], vtm_res, qm_res[u], Ssb_res[u]], writes=[ps_res[1]])
                P.op("dve", lambda e: e.tensor_scalar(out=km[u][:, :], in0=kd[0:NS, ts, :], scalar1=ident_f[0:NS, s:s + 1], scalar2=None, op0=ALU.mult),
                     reads=[kd_res, cres], writes=[km_res[u]])
                P.op("pe", lambda e: [e.matmul(ps[2 + dc][:, :], lhsT=km[u][:, dc * 128:(dc + 1) * 128], rhs=vtm[0:NS, ts, :], start=True, stop=True)
                                      for dc in range(2)][-1],
                     reads=[km_res[u], vtm_res], writes=[ps_res[2], ps_res[3]])
                for dc in range(2):
                    P.op("dve", lambda e, dc=dc: e.scalar_tensor_tensor(out=Sso[u][:, dc, :], in0=Ss[u][:, dc, :], scalar=gm(h), in1=ps[2 + dc][:, :],
                                                                        op0=ALU.mult, op1=ALU.add),
                         reads=[ps_res[2 + dc], Ss_res[u]], writes=[Sso_res[u]])
                P.defer_dma(st_s[s, h, :, :].rearrange("(c p) e -> p c e", p=128), Sso[u][:, :, :], reads=[Sso_res[u]], writes=[Res()])
                if s >= 1:
                    P.flush()
            P.flush()
            s2_gn(NS, 1, ts, 0)
            s2b(NS, ts, 0)
            s3_tr(NS, 0, (TM, T))
            for ec in range(4):
                P.dma(O[4 * h + ec, :, :], oT[:, ec, :], reads=[oT_res], writes=[O_res[4 * h + ec]])

    selrow = sb("selrow", [128, NS, NS])
    P.op("dve", lambda e: e.memset(selrow[:, :, :], 0.0), writes=[cres])
    for s in range(NS):
        P.op("dve", lambda e, s=s: e.memset(selrow[:, s, s:s + 1], 1.0), writes=[cres])

    if NTC > 0:
        with ExitStack() as st:
            load_xT(x_ctx, CTX, 0, st)
        P.barrier()
        with ExitStack() as st:
            norm_phase(norm_mix[0], CTX, cfg.cgs(CTX), st)
        P.barrier()
        with ExitStack() as st:
            retention_pass(False, CTX, NTC, st)
        P.barrier()
    else:
        with ExitStack() as st:
            z = sb("z", [128, 2, 512], stack=st)
            zr = Res()
            P.op("dve", lambda e: e.memset(z[:, :, :], 0.0), writes=[zr])
            for h in range(NH):
                P.dma(S7[h, :, :].rearrange("(c p) e -> p c e", p=128), z[:, :, :], reads=[zr], writes=[S7_res[h]])
        P.barrier()

    with ExitStack() as st:
        load_xT(x_main, TM, 0, st)
        load_xT(x_smp, NS, TM, st)
    P.barrier()
    dump_R("R0")

    def mlp_and_ple(i):
        with ExitStack() as st:
            norm_phase(norm_mlp[i], T, cgsT, st)
        P.barrier()
        with ExitStack() as st:
            hf32 = [sb(f"hf{k}", [128, T], stack=st) for k in range(2)]
            hf_res = RL(2)
            hb16 = [sb(f"hb{k}", [128, T], BF16, stack=st) for k in range(2)]
            hb_res = RL(2)
            cnt = [0]

            def epi_up(j, col, banks):
                s = cnt[0] % 2
                cnt[0] += 1
                for ci, (a0, a1) in enumerate(cgsT):
                    P.op("dve", lambda e, ci=ci, a0=a0, a1=a1, b=banks[ci]: e.scalar_tensor_tensor(
                        out=hf32[s][:, a0:a1], in0=ps[b][:, 0:a1 - a0], scalar=0.0, in1=rstd[:, a0:a1], op0=ALU.max, op1=ALU.mult),
                        reads=[ps_res[banks[ci]], rstd_res], writes=[hf_res[s]])
                P.op("act", lambda e: e.activation(out=hb16[s][:, :], in_=hf32[s][:, :], func=AF.Square), reads=[hf_res[s]], writes=[hb_res[s]])
                P.defer_dma(H[j, :, :], hb16[s][:, :], reads=[hb_res[s]], writes=[H_res[j]])
            gemm(A, A_res, cgsT, epi_up, expect_W=mlp_up)
        P.barrier()
        with ExitStack() as st:
            epi_r = resid_epi_factory(st, "d")
            for p_ in range(DFF // D):
                for kc in range(KC):
                    P.dma(A[:, kc, :], H[p_ * KC + kc, :, :], reads=[H_res[p_ * KC + kc]], writes=[A_res[kc]])
                gemm(A, A_res, cgsT, epi_r, expect_W=mlp_down)
        P.barrier()
        with ExitStack() as st:
            pT = sb("pT", [128, 2, T], BF16, stack=st)
            pT_res = RL(2)
            pt = [sb(f"pt{k}", [128, 256], stack=st) for k in range(2)]
            pt_res = RL(2)
            tl = [(t * 128, 128) for t in range(NTM)] + [(TM, NS)]
            for ti, (c0, n) in enumerate(tl):
                s = ti % 2
                src = p_main[i, c0:c0 + n, :] if c0 < TM else p_smp[i, :, :]
                P.dma(pt[s][0:n, :], src, writes=[pt_res[s]])
                bk = aux_bank()
                P.op("pe", lambda e: [e.transpose(out=ps[bk][:, dc * 128:dc * 128 + n], in_=pt[s][0:n, dc * 128:(dc + 1) * 128], identity=ident_f[0:n, 0:n])
                                      for dc in range(2)][-1],
                     reads=[pt_res[s], cres], writes=[ps_res[bk]])
                P.op("act", lambda e: e.activation(out=pT[:, :, c0:c0 + n], in_=ps[bk][:, 0:256].rearrange("p (g c) -> p g c", g=2)[:, :, 0:n], func=AF.Copy),
                     reads=[ps_res[bk]], writes=pT_res)
            pp = [sb(f"pp{k}", [128, T], stack=st) for k in range(2)]
            pp_res = RL(2)
            cnt = [0]

            def epi_pp(j, col, banks):
                s = cnt[0] % 2
                cnt[0] += 1
                for ci, (a0, a1) in enumerate(cgsT):
                    P.op("act", lambda e, ci=ci, a0=a0, a1=a1, b=banks[ci]: e.activation(out=pp[s][:, a0:a1], in_=ps[b][:, 0:a1 - a0], func=AF.Copy),
                         reads=[ps_res[banks[ci]]], writes=[pp_res[s]])
                P.defer_dma(PP[j, :, :], pp[s][:, :], reads=[pp_res[s]], writes=[PP_res[j]])
            gemm(pT, pT_res, cgsT, epi_pp, expect_W=ple_proj)
        P.barrier()
        with ExitStack() as st:
            norm_phase(norm_ple[i], T, cgsT, st)
        P.barrier()
        with ExitStack() as st:
            rr = [sb(f"rrg{k}", [128, T], stack=st) for k in range(2)]
            rr_res = RL(2)
            pq = [sb(f"pq{k}", [128, T], stack=st) for k in range(2)]
            pq_res = RL(2)
            gt = [sb(f"gt{k}", [128, T], stack=st) for k in range(2)]
            gt_res = RL(2)
            cnt = [0]

            def epi_gate(j, col, banks):
                s = cnt[0] % 2
                cnt[0] += 1
                P.dma(rr[s][:, :], R[j, :, :], reads=[R_res[j]], writes=[rr_res[s]])
                P.dma(pq[s][:, :], PP[j, :, :], reads=[PP_res[j]], writes=[pq_res[s]])
                for ci, (a0, a1) in enumerate(cgsT):
                    P.op("dve", lambda e, ci=ci, a0=a0, a1=a1, b=banks[ci]: e.tensor_tensor(
                        out=gt[s][:, a0:a1], in0=ps[b][:, 0:a1 - a0], in1=rstd[:, a0:a1], op=ALU.mult),
                        reads=[ps_res[banks[ci]], rstd_res], writes=[gt_res[s]])
                P.op("act", lambda e: e.activation(out=gt[s][:, :], in_=gt[s][:, :], func=AF.Sigmoid), writes=[gt_res[s]])
                P.op("dve", lambda e: e.tensor_tensor(out=gt[s][:, :], in0=gt[s][:, :], in1=pq[s][:, :], op=ALU.mult),
                     reads=[pq_res[s]], writes=[gt_res[s]])
                P.op("dve", lambda e: e.tensor_tensor(out=rr[s][:, :], in0=rr[s][:, :], in1=gt[s][:, :], op=ALU.add),
                     reads=[gt_res[s]], writes=[rr_res[s]])
                P.defer_dma(R[j, :, :], rr[s][:, :], reads=[rr_res[s]], writes=[R_res[j]])
            gemm(A, A_res, cgsT, epi_gate, expect_W=ple_gate)
        P.barrier()

    with ExitStack() as st:
        norm_phase(norm_mix[0], T, cgsT, st)
    P.barrier()
    with ExitStack() as st:
        retention_pass(True, T, NTM, st)
    P.barrier()
    with ExitStack() as st:
        epi_r = resid_epi_factory(st, "o")
        for p_ in range(2):
            for kc in range(KC):
                P.dma(A[:, kc, :], O[p_ * KC + kc, :, :], reads=[O_res[p_ * KC + kc]], writes=[A_res[kc]])
            gemm(A, A_res, cgsT, epi_r, expect_W=ret_w_out)
    P.barrier()
    dump_R("R1")
    mlp_and_ple(0)
    dump_R("R2")

    with ExitStack() as st:
        norm_phase(norm_mix[1], T, cgsT, st)
    P.barrier()
    slopes = [2.0 ** (-8.0 * (hh + 1) / NQ) for hh in range(NQ)]
    st = ExitStack()
    st2 = ExitStack()
    if True:
        NT1 = NTM + 1
        bcol = sb("bcol", [128, (D + 2 * KVW) // 128], stack=st)
        qg = sb("qg", [128, 1], stack=st)
        kg = sb("kg", [128, 1], stack=st)
        e64 = sb("e64", [128, 1], stack=st)
        esink = sb("esink", [128, NQ], stack=st)
        negd = sb("negd", [128, 256], stack=st)
        msk = sb("msk", [128, 256], stack=st)
        first = sb("first", [128, 1], stack=st)
        sdist = sb("sdist", [128, 1], stack=st)
        blk = sb("blk", [128, 128], stack=st)
        c1 = Res()
        P.dma(bcol[:, :], att_b.rearrange("(k p) -> p k", p=128), writes=[c1], allow_slow_non_contiguous=True)
        for half in range(2):
            P.dma(qg[half * 64:(half + 1) * 64, :], att_qn.rearrange("(p o) -> p o", o=1), writes=[c1], allow_slow_non_contiguous=True)
            P.dma(kg[half * 64:(half + 1) * 64, :], att_kn.rearrange("(p o) -> p o", o=1), writes=[c1], allow_slow_non_contiguous=True)
        P.op("act", lambda e: e.activation(out=qg[:, :], in_=qg[:, :], func=AF.Copy, scale=0.125), writes=[c1])
        P.dma(esink[:, :], att_sinks.partition_broadcast(128), writes=[c1])
        P.op("act", lambda e: e.activation(out=esink[:, :], in_=esink[:, :], func=AF.Exp), writes=[c1])
        P.dma(negd[:, :], c_negdist[:, :], writes=[c1])
        P.dma(msk[:, :], c_mask[:, :], writes=[c1])
        P.dma(first[:, :], c_first[:, :], writes=[c1])
        P.dma(sdist[:, :], c_sdist[:, :], writes=[c1])
        P.op("dve", lambda e: e.memset(e64[:, :], RMS_EPS), writes=[c1])
        P.op("dve", lambda e: e.memset(blk[:, :], 0.0), writes=[c1])
        P.op("dve", lambda e: e.memset(blk[0:64, 0:64], 1.0), writes=[c1])
        P.op("dve", lambda e: e.memset(blk[64:128, 64:128], 1.0), writes=[c1])

        knf = sb("knf", [128, KVC, 128 + NS], stack=st)
        vf = sb("vf", [128, KVC, 128 + NS], stack=st)
        knf_res = Res()
        vf_res = Res()
        qsT = sb("qsT", [128, KC, NS], stack=st)
        qsT_res = Res()
        knT = [sb(f"knT{k}", [128, KVC, T], BF16, stack=st) for k in range(2)]
        knT_res = [Res(), Res()]
        vaug = sb("vaug", [128, NT1, NKV, 65], BF16, stack=st)
        vaug_res = Res()
        P.op("dve", lambda e: e.memset(vaug[:, :, :, 64:65], 1.0), writes=[vaug_res])
        xb = [sb(f"xb{k}", [128, T], stack=st2) for k in range(2)]
        xb_res = RL(2)
        xq = [sb(f"xq{k}", [128, T], stack=st2) for k in range(2)]
        xq_res = RL(2)
        qn = [sb(f"qn{k}", [128, T], BF16, stack=st2) for k in range(2)]
        qn_res = RL(2)
        cnt = [0]
        ecnt = [0]
        acnt = [0]
        LAST0 = NTO * 128

        def normed(j, col, banks, gcolv):
            s = cnt[0] % 2
            cnt[0] += 1
            bi = col // 128
            for ci, (a0, a1) in enumerate(cgsT):
                P.op("dve", lambda e, ci=ci, a0=a0, a1=a1, b=banks[ci]: e.tensor_tensor(
                    out=xb[s][:, a0:a1], in0=ps[b][:, 0:a1 - a0], in1=rstd[:, a0:a1], op=ALU.mult),
                    reads=[ps_res[banks[ci]], rstd_res], writes=[xb_res[s]])
            P.op("act", lambda e: e.activation(out=xb[s][:, :], in_=xb[s][:, :], func=AF.Identity, bias=bcol[:, bi:bi + 1], scale=1.0),
                 reads=[c1], writes=[xb_res[s]])
            if gcolv is None:
                return s
            P.op("act", lambda e: e.activation(out=xq[s][:, :], in_=xb[s][:, :], func=AF.Square), reads=[xb_res[s]], writes=[xq_res[s]])
            return s

        def normed_b(s):
            for ci, (a0, a1) in enumerate(cgsT):
                bk = aux_bank()
                P.op("pe", lambda e, a0=a0, a1=a1, bk=bk: e.matmul(ps[bk][:, 0:a1 - a0], lhsT=blk[:, :], rhs=xq[s][:, a0:a1], start=True, stop=True),
                     reads=[xq_res[s], c1], writes=[ps_res[bk]])
                P.op("act", lambda e, a0=a0, a1=a1, bk=bk: e.activation(out=xq[s][:, a0:a1], in_=ps[bk][:, 0:a1 - a0], func=AF.Ln, scale=1.0 / 64.0,
                                                                       bias=e64[:, 0:1]),
                     reads=[ps_res[bk], c1], writes=[xq_res[s]])
            P.op("act", lambda e: e.activation(out=xq[s][:, :], in_=xq[s][:, :], func=AF.Exp, scale=-0.5), writes=[xq_res[s]])

        def nrm_out(s, gcolv, out_ap, c0, c1_, reads_extra, wres):
            P.op("dve", lambda e: e.scalar_tensor_tensor(out=out_ap, in0=xb[s][:, c0:c1_], scalar=gcolv[:, 0:1], in1=xq[s][:, c0:c1_], op0=ALU.mult, op1=ALU.mult),
                 reads=[xb_res[s], xq_res[s], c1] + reads_extra, writes=[wres])

        qlate = []

        def run_qlate():
            pend = list(qlate)
            del qlate[:]
            for f in pend:
                f()

        def epi_qkv(j, col, banks):
            run_qlate()
            if col >= D + KVW:
                vc = (col - D - KVW) // 128
                s = normed(j, col, banks, None)
                P.op("pool", lambda e: e.tensor_copy(out=vf[:, vc, 0:128], in_=xb[s][:, LAST0:LAST0 + 128]), reads=[xb_res[s]], writes=[vf_res])
                P.op("pool", lambda e: e.tensor_copy(out=vf[:, vc, 128:128 + NS], in_=xb[s][:, TM:T]), reads=[xb_res[s]], writes=[vf_res])
                P.op("act", lambda e: e.activation(out=qn[s][:, :], in_=xb[s][:, :], func=AF.Copy), reads=[xb_res[s]], writes=[qn_res[s]])
                tl = [(t * 128, 128, t) for t in range(NTM)] + [(TM, NS, NTM)]
                for g0 in range(0, NTM, 8):
                    subs = [tl[g0:min(NTM, g0 + 8)]]
                    if g0 + 8 >= NTM:
                        subs.append([tl[NTM]])
                    for sub in subs:
                        bk = aux_bank()
                        pb = ps[bk].bitcast(BF16)
                        n = sub[0][1]
                        P.op("pe", lambda e, sub=sub, pb=pb: [e.transpose(out=pb[0:g[1], i * 128:(i + 1) * 128], in_=qn[s][:, g[0]:g[0] + g[1]], identity=ident_b[:, :])
                                                              for i, g in enumerate(sub)][-1],
                             reads=[qn_res[s], cres], writes=[ps_res[bk]])
                        t0 = sub[0][2]
                        for kk in range(2):
                            P.op("act", lambda e, sub=sub, pb=pb, n=n, t0=t0, kk=kk: e.activation(
                                out=vaug[0:n, t0:t0 + len(sub), 2 * vc + kk, 0:64],
                                in_=pb[0:n, 0:len(sub) * 128].rearrange("p (t k d) -> p t k d", k=2, d=64)[:, :, kk, :], func=AF.Copy),
                                reads=[ps_res[bk]], writes=[vaug_res])
            elif col >= D:
                kc_ = (col - D) // 128
                s = normed(j, col, banks, kg)

                def k_stage2():
                    normed_b(s)
                    nrm_out(s, kg, knT[0][:, kc_, :], 0, T, [], knT_res[0])
                    nrm_out(s, kg, knf[:, kc_, 0:128], LAST0, LAST0 + 128, [], knf_res)
                    nrm_out(s, kg, knf[:, kc_, 128:128 + NS], TM, T, [], knf_res)
                    P.defer_dma(knT[1][0:64, kc_, :], knT[0][64:128, kc_, :], reads=[knT_res[0]], writes=[knT_res[1]])
                    P.defer_dma(knT[1][64:128, kc_, :], knT[0][0:64, kc_, :], reads=[knT_res[0]], writes=[knT_res[1]])
                qlate.append(k_stage2)
            else:
                jq = col // 128
                s = normed(j, col, banks, qg)

                def q_stage2():
                    normed_b(s)
                    nrm_out(s, qg, qn[s][:, :], 0, T, [], qn_res[s])
                    nrm_out(s, qg, qsT[:, jq, :], TM, T, [], qsT_res)
                    P.defer_dma(H[jq, :, :], qn[s][:, :], reads=[qn_res[s]], writes=[H_res[jq]])
                qlate.append(q_stage2)

        gemm(A, A_res, cgsT, epi_qkv, expect_W=att_w_qkv)
        run_qlate()
        P.flush()
        P.barrier()
        st2.close()

        st3 = ExitStack()
        qc = [sb(f"qc{k}", [128, T], BF16, stack=st3) for k in range(3)]
        qc_res = RL(3)
        ebias = sb("ebias", [128, 2, 256], stack=st3)
        ebias_res = RL(2)
        GB = 4
        stmp = [sb(f"stmp{k}", [128, GB, 256], stack=st3) for k in range(2)]
        stmp_res = RL(2)
        Et = [sb(f"Et{k}", [128, GB, 256], BF16, stack=st3) for k in range(2)]
        Et_res = RL(2)
        den = [sb(f"den{k}", [128, GB], stack=st3) for k in range(2)]
        den_res = RL(2)
        attc = [sb(f"attc{k}", [128, NTM, 128], BF16, stack=st3) for k in range(2)]
        attc_res = RL(2)
        oTa = [sb(f"oTa{k}", [128, T], BF16, stack=st3) for k in range(2)]
        oTa_res = RL(2)
        for k in range(2):
            P.op("pool", lambda e, k=k: e.memset(oTa[k][:, :], 0.0), writes=[oTa_res[k]])
        items = []
        for jq in range(KC):
            for hh in range(2):
                for t0 in range(1, NTM, GB):
                    items.append((jq, hh, t0, min(GB, NTM - t0)))
        nload = [0]

        def load_q(jq):
            if jq < KC and jq >= nload[0]:
                P.dma(qc[jq % 3][:, :], H[jq, :, :], reads=[H_res[jq]], writes=[qc_res[jq % 3]])
                nload[0] = jq + 1
        load_q(0)
        load_q(1)

        def stage_sc(idx):
            jq, hh, t0, nb_ = items[idx]
            h = 2 * jq + hh
            kvh = h // G
            kc_, kb_ = kvh // 2, kvh % 2
            src = knT[0] if kb_ == hh else knT[1]
            sres = knT_res[0] if kb_ == hh else knT_res[1]
            pb0 = hh * 64
            if t0 == 1:
                if hh == 0:
                    load_q(jq + 2)
                eb = h % 2
                P.op("dve", lambda e: e.scalar_tensor_tensor(out=ebias[:, eb, :], in0=negd[:, :], scalar=slopes[h], in1=msk[:, :], op0=ALU.mult, op1=ALU.add),
                     reads=[c1], writes=[ebias_res[eb]])
            r = idx % 2
            banks = [2 * r, 2 * r + 1]

            def fn(e):
                last = None
                for b_ in range(nb_):
                    t = t0 + b_
                    for i in range(2):
                        k0 = (t - 1 + i) * 128
                        last = e.matmul(ps[banks[b_ // 2]][:, (b_ % 2) * 256 + i * 128:(b_ % 2) * 256 + (i + 1) * 128],
                                        lhsT=src[pb0:pb0 + 64, kc_, k0:k0 + 128], rhs=qc[jq % 3][pb0:pb0 + 64, t * 128:(t + 1) * 128],
                                        start=True, stop=True)
                return last
            P.op("pe", fn, reads=[sres, qc_res[jq % 3]], writes=[ps_res[banks[0]], ps_res[banks[1]]])

        def stage_exp(idx):
            jq, hh, t0, nb_ = items[idx]
            h = 2 * jq + hh
            eb = h % 2
            r = idx % 2
            banks = [2 * r, 2 * r + 1]
            for bi in range((nb_ + 1) // 2):
                n2 = min(2, nb_ - 2 * bi)
                P.op("dve", lambda e, bi=bi, n2=n2: e.tensor_tensor(
                    out=stmp[r][:, 2 * bi:2 * bi + n2, :], in0=ps[banks[bi]][:, 0:n2 * 256].rearrange("p (b c) -> p b c", c=256),
                    in1=ebias[:, eb, :].unsqueeze(1).broadcast_to([128, n2, 256]), op=ALU.add),
                    reads=[ps_res[banks[bi]], ebias_res[eb]], writes=[stmp_res[r]])
            if t0 == 1:
                P.op("dve", lambda e: e.tensor_scalar(out=stmp[r][:, 0, 0:128], in0=stmp[r][:, 0, 0:128], scalar1=first[:, 0:1], scalar2=None, op0=ALU.add),
                     reads=[c1], writes=[stmp_res[r]])
            P.op("act", lambda e: e.activation(out=Et[r][:, 0:nb_, :], in_=stmp[r][:, 0:nb_, :], func=AF.Exp), reads=[stmp_res[r]], writes=[Et_res[r]])

        def stage_pv(idx):
            jq, hh, t0, nb_ = items[idx]
            h = 2 * jq + hh
            kvh = h // G
            r = idx % 2
            ac = jq % 2
            zb = 4 + r

            def fn(e):
                last = None
                for b_ in range(nb_):
                    t = t0 + b_
                    for i in range(2):
                        last = e.matmul(ps[zb][:, b_ * 65:b_ * 65 + 65], lhsT=Et[r][:, b_, i * 128:(i + 1) * 128], rhs=vaug[:, t - 1 + i, kvh, :],
                                        start=(i == 0), stop=(i == 1))
                return last
            P.op("pe", fn, reads=[Et_res[r], vaug_res], writes=[ps_res[zb]])

        def stage_norm(idx):
            jq, hh, t0, nb_ = items[idx]
            h = 2 * jq + hh
            r = idx % 2
            ac = jq % 2
            zb = 4 + r
            zv = ps[zb][:, 0:nb_ * 65].rearrange("p (b c) -> p b c", c=65)
            P.op("dve", lambda e: e.tensor_scalar(out=den[r][:, 0:nb_], in0=zv[:, :, 64], scalar1=esink[:, h:h + 1], scalar2=None, op0=ALU.add),
                 reads=[ps_res[zb], c1], writes=[den_res[r]])
            P.op("dve", lambda e: e.reciprocal(out=den[r][:, 0:nb_], in_=den[r][:, 0:nb_]), writes=[den_res[r]])
            P.op("dve", lambda e: e.tensor_tensor(out=attc[ac][:, t0:t0 + nb_, hh * 64:(hh + 1) * 64], in0=zv[:, :, 0:64],
                                                  in1=den[r][:, 0:nb_].unsqueeze(2).broadcast_to([128, nb_, 64]), op=ALU.mult),
                 reads=[ps_res[zb], den_res[r]], writes=[attc_res[ac]])
            last_of_chunk = (hh == 1 and t0 + nb_ >= NTM)
            if last_of_chunk:
                for g0 in range(1, NTM, 8):
                    grp = list(range(g0, min(NTM, g0 + 8)))
                    bk = aux_bank()
                    pb = ps[bk].bitcast(BF16)
                    P.op("pe", lambda e: [e.transpose(out=pb[:, i * 128:(i + 1) * 128], in_=attc[ac][:, t, :], identity=ident_b[:, :])
                                          for i, t in enumerate(grp)][-1],
                         reads=[attc_res[ac], cres], writes=[ps_res[bk]])
                    P.op("act", lambda e: e.activation(out=oTa[ac][:, grp[0] * 128:(grp[-1] + 1) * 128], in_=pb[:, 0:len(grp) * 128], func=AF.Copy),
                         reads=[ps_res[bk]], writes=[oTa_res[ac]])
                P.dma(O[jq, :, :], oTa[ac][:, :], reads=[oTa_res[ac]], writes=[O_res[jq]])

        n_it = len(items)
        stage_sc(0)
        stage_exp(0)
        if n_it > 1:
            stage_sc(1)
        for idx in range(n_it):
            stage_pv(idx)
            if idx + 1 < n_it:
                stage_exp(idx + 1)
            if idx + 2 < n_it:
                stage_sc(idx + 2)
            stage_norm(idx)
        P.barrier()
        st3.close()

        ktm = sb("ktm", [128, KVW], stack=st)
        vtm2 = sb("vtm2", [128, KVW], stack=st)
        ksm = sb("ksm", [NS, KVW], stack=st)
        vsm = sb("vsm", [NS, KVW], stack=st)
        tm_res = Res()
        for (srcf, sres, dst, dsts) in ((knf, knf_res, ktm, ksm), (vf, vf_res, vtm2, vsm)):
            for kc_ in range(KVC):
                bk = aux_bank()
                P.op("pe", lambda e: e.transpose(out=ps[bk][:, 0:128], in_=srcf[:, kc_, 0:128], identity=ident_f[:, :]), reads=[sres, cres], writes=[ps_res[bk]])
                P.op("act", lambda e: e.activation(out=dst[:, kc_ * 128:(kc_ + 1) * 128], in_=ps[bk][:, 0:128], func=AF.Copy), reads=[ps_res[bk]], writes=[tm_res])
                bk = aux_bank()
                P.op("pe", lambda e: e.transpose(out=ps[bk][0:NS, 0:128], in_=srcf[:, kc_, 128:128 + NS], identity=ident_f[:, :]), reads=[sres, cres], writes=[ps_res[bk]])
                P.op("act", lambda e: e.activation(out=dsts[:, kc_ * 128:(kc_ + 1) * 128], in_=ps[bk][0:NS, 0:128], func=AF.Copy), reads=[ps_res[bk]], writes=[tm_res])
        P.dma(ck_p[:, :], ktm[:, :], reads=[tm_res], writes=[Res()])
        P.dma(cv_p[:, :], vtm2[:, :], reads=[tm_res], writes=[Res()])
        for s_ in range(NS):
            P.dma(ck_s[s_, 0:127, :], cache_k[s_, 1:128, :], writes=[Res()])
            P.dma(cv_s[s_, 0:127, :], cache_v[s_, 1:128, :], writes=[Res()])
        P.dma(ck_s[:, 127, :], ksm[:, :], reads=[tm_res], writes=[Res()])
        P.dma(cv_s[:, 127, :], vsm[:, :], reads=[tm_res], writes=[Res()])

        for kc in range(KC):
            P.dma(A[:, kc, :], O[kc, :, :], reads=[O_res[kc]], writes=[A_res[kc]])

        sel32 = sb("sel32", [NS, 128], stack=st)
        selC = sb("selC", [128, NS, NS], stack=st)
        sel_res = Res()
        P.op("dve", lambda e: e.memset(sel32[:, :], 0.0), writes=[sel_res])
        P.op("dve", lambda e: e.tensor_copy(out=sel32[:, :].rearrange("p (a b) -> p a b", b=32)[:, :, 0], in_=ident_f[0:NS, 0:NS]), reads=[cres], writes=[sel_res])
        P.op("dve", lambda e: e.tensor_copy(out=selC[:, :, :], in_=selrow[:, :, :]), reads=[cres], writes=[sel_res])
        selN = sb("selN", [128, NS], stack=st)
        bk = aux_bank()
        P.op("pe", lambda e: e.transpose(out=ps[bk][:, 0:NS], in_=sel32[0:NS, :], identity=ident_f[0:NS, 0:NS]), reads=[sel_res, cres], writes=[ps_res[bk]])
        P.op("act", lambda e: e.activation(out=selN[:, :], in_=ps[bk][:, 0:NS], func=AF.Copy), reads=[ps_res[bk]], writes=[sel_res])
        k32 = sb("k32", [128, KVW], stack=st)
        v32 = sb("v32", [128, KVW], stack=st)
        Kc = sb("Kc", [128, NS, KVW], stack=st)
        Vc = sb("Vc", [128, NS, KVW], stack=st)
        smp_res = Res()
        for (srct, dstb) in ((ksm, k32), (vsm, v32)):
            for c0 in range(0, KVW, 512):
                bk = aux_bank()
                cw = min(512, KVW - c0)
                P.op("pe", lambda e: e.matmul(ps[bk][:, 0:cw], lhsT=sel32[:, :], rhs=srct[:, c0:c0 + cw], start=True, stop=True),
                     reads=[sel_res, tm_res], writes=[ps_res[bk]])
                P.op("act", lambda e: e.activation(out=dstb[:, c0:c0 + cw], in_=ps[bk][:, 0:cw], func=AF.Copy), reads=[ps_res[bk]], writes=[smp_res])
        for s_ in range(NS):
            P.dma(Kc[:, s_, :], cache_k[s_, :, :], writes=[smp_res])
            P.dma(Vc[:, s_, :], cache_v[s_, :, :], writes=[smp_res])
        sbias = sb("sbias", [128, NQ], stack=st)
        slp = sb("slp", [128, NQ], stack=st)
        for hh in range(NQ):
            P.op("pool", lambda e, hh=hh: e.memset(slp[:, hh:hh + 1], -slopes[hh]), writes=[smp_res])
        P.op("dve", lambda e: e.tensor_scalar(out=sbias[:, :], in0=slp[:, :], scalar1=sdist[:, 0:1], scalar2=None, op0=ALU.mult), reads=[c1], writes=[smp_res])
        E_all = sb("E_all", [128, NS, NQ], stack=st)
        En = sb("En", [128, NQ], stack=st)
        prod = sb("prod", [128, 2, 64], stack=st)
        prodn = sb("prodn", [128, 2, 64], stack=st)
        qbc = sb("qbc", [128, 128], stack=st)
        ev = sb("ev", [128, 512], stack=st)
        evn = sb("evn", [128, 512], stack=st)
        dn_s = sb("dn_s", [NS, NQ], stack=st)
        at_s = sb("at_s", [NS, 512], stack=st)
        at_b = sb("at_b", [NS, 512], BF16, stack=st)
        P.op("dve", lambda e: e.memset(En[:, :], 0.0), writes=[smp_res])
        P.op("dve", lambda e: e.memset(evn[:, :], 0.0), writes=[smp_res])
        for s_ in range(NS):
            p0 = 32 * s_
            for jq in range(KC):
                kvh = (2 * jq) // G
                kvh1 = (2 * jq + 1) // G
                bk = aux_bank()
                P.op("pe", lambda e: e.matmul(ps[bk][:, 0:128], lhsT=qsT[:, jq, s_:s_ + 1].broadcast_to([128, 128]), rhs=ident_f[:, :], start=True, stop=True),
                     reads=[qsT_res, cres], writes=[ps_res[bk]])
                P.op("act", lambda e: e.activation(out=qbc[:, :], in_=ps[bk][:, 0:128], func=AF.Copy), reads=[ps_res[bk]], writes=[smp_res])
                for hh, kv in ((0, kvh), (1, kvh1)):
                    P.op("dve", lambda e: e.tensor_tensor(out=prod[:, hh, :], in0=qbc[:, hh * 64:(hh + 1) * 64], in1=Kc[:, s_, kv * 64:(kv + 1) * 64], op=ALU.mult),
                         writes=[smp_res])
                    P.op("dve", lambda e: e.tensor_tensor(out=prodn[p0:p0 + 1, hh, :], in0=qbc[p0:p0 + 1, hh * 64:(hh + 1) * 64], in1=k32[p0:p0 + 1, kv * 64:(kv + 1) * 64], op=ALU.mult),
                         writes=[smp_res])
                P.op("dve", lambda e: e.tensor_reduce(out=E_all[:, s_, 2 * jq:2 * jq + 2], in_=prod[:, :, :], axis=AX.X, op=ALU.add), writes=[smp_res])
                P.op("dve", lambda e: e.tensor_reduce(out=En[p0:p0 + 1, 2 * jq:2 * jq + 2], in_=prodn[p0:p0 + 1, :, :], axis=AX.X, op=ALU.add), writes=[smp_res])
            P.op("dve", lambda e: e.tensor_tensor(out=E_all[:, s_, :], in0=E_all[:, s_, :], in1=sbias[:, :], op=ALU.add), writes=[smp_res])
            P.op("act", lambda e: e.activation(out=E_all[:, s_, :], in_=E_all[:, s_, :], func=AF.Exp), writes=[smp_res])
            P.op("act", lambda e: e.activation(out=En[p0:p0 + 1, :], in_=En[p0:p0 + 1, :], func=AF.Exp), writes=[smp_res])
        bk = aux_bank()

        def fn_den(e):
            last = None
            for s_ in range(NS):
                e.matmul(ps[bk][0:NS, 0:NQ], lhsT=selC[:, s_, :], rhs=E_all[:, s_, :], start=(s_ == 0), stop=False)
            return e.matmul(ps[bk][0:NS, 0:NQ], lhsT=selN[:, :], rhs=En[:, :], start=False, stop=True)
        P.op("pe", fn_den, reads=[sel_res, smp_res], writes=[ps_res[bk]])
        P.op("dve", lambda e: e.tensor_tensor(out=dn_s[:, :], in0=ps[bk][0:NS, 0:NQ], in1=esink[0:NS, :], op=ALU.add), reads=[ps_res[bk], c1], writes=[smp_res])
        P.op("dve", lambda e: e.reciprocal(out=dn_s[:, :], in_=dn_s[:, :]), writes=[smp_res])
        HP = 512 // 64
        for c0 in range(0, D, 512):
            h0 = c0 // 64
            bkn = aux_bank()
            for s_ in range(NS):
                p0 = 32 * s_
                for hh in range(HP):
                    h = h0 + hh
                    kv = h // G
                    P.op("dve", lambda e: e.tensor_scalar(out=ev[:, hh * 64:(hh + 1) * 64], in0=Vc[:, s_, kv * 64:(kv + 1) * 64], scalar1=E_all[:, s_, h:h + 1], scalar2=None, op0=ALU.mult),
                         writes=[smp_res])
                    P.op("dve", lambda e: e.tensor_scalar(out=evn[p0:p0 + 1, hh * 64:(hh + 1) * 64], in0=v32[p0:p0 + 1, kv * 64:(kv + 1) * 64], scalar1=En[p0:p0 + 1, h:h + 1], scalar2=None, op0=ALU.mult),
                         writes=[smp_res])
                P.op("pe", lambda e: e.matmul(ps[bkn][0:NS, 0:512], lhsT=selC[:, s_, :], rhs=ev[:, :], start=(s_ == 0), stop=False),
                     reads=[sel_res, smp_res], writes=[ps_res[bkn]])
            P.op("pe", lambda e: e.matmul(ps[bkn][0:NS, 0:512], lhsT=selN[:, :], rhs=evn[:, :], start=False, stop=True),
                 reads=[sel_res, smp_res], writes=[ps_res[bkn]])
            P.op("dve", lambda e: e.tensor_tensor(out=at_b[:, :].rearrange("p (h d) -> p h d", d=64),
                                                  in0=ps[bkn][0:NS, 0:512].rearrange("p (h d) -> p h d", d=64),
                                                  in1=dn_s[:, h0:h0 + HP].unsqueeze(2).broadcast_to([NS, HP, 64]), op=ALU.mult),
                 reads=[ps_res[bkn]], writes=[smp_res])
            for i in range(4):
                kc = c0 // 128 + i
                bk2 = aux_bank()
                pb = ps[bk2].bitcast(BF16)
                P.op("pe", lambda e: e.transpose(out=pb[:, 0:NS], in_=at_b[0:NS, i * 128:(i + 1) * 128], identity=ident_b[0:NS, 0:NS]),
                     reads=[smp_res, cres], writes=[ps_res[bk2]])
                P.op("act", lambda e: e.activation(out=A[:, kc, TM:T], in_=pb[:, 0:NS], func=AF.Copy), reads=[ps_res[bk2]], writes=[A_res[kc]])
    P.barrier()
    st.close()
    cgsT = []
    c_ = 128
    while c_ < T:
        cgsT.append((c_, min(T, c_ + 512) if c_ + 512 <= TM else (TM if c_ < TM else T)))
        c_ = cgsT[-1][1]
    with ExitStack() as st:
        epi_r = resid_epi_factory(st, "a")
        gemm(A, A_res, cgsT, epi_r, expect_W=att_w_out)
    P.barrier()
    dump_R("R3")
    mlp_and_ple(1)
    dump_R("R4")

    with ExitStack() as st:
        rb = [sb(f"frb{i}", [128, T], stack=st) for i in range(2)]
        rb_res = RL(2)
        yt = [sb(f"yt{i}", [128, 512], stack=st) for i in range(2)]
        yt_res = RL(2)
        tl = [(t * 128, 128) for t in range(1, NTM)] + [(TM, NS)]
        cnt = 0
        for g0 in range(0, KC, 4):
            gn = min(4, KC - g0)
            rbs = []
            for i in range(gn):
                s = (g0 + i) % 2
            for (c0, n) in tl:
                bk = aux_bank()
                u = cnt % 2
                cnt += 1
                for i in range(gn):
                    s = i % 2
                    P.dma(rb[s][:, 0:n], R[g0 + i, :, c0:c0 + n], reads=[R_res[g0 + i]], writes=[rb_res[s]])
                    P.op("pe", lambda e, i=i, s=s: e.transpose(out=ps[bk][0:n, i * 128:(i + 1) * 128], in_=rb[s][:, 0:n], identity=ident_f[:, :]),
                         reads=[rb_res[s], cres], writes=[ps_res[bk]])
                P.op("act", lambda e: e.activation(out=yt[u][0:n, 0:gn * 128], in_=ps[bk][0:n, 0:gn * 128], func=AF.Copy), reads=[ps_res[bk]], writes=[yt_res[u]])
                if c0 < TM:
                    P.dma(y_main[c0 - 128:c0 - 128 + n, g0 * 128:(g0 + gn) * 128], yt[u][0:n, 0:gn * 128], reads=[yt_res[u]], writes=[Res()])
                else:
                    P.dma(y_smp[:, g0 * 128:(g0 + gn) * 128], yt[u][0:n, 0:gn * 128], reads=[yt_res[u]], writes=[Res()])
    P.barrier()
    assert job_ptr[0] == len(jobs), (job_ptr[0], len(jobs))
    assert ws_state["ptr"] == len(tiles)
    es.close()
    return nc


def host_consts(cfg, hf):
    NH = cfg.NH
    lg = np.log1p(-np.exp2(-5.0 - np.arange(NH, dtype=np.float64)))
    idx = np.arange(128, dtype=np.float64)
    diff = idx[None, :] - idx[:, None]
    decT = np.where(diff[None] >= 0, np.exp(np.maximum(diff, 0)[None] * lg[:, None, None]), 0.0).astype(np.float32)
    qdec = np.exp((idx + 1.0)[None, :] * lg[:, None]).astype(np.float32)
    kdec = np.exp((127.0 - idx)[:, None] * lg[None, :]).astype(np.float32)
    k = np.arange(128)[:, None]
    q = np.arange(128)[None, :]
    dist_prev = 128 + q - k
    dist_cur = q - k
    negdist = np.concatenate([-dist_prev, -np.maximum(dist_cur, 0)], axis=1).astype(np.float32)
    mask = np.concatenate([np.where(dist_prev <= 128, 0.0, NEG), np.where(dist_cur >= 0, 0.0, NEG)], axis=1).astype(np.float32)
    first = np.full((128, 1), NEG if hf == 0 else 0.0, np.float32)
    sdist = (128.0 - np.arange(128, dtype=np.float32)).reshape(128, 1)
    return {"c_ident": np.eye(128, dtype=np.float32), "c_decT": decT, "c_qdec": qdec, "c_kdec": kdec,
            "c_negdist": negdist, "c_mask": mask, "c_first": first, "c_sdist": sdist}


def make_in_maps(cfg, inp):
    f = lambda a: np.ascontiguousarray(np.asarray(a, dtype=np.float32))
    xp = f(inp["x_prompt"])
    xs = f(inp["x_sample"])
    pp = f(inp["p_prompt"])
    psm = f(inp["p_sample"])
    st = f(inp["state_ret"])
    ck = f(inp["cache_k"])
    cv = f(inp["cache_v"])
    OWN, NS, D = cfg.OWN, cfg.NS, cfg.D
    shared = {
        "norm_mix": f(inp["norm_mix"]), "norm_mlp": f(inp["norm_mlp"]), "norm_ple": f(inp["norm_ple"]),
        "ret_w_in": f(inp["ret_w_in"])[0], "ret_gn": f(inp["ret_gn_gain"])[0], "ret_w_out": f(inp["ret_w_out"])[0],
        "att_w_qkv": f(inp["attn_w_qkv"])[0], "att_b": f(inp["attn_b_qkv"])[0], "att_qn": f(inp["attn_q_norm"])[0],
        "att_kn": f(inp["attn_k_norm"])[0], "att_sinks": f(inp["attn_sinks"])[0], "att_w_out": f(inp["attn_w_out"])[0],
        "mlp_up": f(inp["mlp_w_up"]), "mlp_down": f(inp["mlp_w_down"]), "ple_gate": f(inp["ple_w_gate"]), "ple_proj": f(inp["ple_w_proj"]),
    }
    maps = []
    for c in range(8):
        b, hf = c // 2, c % 2
        m = dict(shared)
        if hf == 1:
            m["x_main"] = np.ascontiguousarray(xp[b, OWN - 128:2 * OWN])
            m["x_ctx"] = np.ascontiguousarray(xp[b, 0:max(OWN - 128, 1)])
            m["p_main"] = np.ascontiguousarray(pp[:, b, OWN - 128:2 * OWN])
        else:
            m["x_main"] = np.concatenate([np.zeros((128, D), np.float32), xp[b, 0:OWN]], axis=0)
            m["x_ctx"] = np.zeros((max(OWN - 128, 1), D), np.float32)
            m["p_main"] = np.concatenate([np.zeros((2, 128, 256), np.float32), pp[:, b, 0:OWN]], axis=1)
        sl = slice(c * NS, (c + 1) * NS)
        m["x_smp"] = np.ascontiguousarray(xs[sl, 0])
        m["p_smp"] = np.ascontiguousarray(psm[:, sl, 0])
        m["state_in"] = np.ascontiguousarray(st[0, sl])
        m["cache_k"] = np.ascontiguousarray(ck[0, sl].reshape(NS, 128, cfg.KVW))
        m["cache_v"] = np.ascontiguousarray(cv[0, sl].reshape(NS, 128, cfg.KVW))
        m.update(host_consts(cfg, hf))
        maps.append(m)
    return maps


def assemble(cfg, res):
    B, OWN, NS, D, NH, NKV = cfg.B, cfg.OWN, cfg.NS, cfg.D, cfg.NH, cfg.NKV
    y_p = np.zeros((B, 2 * OWN, D), np.float32)
    y_s = np.zeros((8 * NS, 1, D), np.float32)
    sp = np.zeros((1, B, NH, 256, 512), np.float32)
    ss = np.zeros((1, 8 * NS, NH, 256, 512), np.float32)
    ckp = np.zeros((1, B, 128, NKV, 64), np.float32)
    cvp = np.zeros((1, B, 128, NKV, 64), np.float32)
    cks = np.zeros((1, 8 * NS, 128, NKV, 64), np.float32)
    cvs = np.zeros((1, 8 * NS, 128, NKV, 64), np.float32)
    for c in range(8):
        b, hf = c // 2, c % 2
        r = res[c]
        y_p[b, hf * OWN:(hf + 1) * OWN] = r["y_main"]
        sl = slice(c * NS, (c + 1) * NS)
        y_s[sl, 0] = r["y_smp"]
        ss[0, sl] = r["st_s"]
        cks[0, sl] = r["ck_s"].reshape(NS, 128, NKV, 64)
        cvs[0, sl] = r["cv_s"].reshape(NS, 128, NKV, 64)
        if hf == 1:
            sp[0, b] = r["st_p"]
            ckp[0, b] = r["ck_p"].reshape(128, NKV, 64)
            cvp[0, b] = r["cv_p"].reshape(128, NKV, 64)
    return (y_p, y_s, sp, ss, ckp, cvp, cks, cvs)


_NC_CACHE = {}


def kernel(**inputs):
    cfg = Cfg()
    if "nc" not in _NC_CACHE:
        _NC_CACHE["nc"] = build(cfg)
    nc = _NC_CACHE["nc"]
    maps = make_in_maps(cfg, inputs)
    res = run_bass_kernel_spmd(nc, maps, core_ids=list(range(8)))
    return assemble(cfg, res.results)
```

```python
import math
from contextlib import ExitStack

import numpy as np
import concourse.bass as bass
import concourse.mybir as mybir
from concourse.bass_utils import run_bass_kernel_spmd

F32 = mybir.dt.float32
BF16 = mybir.dt.bfloat16
AF = mybir.ActivationFunctionType
ALU = mybir.AluOpType
AX = mybir.AxisListType

RMS_EPS = 1e-6
GN_EPS = 1e-6
NEG = -30000.0


class Cfg:
    def __init__(self, D=4096, NH=16, NKV=8, DFF=16384, SEQ=2048, B=4, DECB=32, PAST=16384):
        self.D = D
        self.KC = D // 128
        self.NH = NH
        assert NH * 256 == D
        self.NQ = D // 64
        self.NKV = NKV
        self.G = self.NQ // NKV
        self.KVW = NKV * 64
        assert self.KVW % 128 == 0
        self.KVC = self.KVW // 128
        self.DFF = DFF
        self.FC = DFF // 128
        self.PLE = 256
        self.SEQ = SEQ
        self.B = B
        self.DECB = DECB
        self.OWN = SEQ // 2
        self.NTO = self.OWN // 128
        self.TM = 128 + self.OWN
        self.NTM = 1 + self.NTO
        self.NS = DECB // 8
        self.T = self.TM + self.NS
        self.CTX = self.OWN - 128
        self.NTC = self.CTX // 128
        self.KH = min(8, self.KC)
        self.WIN = D * 6

    def cgs(self, T):
        out = []
        c = 0
        while c < T:
            out.append((c, min(T, c + 512)))
            c += 512
        return out


class Res:
    __slots__ = ("w", "rs")

    def __init__(self):
        self.w = None
        self.rs = {}


def RL(n):
    return [Res() for _ in range(n)]


class Prog:
    NDS = 56

    def __init__(self, nc, es):
        self.nc = nc
        self.es = es
        self.E = {"pe": nc.tensor, "act": nc.scalar, "dve": nc.vector, "pool": nc.gpsimd, "sp": nc.sync}
        self.cur = {}
        self.known = {e: {} for e in self.E}
        self.dsems = []
        self.dptr = 0
        self.nsem = 0
        self.allsems = []
        self.last = {}
        self.pending = []

    def _newsem(self):
        self.nsem += 1
        s = self.es.enter_context(self.nc.semaphore(f"s{self.nsem}"))
        self.allsems.append(s)
        return s

    def _ticket(self, eng, instr):
        c = self.cur.get(eng)
        if c is None or c[1] >= 30000:
            c = [self._newsem(), 0]
            self.cur[eng] = c
        c[1] += 1
        instr.then_inc(c[0], 1)
        t = (c[0], c[1], eng)
        self.last[id(c[0])] = t
        return t

    def _wait(self, eng, deps):
        k = self.known[eng]
        best = {}
        for d in deps:
            sem, val, src = d
            if src == "pe" and eng == "pe":
                continue
            sid = id(sem)
            if k.get(sid, 0) >= val:
                continue
            if sid not in best or best[sid][1] < val:
                best[sid] = (sem, val)
        for sid, (sem, val) in best.items():
            self.E[eng].wait_ge(sem, val)
            k[sid] = val

    @staticmethod
    def _deps(reads, writes):
        d = []
        for r in reads:
            if r.w is not None:
                d.append(r.w)
        for w in writes:
            if w.w is not None:
                d.append(w.w)
            d.extend(w.rs.values())
        return d

    @staticmethod
    def _reg(t, reads, writes):
        key = id(t[0])
        for r in reads:
            old = r.rs.get(key)
            if old is None or old[1] < t[1]:
                r.rs[key] = t
        for w in writes:
            w.w = t
            w.rs = {}

    def op(self, eng, fn, reads=(), writes=()):
        self._wait(eng, self._deps(reads, writes))
        instr = fn(self.E[eng])
        t = self._ticket(eng, instr)
        self._reg(t, reads, writes)
        return t

    def dma(self, out, in_, reads=(), writes=(), q="sp", **kw):
        deps = self._deps(reads, writes)
        if len(self.dsems) < self.NDS:
            self.dsems.append([self._newsem(), 0])
            slot = self.dsems[-1]
        else:
            slot = self.dsems[self.dptr % self.NDS]
        self.dptr += 1
        if slot[1] > 0:
            deps.append((slot[0], slot[1], "dma"))
        self._wait(q, deps)
        slot[1] += 16
        self.E[q].dma_start(out=out, in_=in_, **kw).then_inc(slot[0], 16)
        t = (slot[0], slot[1], "dma")
        self.last[id(slot[0])] = t
        self._reg(t, reads, writes)
        return t

    def defer_dma(self, out, in_, reads=(), writes=(), **kw):
        self.pending.append((out, in_, list(reads), list(writes), kw))

    def flush(self):
        pend, self.pending = self.pending, []
        for (out, in_, reads, writes, kw) in pend:
            self.dma(out, in_, reads=reads, writes=writes, **kw)

    def barrier(self, engines=("pe", "act", "dve", "pool", "sp")):
        self.flush()
        deps = list(self.last.values())
        for e in engines:
            self._wait(e, deps)


class Rot:
    def __init__(self, items):
        self.items = items
        self.i = 0

    def next(self):
        it = self.items[self.i % len(self.items)]
        self.i += 1
        return it


def build(cfg, dbg=None):
    nc = bass.Bass("TRN2", target_bir_lowering=False)
    es = ExitStack()
    P = Prog(nc, es)
    D, KC, NH, T, TM, NTM, NS, KH = cfg.D, cfg.KC, cfg.NH, cfg.T, cfg.TM, cfg.NTM, cfg.NS, cfg.KH
    OWN, NTO, CTX, NTC, NQ, NKV, G, KVW, KVC = cfg.OWN, cfg.NTO, cfg.CTX, cfg.NTC, cfg.NQ, cfg.NKV, cfg.G, cfg.KVW, cfg.KVC
    DFF, FC = cfg.DFF, cfg.FC
    cgsT = cfg.cgs(T)
    NCG = len(cgsT)

    def din(name, shape, dt=F32):
        return nc.dram_tensor(name, list(shape), dt, kind="ExternalInput").ap()

    def dout(name, shape, dt=F32):
        return nc.dram_tensor(name, list(shape), dt, kind="ExternalOutput").ap()

    def dscr(name, shape, dt=F32):
        return nc.dram_tensor(name, list(shape), dt, kind="Internal").ap()

    x_main = din("x_main", [TM, D])
    x_ctx = din("x_ctx", [max(CTX, 1), D])
    x_smp = din("x_smp", [NS, D])
    p_main = din("p_main", [2, TM, 256])
    p_smp = din("p_smp", [2, NS, 256])
    state_in = din("state_in", [NS, NH, 256, 512])
    cache_k = din("cache_k", [NS, 128, KVW])
    cache_v = din("cache_v", [NS, 128, KVW])
    norm_mix = din("norm_mix", [2, D])
    norm_mlp = din("norm_mlp", [2, D])
    norm_ple = din("norm_ple", [2, D])
    ret_w_in = din("ret_w_in", [D, 6 * D])
    ret_gn = din("ret_gn", [2 * D])
    ret_w_out = din("ret_w_out", [2 * D, D])
    att_w_qkv = din("att_w_qkv", [D, D + 2 * KVW])
    att_b = din("att_b", [D + 2 * KVW])
    att_qn = din("att_qn", [64])
    att_kn = din("att_kn", [64])
    att_sinks = din("att_sinks", [NQ])
    att_w_out = din("att_w_out", [D, D])
    mlp_up = din("mlp_up", [2, D, DFF])
    mlp_down = din("mlp_down", [2, DFF, D])
    ple_gate = din("ple_gate", [2, D, D])
    ple_proj = din("ple_proj", [2, 256, D])
    c_ident = din("c_ident", [128, 128])
    c_decT = din("c_decT", [NH, 128, 128])
    c_qdec = din("c_qdec", [NH, 128])
    c_kdec = din("c_kdec", [128, NH])
    c_negdist = din("c_negdist", [128, 256])
    c_mask = din("c_mask", [128, 256])
    c_first = din("c_first", [128, 1])
    c_sdist = din("c_sdist", [128, 1])

    y_main = dout("y_main", [OWN, D])
    y_smp = dout("y_smp", [NS, D])
    st_p = dout("st_p", [NH, 256, 512])
    st_s = dout("st_s", [NS, NH, 256, 512])
    ck_p = dout("ck_p", [128, KVW])
    cv_p = dout("cv_p", [128, KVW])
    ck_s = dout("ck_s", [NS, 128, KVW])
    cv_s = dout("cv_s", [NS, 128, KVW])

    R = dscr("R", [KC, 128, T])
    O = dscr("O", [2 * KC, 128, T], BF16)
    H = dscr("H", [FC, 128, T], BF16)
    PP = dscr("PP", [KC, 128, T])
    S7 = dscr("S7", [NH, 256, 512])
    R_res = RL(KC)
    O_res = RL(2 * KC)
    H_res = RL(FC)
    PP_res = RL(KC)
    S7_res = RL(NH)
    dbg_out = {}
    if dbg:
        for name in dbg:
            if name.startswith("R"):
                dbg_out[name] = dout("dbg_" + name, [KC, 128, T])

    sbn = [0]

    def sb(name, shape, dt=F32, stack=es):
        sbn[0] += 1
        return stack.enter_context(nc.sbuf_tensor(f"{name}_{sbn[0]}", list(shape), dt))

    A = sb("A", [128, KC, T], BF16)
    A_res = RL(KC)
    NWS, NWB = 2, 3
    wst = [sb(f"wst{i}", [128, KH, 256]) for i in range(NWS)]
    wst_res = RL(NWS)
    wbf = [sb(f"wbf{i}", [128, KH, 256], BF16) for i in range(NWB)]
    wbf_res = RL(NWB)
    rstd = sb("rstd", [128, T])
    rstd_res = Res()
    ident_f = sb("ident_f", [128, 128])
    ident_b = sb("ident_b", [128, 128], BF16)
    ones_f = sb("ones_f", [128, 128])
    cres = Res()
    ps = [es.enter_context(nc.psum_tensor(f"ps{i}", [128, 512], F32)) for i in range(8)]
    ps_res = RL(8)
    auxrot = [0]

    def aux_bank():
        auxrot[0] += 1
        return 6 + (auxrot[0] % 2)

    P.dma(ident_f[:, :], c_ident[:, :], writes=[cres])
    P.op("dve", lambda e: e.tensor_copy(out=ident_b[:, :], in_=ident_f[:, :]), writes=[cres])
    P.op("dve", lambda e: e.memset(ones_f[:, :], 1.0), writes=[cres])

    class Job:
        def __init__(self, W, row0, panels, KCj):
            self.W, self.row0, self.panels, self.KCj = W, row0, panels, KCj

    def full_panels(N):
        return [(c, 2) for c in range(0, N, 256)]

    jobs = []

    def ret_job(h, full):
        pn = []
        if full:
            pn.append((h * 256, 2))
        pn.append((D + h * 256, 2))
        pn += [(2 * D + h * 512, 2), (2 * D + h * 512 + 256, 2)]
        if full:
            pn += [(4 * D + h * 512, 2), (4 * D + h * 512 + 256, 2)]
        return Job(ret_w_in, 0, pn, KC)

    def layer_tail_jobs(i):
        js = []
        js.append(Job(mlp_up[i], 0, full_panels(DFF), KC))
        for p_ in range(DFF // D):
            js.append(Job(mlp_down[i], p_ * D, full_panels(D), KC))
        js.append(Job(ple_proj[i], 0, full_panels(D), 2))
        js.append(Job(ple_gate[i], 0, full_panels(D), KC))
        return js

    if NTC > 0:
        for h in range(NH):
            jobs.append(ret_job(h, False))
    for h in range(NH):
        jobs.append(ret_job(h, True))
    for p_ in range(2):
        jobs.append(Job(ret_w_out, p_ * D, full_panels(D), KC))
    jobs += layer_tail_jobs(0)
    qkv_panels = []
    for c in range(D, D + 2 * KVW, 256):
        qkv_panels.append((c, min(2, (D + 2 * KVW - c) // 128)))
    qkv_panels += full_panels(D)
    jobs.append(Job(att_w_qkv, 0, qkv_panels, KC))
    jobs.append(Job(att_w_out, 0, full_panels(D), KC))
    jobs += layer_tail_jobs(1)

    tiles = []
    for jb in jobs:
        khj = min(KH, jb.KCj)
        for (c0, nch) in jb.panels:
            for hk in range(jb.KCj // khj):
                r0 = jb.row0 + hk * khj * 128
                tiles.append((jb.W[r0:r0 + khj * 128, c0:c0 + nch * 128], khj, nch * 128))

    class WS:
        def __init__(self):
            self.emitted = 0
            self.ptr = 0

        def _load(self, i):
            ap, kh, ncol = tiles[i]
            s = i % NWS
            P.dma(wst[s][:, 0:kh, 0:ncol], ap.rearrange("(k p) n -> p k n", p=128), writes=[wst_res[s]])

        def _conv(self, i):
            ap, kh, ncol = tiles[i]
            s, b = i % NWS, i % NWB
            eng = "pool"
            P.op(eng, lambda e: e.tensor_copy(out=wbf[b][:, 0:kh, 0:ncol], in_=wst[s][:, 0:kh, 0:ncol]),
                 reads=[wst_res[s]], writes=[wbf_res[b]])

        def get(self):
            i = self.ptr
            while self.emitted < min(len(tiles), i + 2):
                j = self.emitted
                self._load(j)
                self.emitted += 1
            self.ptr += 1
            return i

    ws_state = {"ld": 0, "cv": 0, "ptr": 0}

    def ws_emit_load():
        j = ws_state["ld"]
        ap, kh, ncol = tiles[j]
        s = j % NWS
        P.dma(wst[s][:, 0:kh, 0:ncol], ap.rearrange("(k p) n -> p k n", p=128), writes=[wst_res[s]])
        ws_state["ld"] += 1

    def ws_emit_conv():
        jj = ws_state["cv"]
        _, kh2, nc2 = tiles[jj]
        s2, b2 = jj % NWS, jj % NWB
        if jj % 2 == 0:
            P.op("act", lambda e: e.activation(out=wbf[b2][:, 0:kh2, 0:nc2], in_=wst[s2][:, 0:kh2, 0:nc2], func=AF.Copy),
                 reads=[wst_res[s2]], writes=[wbf_res[b2]])
        else:
            P.op("pool", lambda e: e.tensor_copy(out=wbf[b2][:, 0:kh2, 0:nc2], in_=wst[s2][:, 0:kh2, 0:nc2]),
                 reads=[wst_res[s2]], writes=[wbf_res[b2]])
        ws_state["cv"] += 1

    def ws_advance(upto):
        n = len(tiles)
        while ws_state["ld"] < min(n, upto) or ws_state["cv"] < min(n, upto):
            if ws_state["ld"] < min(n, upto) and ws_state["ld"] <= ws_state["cv"] + 1 - 1 + 1 and ws_state["ld"] - ws_state["cv"] < NWS:
                ws_emit_load()
            elif ws_state["cv"] < ws_state["ld"]:
                ws_emit_conv()
            else:
                break

    def ws_get():
        i = ws_state["ptr"]
        ws_advance(i + 2)
        P.flush()
        ws_state["ptr"] += 1
        return wbf[i % NWB], wbf_res[i % NWB]

    def ws_prefetch():
        ws_advance(ws_state["ptr"] + 3)

    job_ptr = [0]

    def gemm(At, Ares, cgs, epi, expect_W=None):
        jb = jobs[job_ptr[0]]
        job_ptr[0] += 1
        if expect_W is not None:
            assert jb.W.tensor.name == expect_W.tensor.name, (jb.W.tensor.name, expect_W.tensor.name)
        KCj = jb.KCj
        khj = min(KH, KCj)
        nh = KCj // khj
        ncg = len(cgs)
        jglob = 0
        for (c0, nch) in jb.panels:
            for hk in range(nh):
                wt, wres = ws_get()
                for ch in range(nch):
                    banks = [ch * 3 + ci for ci in range(ncg)]

                    def fn(e, wt=wt, ch=ch, hk=hk, banks=banks):
                        last = None
                        for k in range(khj):
                            kc = hk * khj + k
                            for ci, (a0, a1) in enumerate(cgs):
                                last = e.matmul(ps[banks[ci]][:, 0:a1 - a0], lhsT=wt[:, k, ch * 128:(ch + 1) * 128],
                                                rhs=At[:, kc, a0:a1], start=(kc == 0), stop=(kc == KCj - 1))
                        return last
                    P.op("pe", fn, reads=[wres] + [Ares[hk * khj + k] for k in range(khj)],
                         writes=[ps_res[b] for b in banks])
                    if hk == nh - 1:
                        epi(jglob + ch, c0 + ch * 128, banks)
            jglob += nch
        P.flush()
        ws_prefetch()

    def load_xT(xap, nrows_total, col0, stack):
        xt = [sb(f"xt{i}", [128, D], stack=stack) for i in range(2)]
        xt_res = RL(2)
        xs = [sb(f"xs{i}", [128, 4, 128], stack=stack) for i in range(2)]
        xs_res = RL(2)
        it = 0
        r = 0
        while r < nrows_total:
            n = min(128, nrows_total - r)
            s = it % 2
            P.dma(xt[s][0:n, :], xap[r:r + n, :], writes=[xt_res[s]])
            for g0 in range(0, KC, 4):
                gn = min(4, KC - g0)
                bk = aux_bank()
                u = (it * ((KC + 3) // 4) + g0 // 4) % 2

                def fn(e, s=s, n=n, g0=g0, gn=gn, bk=bk):
                    last = None
                    for i in range(gn):
                        last = e.transpose(out=ps[bk][:, i * 128:i * 128 + n], in_=xt[s][0:n, (g0 + i) * 128:(g0 + i + 1) * 128],
                                           identity=ident_f[0:n, 0:n])
                    return last
                P.op("pe", fn, reads=[xt_res[s], cres], writes=[ps_res[bk]])
                P.op("act", lambda e, u=u, n=n, gn=gn, bk=bk: e.activation(
                    out=xs[u][:, 0:gn, 0:n], in_=ps[bk][:, 0:gn * 128].rearrange("p (g c) -> p g c", g=gn)[:, :, 0:n], func=AF.Copy),
                    reads=[ps_res[bk]], writes=[xs_res[u]])
                P.dma(R[g0:g0 + gn, :, col0 + r:col0 + r + n].rearrange("k p c -> p k c"), xs[u][:, 0:gn, 0:n],
                      reads=[xs_res[u]], writes=R_res[g0:g0 + gn])
            r += n
            it += 1

    def norm_phase(gain_ap, Tn, cgs, stack, Rsrc=R, Rres=R_res):
        gcol = sb("gcol", [128, KC], stack=stack)
        gres = Res()
        P.dma(gcol[:, :], gain_ap.rearrange("(k p) -> p k", p=128), writes=[gres], allow_slow_non_contiguous=True)
        rb = [sb(f"rb{i}", [128, T], stack=stack) for i in range(2)]
        rb_res = RL(2)
        sq = [sb(f"sq{i}", [128, T], stack=stack) for i in range(2)]
        sq_res = RL(2)
        for kc in range(KC):
            s = kc % 2
            P.dma(rb[s][:, 0:Tn], Rsrc[kc, :, 0:Tn], reads=[Rres[kc]], writes=[rb_res[s]])
            P.op("act", lambda e, s=s, kc=kc: e.activation(out=A[:, kc, 0:Tn], in_=rb[s][:, 0:Tn], func=AF.Copy,
                                                           scale=gcol[:, kc:kc + 1]),
                 reads=[rb_res[s], gres], writes=[A_res[kc]])
            P.op("dve", lambda e, s=s: e.tensor_tensor(out=sq[s][:, 0:Tn], in0=rb[s][:, 0:Tn], in1=rb[s][:, 0:Tn], op=ALU.mult),
                 reads=[rb_res[s]], writes=[sq_res[s]])

            def fn(e, s=s, kc=kc):
                last = None
                for ci, (a0, a1) in enumerate(cgs):
                    last = e.matmul(ps[ci][:, 0:a1 - a0], lhsT=ones_f[:, :], rhs=sq[s][:, a0:a1], start=(kc == 0), stop=(kc == KC - 1))
                return last
            P.op("pe", fn, reads=[sq_res[s], cres], writes=[ps_res[ci] for ci in range(len(cgs))])
        for ci, (a0, a1) in enumerate(cgs):
            P.op("act", lambda e, ci=ci, a0=a0, a1=a1: e.activation(out=rstd[:, a0:a1], in_=ps[ci][:, 0:a1 - a0], func=AF.Ln,
                                                                    scale=1.0 / D, bias=eps_col[:, 0:1]),
                 reads=[ps_res[ci], cres], writes=[rstd_res])
        P.op("act", lambda e: e.activation(out=rstd[:, 0:Tn], in_=rstd[:, 0:Tn], func=AF.Exp, scale=-0.5), writes=[rstd_res])

    eps_col = sb("eps_col", [128, 1])
    P.op("dve", lambda e: e.memset(eps_col[:, :], RMS_EPS), writes=[cres])

    def resid_epi_factory(stack, tag, scale_rstd=False):
        rr = [sb(f"rr{tag}{i}", [128, T], stack=stack) for i in range(2)]
        rr_res = RL(2)
        cnt = [0]

        def epi(j, col, banks):
            s = cnt[0] % 2
            cnt[0] += 1
            P.dma(rr[s][:, :], R[j, :, :], reads=[R_res[j]], writes=[rr_res[s]])
            for ci, (a0, a1) in enumerate(cgsT):
                eng = "dve"
                P.op(eng, lambda e, s=s, ci=ci, a0=a0, a1=a1, b=banks[ci]: e.tensor_tensor(
                    out=rr[s][:, a0:a1], in0=ps[b][:, 0:a1 - a0], in1=rr[s][:, a0:a1], op=ALU.add),
                    reads=[ps_res[banks[ci]]], writes=[rr_res[s]])
            P.defer_dma(R[j, :, :], rr[s][:, :], reads=[rr_res[s]], writes=[R_res[j]])
        return epi

    def dump_R(name):
        if name in dbg_out:
            with ExitStack() as st:
                tb = sb("dbgt", [128, T], stack=st)
                tr = Res()
                for kc in range(KC):
                    P.dma(tb[:, :], R[kc, :, :], reads=[R_res[kc]], writes=[tr])
                    P.dma(dbg_out[name][kc, :, :], tb[:, :], reads=[tr], writes=[Res()])
                P.barrier()

    loggam = [math.log1p(-2.0 ** (-5 - h)) for h in range(NH)]

    def retention_pass(full, Tn, ntiles, stack):
        cgs = cfg.cgs(Tn)
        gm = lambda h: math.exp(loggam[h])
        g128 = lambda h: math.exp(128.0 * loggam[h])
        nt_all = ntiles + (1 if full else 0)
        kT = sb("kT", [128, 2, Tn], BF16, stack=stack)
        kT_res = Res()
        kd = sb("kd", [128, nt_all, 256], BF16, stack=stack)
        kd_res = Res()
        vtm = sb("vtm", [128, nt_all, 512], BF16, stack=stack)
        vtm_res = Res()
        fT = [sb(f"fT{i}", [128, Tn], BF16, stack=stack) for i in range(2)]
        fT_res = RL(2)
        Sf = sb("Sf", [128, 2, 512], stack=stack)
        Sf_res = Res()
        Sb = sb("Sb", [128, 2, 512], BF16, stack=stack)
        Sb_res = Res()
        kdec = sb("kdec", [128, NH], stack=stack)
        kdec_res = Res()
        P.dma(kdec[:, :], c_kdec[:, :], writes=[kdec_res])
        if full:
            qT = sb("qT", [128, 2, Tn], BF16, stack=stack)
            qT_res = Res()
            qdT = sb("qdT", [128, 2, TM], BF16, stack=stack)
            qdT_res = Res()
            sg = sb("sg", [128, nt_all, 512], BF16, stack=stack)
            sg_res = Res()
            oT = sb("oT", [128, 4, Tn], BF16, stack=stack)
            oT_res = Res()
            decT = [sb(f"decT{i}", [128, 128], stack=stack) for i in range(2)]
            decT_res = RL(2)
            qdec = [sb(f"qdec{i}", [128, 128], stack=stack) for i in range(2)]
            qdec_res = RL(2)
            gncol = sb("gncol", [128, 2 * KC], stack=stack)
            P.dma(gncol[:, :], ret_gn.rearrange("(k p) -> p k", p=128), writes=[kdec_res], allow_slow_non_contiguous=True)
            sT = [sb(f"sT{i}", [128, 128], BF16, stack=stack) for i in range(2)]
            sT_res = RL(2)
            o1 = [sb(f"o1{i}", [128, 512], stack=stack) for i in range(2)]
            o1_res = RL(2)
            og = [sb(f"og{i}", [128, 512], BF16, stack=stack) for i in range(2)]
            og_res = RL(2)
            st6 = [sb(f"st6{i}", [128, 6], stack=stack) for i in range(2)]
            mv = [sb(f"mv{i}", [128, 2], stack=stack) for i in range(2)]
            rs_o = [sb(f"rso{i}", [128, 1], stack=stack) for i in range(2)]
            stat_res = RL(2)
            rs_res = RL(2)
            nbias = [sb(f"nbias{i}", [128, 1], stack=stack) for i in range(2)]
            nb_res = RL(2)
            NSL = NS
            Ss = [sb(f"Ss{i}", [128, 2, 512], stack=stack) for i in range(NSL)]
            Ss_res = RL(NSL)
            Ssb = [sb(f"Ssb{i}", [128, 2, 512], BF16, stack=stack) for i in range(NSL)]
            Ssb_res = RL(NSL)
            Sso = [sb(f"Sso{i}", [128, 2, 512], stack=stack) for i in range(2)]
            Sso_res = RL(2)
            qm = [sb(f"qm{i}", [128, 2, NS], BF16, stack=stack) for i in range(2)]
            qm_res = RL(2)
            km = [sb(f"km{i}", [NS, 256], BF16, stack=stack) for i in range(2)]
            km_res = RL(2)
            geps = sb("geps", [128, 1], stack=stack)
            P.op("dve", lambda e: e.memset(geps[:, :], GN_EPS), writes=[kdec_res])
        fcnt = [0]
        ccnt = [0]

        def tile_rows(t):
            return NS if (full and t == ntiles) else 128

        def tile_cols(t):
            if full and t == ntiles:
                return (TM, T)
            return (t * 128, (t + 1) * 128)

        def transposes_to(srcT, src_res, dst, dst_res, dst_c0, evac):
            t = 0
            while t < nt_all:
                grp = []
                while t < nt_all and len(grp) < 8 and tile_rows(t) == 128:
                    grp.append(t)
                    t += 1
                if not grp:
                    grp = [t]
                    t += 1
                bk = aux_bank()
                pb = ps[bk].bitcast(BF16)

                def fn(e, grp=grp, pb=pb):
                    last = None
                    for i, tt in enumerate(grp):
                        a0, a1 = tile_cols(tt)
                        n = a1 - a0
                        last = e.transpose(out=pb[0:n, i * 128:(i + 1) * 128], in_=srcT(a0, a1), identity=ident_b[:, :])
                    return last
                P.op("pe", fn, reads=[src_res, cres], writes=[ps_res[bk]])
                n = tile_rows(grp[0])
                evac(grp, pb, bk, n)

        for h in range(NH):
            hb = h % 2
            if full:
                P.dma(decT[hb][:, :], c_decT[h, :, :], writes=[decT_res[hb]])
                P.dma(qdec[hb][:, :], c_qdec[h, :].partition_broadcast(128), writes=[qdec_res[hb]])
                P.dma(Sf[:, :, :], S7[h, :, :].rearrange("(c p) e -> p c e", p=128), reads=[S7_res[h]], writes=[Sf_res])
                P.op("act", lambda e: e.activation(out=Sb[:, :, :], in_=Sf[:, :, :], func=AF.Copy), reads=[Sf_res], writes=[Sb_res])
            else:
                P.op("dve", lambda e: e.memset(Sf[:, :, :], 0.0), writes=[Sf_res])

            late = []

            def run_late():
                pend = list(late)
                del late[:]
                for f in pend:
                    f()

            def epi(j, col, banks, h=h, hb=hb):
                run_late()
                kind = col // D
                if kind == 0:
                    dc = (col - h * 256) // 128
                    for ci, (a0, a1) in enumerate(cgs):
                        P.op("dve", lambda e, ci=ci, a0=a0, a1=a1, b=banks[ci]: e.tensor_tensor(
                            out=qT[:, dc, a0:a1], in0=ps[b][:, 0:a1 - a0], in1=rstd[:, a0:a1], op=ALU.mult),
                            reads=[ps_res[banks[ci]], rstd_res], writes=[qT_res])
                    P.op("pool", lambda e: e.tensor_tensor(
                        out=qdT[:, dc, :].rearrange("p (t c) -> p t c", c=128),
                        in0=qT[:, dc, 0:TM].rearrange("p (t c) -> p t c", c=128),
                        in1=qdec[hb][:, :].unsqueeze(1).broadcast_to([128, NTM, 128]), op=ALU.mult),
                        reads=[qT_res, qdec_res[hb]], writes=[qdT_res])
                elif kind == 1:
                    dc = (col - D - h * 256) // 128
                    for ci, (a0, a1) in enumerate(cgs):
                        P.op("dve", lambda e, ci=ci, a0=a0, a1=a1, b=banks[ci]: e.scalar_tensor_tensor(
                            out=kT[:, dc, a0:a1], in0=ps[b][:, 0:a1 - a0], scalar=1.0 / 16.0, in1=rstd[:, a0:a1],
                            op0=ALU.mult, op1=ALU.mult),
                            reads=[ps_res[banks[ci]], rstd_res], writes=[kT_res])

                    def evac(grp, pb, bk, n):
                        t0 = grp[0]
                        if n == 128:
                            P.op("act", lambda e: e.activation(
                                out=kd[:, t0:t0 + len(grp), dc * 128:(dc + 1) * 128],
                                in_=pb[:, 0:len(grp) * 128].rearrange("p (t c) -> p t c", c=128),
                                func=AF.Copy, scale=kdec[:, h:h + 1]),
                                reads=[ps_res[bk], kdec_res], writes=[kd_res])
                        else:
                            P.op("act", lambda e: e.activation(out=kd[0:n, t0, dc * 128:(dc + 1) * 128], in_=pb[0:n, 0:128], func=AF.Copy),
                                 reads=[ps_res[bk]], writes=[kd_res])
                    late.append(lambda: transposes_to(lambda a0, a1: kT[:, dc, a0:a1], kT_res, kd, kd_res, dc * 128, evac))
                else:
                    isv = kind in (2, 3)
                    base = (2 * D if isv else 4 * D) + h * 512
                    ec = (col - base) // 128
                    s = fcnt[0] % 2
                    fcnt[0] += 1
                    for ci, (a0, a1) in enumerate(cgs):
                        P.op("dve", lambda e, ci=ci, a0=a0, a1=a1, b=banks[ci]: e.tensor_tensor(
                            out=fT[s][:, a0:a1], in0=ps[b][:, 0:a1 - a0], in1=rstd[:, a0:a1], op=ALU.mult),
                            reads=[ps_res[banks[ci]], rstd_res], writes=[fT_res[s]])
                    dst, dres = (vtm, vtm_res) if isv else (sg, sg_res)
                    func = AF.Copy if isv else AF.Silu

                    def evac(grp, pb, bk, n):
                        t0 = grp[0]
                        if n == 128:
                            P.op("act", lambda e: e.activation(
                                out=dst[:, t0:t0 + len(grp), ec * 128:(ec + 1) * 128],
                                in_=pb[:, 0:len(grp) * 128].rearrange("p (t c) -> p t c", c=128), func=func),
                                reads=[ps_res[bk]], writes=[dres])
                        else:
                            P.op("act", lambda e: e.activation(out=dst[0:n, t0, ec * 128:(ec + 1) * 128], in_=pb[0:n, 0:128], func=func),
                                 reads=[ps_res[bk]], writes=[dres])
                    late.append(lambda: transposes_to(lambda a0, a1: fT[s][:, a0:a1], fT_res[s], dst, dres, ec * 128, evac))

            gemm(A, A_res, cgs, epi, expect_W=ret_w_in)
            run_late()

            def sample_load(s_):
                u_ = s_ % NSL
                P.dma(Ss[u_][:, :, :], state_in[s_, h, :, :].rearrange("(c p) e -> p c e", p=128), writes=[Ss_res[u_]])
                P.op("act", lambda e: e.activation(out=Ssb[u_][:, :, :], in_=Ss[u_][:, :, :], func=AF.Copy), reads=[Ss_res[u_]], writes=[Ssb_res[u_]])

            def sample_prefetch():
                for s_ in range(NS):
                    sample_load(s_)

            def s2_gn(n, obank, t, u):
                P.op("dve", lambda e: e.bn_stats(out=st6[u][0:n, :], in_=ps[obank][0:n, :]), reads=[ps_res[obank]], writes=[stat_res[u]])
                P.op("dve", lambda e: e.bn_aggr(out=mv[u][0:n, :], in_=st6[u][0:n, :]), writes=[stat_res[u]])
                P.op("act", lambda e: e.activation(out=rs_o[u][0:n, :], in_=mv[u][0:n, 1:2], func=AF.Ln, bias=geps[0:n, :], scale=1.0),
                     reads=[kdec_res, stat_res[u]], writes=[rs_res[u]])
                P.op("act", lambda e: e.activation(out=rs_o[u][0:n, :], in_=rs_o[u][0:n, :], func=AF.Exp, scale=-0.5), writes=[rs_res[u]])
                P.op("dve", lambda e: e.scalar_tensor_tensor(out=nbias[u][0:n, :], in0=mv[u][0:n, 0:1], scalar=-1.0, in1=rs_o[u][0:n, :],
                                                             op0=ALU.mult, op1=ALU.mult),
                     reads=[stat_res[u], rs_res[u]], writes=[nb_res[u]])
                P.op("act", lambda e: e.activation(out=o1[u][0:n, :], in_=ps[obank][0:n, :], func=AF.Identity, scale=rs_o[u][0:n, :], bias=nbias[u][0:n, :]),
                     reads=[ps_res[obank], nb_res[u], rs_res[u]], writes=[o1_res[u]])

            def s2b(n, t, u):
                P.op("dve", lambda e: e.tensor_tensor(out=og[u][0:n, :], in0=o1[u][0:n, :], in1=sg[0:n, t, :], op=ALU.mult),
                     reads=[o1_res[u], sg_res], writes=[og_res[u]])

            def s3_tr(n, u, ocols):
                bk = aux_bank()
                pb = ps[bk].bitcast(BF16)

                def fn(e):
                    last = None
                    for ec in range(4):
                        last = e.transpose(out=pb[:, ec * 128:ec * 128 + n], in_=og[u][0:n, ec * 128:(ec + 1) * 128], identity=ident_b[0:n, 0:n])
                    return last
                P.op("pe", fn, reads=[og_res[u], cres], writes=[ps_res[bk]])
                for ec in range(4):
                    P.op("act", lambda e, ec=ec: e.activation(out=oT[:, ec, ocols[0]:ocols[1]], in_=pb[:, ec * 128:ec * 128 + n], func=AF.Copy,
                                                              scale=gncol[:, 4 * h + ec:4 * h + ec + 1]),
                         reads=[ps_res[bk], kdec_res], writes=[oT_res])

            def s1(c):
                a0, a1 = c * 128, (c + 1) * 128
                if full:
                    u = c % 2
                    ob = 1 if c % 2 == 0 else 5
                    P.op("pe", lambda e: [e.matmul(ps[0][:, 0:128], lhsT=kT[:, dc, a0:a1], rhs=qT[:, dc, a0:a1], start=(dc == 0), stop=(dc == 1))
                                          for dc in range(2)][-1],
                         reads=[kT_res, qT_res], writes=[ps_res[0]])
                    P.op("dve", lambda e: e.tensor_tensor(out=sT[u][:, :], in0=ps[0][:, 0:128], in1=decT[hb][:, :], op=ALU.mult),
                         reads=[ps_res[0], decT_res[hb]], writes=[sT_res[u]])

                P.op("pe", lambda e: [e.matmul(ps[2 + dc][:, :], lhsT=kd[:, c, dc * 128:(dc + 1) * 128], rhs=vtm[:, c, :], start=True, stop=True)
                                      for dc in range(2)][-1],
                     reads=[kd_res, vtm_res], writes=[ps_res[2], ps_res[3]])

            def s1o(c):
                a0, a1 = c * 128, (c + 1) * 128
                u = c % 2
                ob = 1 if c % 2 == 0 else 5

                def fn(e):
                    e.matmul(ps[ob][:, :], lhsT=sT[u][:, :], rhs=vtm[:, c, :], start=True, stop=False)
                    last = None
                    for dc in range(2):
                        last = e.matmul(ps[ob][:, :], lhsT=qdT[:, dc, a0:a1], rhs=Sb[:, dc, :], start=False, stop=(dc == 1))
                    return last
                P.op("pe", fn, reads=[sT_res[u], vtm_res, qdT_res, Sb_res], writes=[ps_res[ob]])

            def s1u(c):
                for dc in range(2):
                    P.op("dve", lambda e, dc=dc: e.scalar_tensor_tensor(out=Sf[:, dc, :], in0=Sf[:, dc, :], scalar=g128(h), in1=ps[2 + dc][:, :],
                                                                        op0=ALU.mult, op1=ALU.add),
                         reads=[ps_res[2 + dc]], writes=[Sf_res])
                if full and c < ntiles - 1:
                    P.op("act", lambda e: e.activation(out=Sb[:, :, :], in_=Sf[:, :, :], func=AF.Copy), reads=[Sf_res], writes=[Sb_res])

            if full:
                sample_prefetch()
            s1(0)
            if full:
                s1o(0)
            s1u(0)
            for c in range(ntiles):
                if c + 1 < ntiles:
                    s1(c + 1)
                if full:
                    if c >= 2:
                        s3_tr(128, (c - 2) % 2, ((c - 2) * 128, (c - 1) * 128))
                    if c + 1 < ntiles:
                        s1o(c + 1)
                    s2_gn(128, 1 if c % 2 == 0 else 5, c, c % 2)
                if c + 1 < ntiles:
                    s1u(c + 1)
                if full and c >= 1:
                    s2b(128, c - 1, (c - 1) % 2)
            if full:
                if ntiles >= 2:
                    s3_tr(128, (ntiles - 2) % 2, ((ntiles - 2) * 128, (ntiles - 1) * 128))
                s2b(128, ntiles - 1, (ntiles - 1) % 2)
                s3_tr(128, (ntiles - 1) % 2, ((ntiles - 1) * 128, ntiles * 128))
            if not full:
                P.dma(S7[h, :, :].rearrange("(c p) e -> p c e", p=128), Sf[:, :, :], reads=[Sf_res], writes=[S7_res[h]])
                continue
            P.dma(st_p[h, :, :].rearrange("(c p) e -> p c e", p=128), Sf[:, :, :], reads=[Sf_res], writes=[Res()])
            ts = ntiles
            P.op("pe", lambda e: [e.matmul(ps[0][0:NS, 0:NS], lhsT=kT[:, dc, TM:T], rhs=qT[:, dc, TM:T], start=(dc == 0), stop=(dc == 1))
                                  for dc in range(2)][-1],
                 reads=[kT_res, qT_res], writes=[ps_res[0]])
            P.op("dve", lambda e: e.tensor_tensor(out=sT[0][0:NS, 0:NS], in0=ps[0][0:NS, 0:NS], in1=ident_f[0:NS, 0:NS], op=ALU.mult),
                 reads=[ps_res[0], cres], writes=[sT_res[0]])
            for s in range(NS):
                u = s % 2
                us = s % NSL
                P.op("dve", lambda e: e.scalar_tensor_tensor(
                    out=qm[u][:, :, :], in0=qT[:, :, TM:T], scalar=gm(h),
                    in1=selrow[:, s, :].unsqueeze(1).broadcast_to([128, 2, NS]),
                    op0=ALU.mult, op1=ALU.mult),
                    reads=[qT_res, cres], writes=[qm_res[u]])

                def fn(e, s=s, u=u):
                    if s == 0:
                        e.matmul(ps[1][0:NS, :], lhsT=sT[0][0:NS, 0:NS], rhs=vtm[0:NS, ts, :], start=True, stop=False)
                    last = None
                    for dc in range(2):
                        last = e.matmul(ps[1][0:NS, :], lhsT=qm[u][:, dc, :], rhs=Ssb[us][:, dc, :], start=False, stop=(s == NS - 1 and dc == 1))
                    return last
                P.op("pe", fn, reads=[sT_res[0], vtm_res, qm_res[u], Ssb_res[us]], writes=[ps_res[1]])
                P.op("dve", lambda e: e.tensor_scalar(out=km[u][:, :], in0=kd[0:NS, ts, :], scalar1=ident_f[0:NS, s:s + 1], scalar2=None, op0=ALU.mult),
                     reads=[kd_res, cres], writes=[km_res[u]])
                P.op("pe", lambda e: [e.matmul(ps[2 + dc][:, :], lhsT=km[u][:, dc * 128:(dc + 1) * 128], rhs=vtm[0:NS, ts, :], start=True, stop=True)
                                      for dc in range(2)][-1],
                     reads=[km_res[u], vtm_res], writes=[ps_res[2], ps_res[3]])
                for dc in range(2):
                    P.op("dve", lambda e, dc=dc: e.scalar_tensor_tensor(out=Sso[u][:, dc, :], in0=Ss[us][:, dc, :], scalar=gm(h), in1=ps[2 + dc][:, :],
                                                                        op0=ALU.mult, op1=ALU.add),
                         reads=[ps_res[2 + dc], Ss_res[us]], writes=[Sso_res[u]])
                P.defer_dma(st_s[s, h, :, :].rearrange("(c p) e -> p c e", p=128), Sso[u][:, :, :], reads=[Sso_res[u]], writes=[Res()])
                if s >= 1:
                    P.flush()
            P.flush()
            s2_gn(NS, 1, ts, 0)
            s2b(NS, ts, 0)
            s3_tr(NS, 0, (TM, T))
            for ec in range(4):
                P.dma(O[4 * h + ec, :, :], oT[:, ec, :], reads=[oT_res], writes=[O_res[4 * h + ec]])

    selrow = sb("selrow", [128, NS, NS])
    P.op("dve", lambda e: e.memset(selrow[:, :, :], 0.0), writes=[cres])
    for s in range(NS):
        P.op("dve", lambda e, s=s: e.memset(selrow[:, s, s:s + 1], 1.0), writes=[cres])

    if NTC > 0:
        with ExitStack() as st:
            load_xT(x_ctx, CTX, 0, st)
        P.barrier()
        with ExitStack() as st:
            norm_phase(norm_mix[0], CTX, cfg.cgs(CTX), st)
        P.barrier()
        with ExitStack() as st:
            retention_pass(False, CTX, NTC, st)
        P.barrier()
    else:
        with ExitStack() as st:
            z = sb("z", [128, 2, 512], stack=st)
            zr = Res()
            P.op("dve", lambda e: e.memset(z[:, :, :], 0.0), writes=[zr])
            for h in range(NH):
                P.dma(S7[h, :, :].rearrange("(c p) e -> p c e", p=128), z[:, :, :], reads=[zr], writes=[S7_res[h]])
        P.barrier()

    with ExitStack() as st:
        load_xT(x_main, TM, 0, st)
        load_xT(x_smp, NS, TM, st)
    P.barrier()
    dump_R("R0")

    def mlp_and_ple(i):
        with ExitStack() as st:
            norm_phase(norm_mlp[i], T, cgsT, st)
        P.barrier()
        with ExitStack() as st:
            hf32 = [sb(f"hf{k}", [128, T], stack=st) for k in range(2)]
            hf_res = RL(2)
            hb16 = [sb(f"hb{k}", [128, T], BF16, stack=st) for k in range(2)]
            hb_res = RL(2)
            cnt = [0]

            def epi_up(j, col, banks):
                s = cnt[0] % 2
                cnt[0] += 1
                for ci, (a0, a1) in enumerate(cgsT):
                    P.op("dve", lambda e, ci=ci, a0=a0, a1=a1, b=banks[ci]: e.scalar_tensor_tensor(
                        out=hf32[s][:, a0:a1], in0=ps[b][:, 0:a1 - a0], scalar=0.0, in1=rstd[:, a0:a1], op0=ALU.max, op1=ALU.mult),
                        reads=[ps_res[banks[ci]], rstd_res], writes=[hf_res[s]])
                P.op("act", lambda e: e.activation(out=hb16[s][:, :], in_=hf32[s][:, :], func=AF.Square), reads=[hf_res[s]], writes=[hb_res[s]])
                P.defer_dma(H[j, :, :], hb16[s][:, :], reads=[hb_res[s]], writes=[H_res[j]])
            gemm(A, A_res, cgsT, epi_up, expect_W=mlp_up)
        P.barrier()
        with ExitStack() as st:
            epi_r = resid_epi_factory(st, "d")
            for p_ in range(DFF // D):
                for kc in range(KC):
                    P.dma(A[:, kc, :], H[p_ * KC + kc, :, :], reads=[H_res[p_ * KC + kc]], writes=[A_res[kc]])
                gemm(A, A_res, cgsT, epi_r, expect_W=mlp_down)
        P.barrier()
        with ExitStack() as st:
            pT = sb("pT", [128, 2, T], BF16, stack=st)
            pT_res = RL(2)
            pt = [sb(f"pt{k}", [128, 256], stack=st) for k in range(2)]
            pt_res = RL(2)
            tl = [(t * 128, 128) for t in range(NTM)] + [(TM, NS)]
            for ti, (c0, n) in enumerate(tl):
                s = ti % 2
                src = p_main[i, c0:c0 + n, :] if c0 < TM else p_smp[i, :, :]
                P.dma(pt[s][0:n, :], src, writes=[pt_res[s]])
                bk = aux_bank()
                P.op("pe", lambda e: [e.transpose(out=ps[bk][:, dc * 128:dc * 128 + n], in_=pt[s][0:n, dc * 128:(dc + 1) * 128], identity=ident_f[0:n, 0:n])
                                      for dc in range(2)][-1],
                     reads=[pt_res[s], cres], writes=[ps_res[bk]])
                P.op("act", lambda e: e.activation(out=pT[:, :, c0:c0 + n], in_=ps[bk][:, 0:256].rearrange("p (g c) -> p g c", g=2)[:, :, 0:n], func=AF.Copy),
                     reads=[ps_res[bk]], writes=pT_res)
            pp = [sb(f"pp{k}", [128, T], stack=st) for k in range(2)]
            pp_res = RL(2)
            cnt = [0]

            def epi_pp(j, col, banks):
                s = cnt[0] % 2
                cnt[0] += 1
                for ci, (a0, a1) in enumerate(cgsT):
                    P.op("act", lambda e, ci=ci, a0=a0, a1=a1, b=banks[ci]: e.activation(out=pp[s][:, a0:a1], in_=ps[b][:, 0:a1 - a0], func=AF.Copy),
                         reads=[ps_res[banks[ci]]], writes=[pp_res[s]])
                P.defer_dma(PP[j, :, :], pp[s][:, :], reads=[pp_res[s]], writes=[PP_res[j]])
            gemm(pT, pT_res, cgsT, epi_pp, expect_W=ple_proj)
        P.barrier()
        with ExitStack() as st:
            norm_phase(norm_ple[i], T, cgsT, st)
        P.barrier()
        with ExitStack() as st:
            rr = [sb(f"rrg{k}", [128, T], stack=st) for k in range(2)]
            rr_res = RL(2)
            pq = [sb(f"pq{k}", [128, T], stack=st) for k in range(2)]
            pq_res = RL(2)
            gt = [sb(f"gt{k}", [128, T], stack=st) for k in range(2)]
            gt_res = RL(2)
            cnt = [0]

            def epi_gate(j, col, banks):
                s = cnt[0] % 2
                cnt[0] += 1
                P.dma(rr[s][:, :], R[j, :, :], reads=[R_res[j]], writes=[rr_res[s]])
                P.dma(pq[s][:, :], PP[j, :, :], reads=[PP_res[j]], writes=[pq_res[s]])
                for ci, (a0, a1) in enumerate(cgsT):
                    P.op("dve", lambda e, ci=ci, a0=a0, a1=a1, b=banks[ci]: e.tensor_tensor(
                        out=gt[s][:, a0:a1], in0=ps[b][:, 0:a1 - a0], in1=rstd[:, a0:a1], op=ALU.mult),
                        reads=[ps_res[banks[ci]], rstd_res], writes=[gt_res[s]])
                P.op("act", lambda e: e.activation(out=gt[s][:, :], in_=gt[s][:, :], func=AF.Sigmoid), writes=[gt_res[s]])
                P.op("dve", lambda e: e.tensor_tensor(out=gt[s][:, :], in0=gt[s][:, :], in1=pq[s][:, :], op=ALU.mult),
                     reads=[pq_res[s]], writes=[gt_res[s]])
                P.op("dve", lambda e: e.tensor_tensor(out=rr[s][:, :], in0=rr[s][:, :], in1=gt[s][:, :], op=ALU.add),
                     reads=[gt_res[s]], writes=[rr_res[s]])
                P.defer_dma(R[j, :, :], rr[s][:, :], reads=[rr_res[s]], writes=[R_res[j]])
            gemm(A, A_res, cgsT, epi_gate, expect_W=ple_gate)
        P.barrier()

    with ExitStack() as st:
        norm_phase(norm_mix[0], T, cgsT, st)
    P.barrier()
    with ExitStack() as st:
        retention_pass(True, T, NTM, st)
    P.barrier()
    with ExitStack() as st:
        epi_r = resid_epi_factory(st, "o")
        for p_ in range(2):
            for kc in range(KC):
                P.dma(A[:, kc, :], O[p_ * KC + kc, :, :], reads=[O_res[p_ * KC + kc]], writes=[A_res[kc]])
            gemm(A, A_res, cgsT, epi_r, expect_W=ret_w_out)
    P.barrier()
    dump_R("R1")
    mlp_and_ple(0)
    dump_R("R2")

    with ExitStack() as st:
        norm_phase(norm_mix[1], T, cgsT, st)
    P.barrier()
    slopes = [2.0 ** (-8.0 * (hh + 1) / NQ) for hh in range(NQ)]
    st = ExitStack()
    st2 = ExitStack()
    if True:
        NT1 = NTM + 1
        bcol = sb("bcol", [128, (D + 2 * KVW) // 128], stack=st)
        qg = sb("qg", [128, 1], stack=st)
        kg = sb("kg", [128, 1], stack=st)
        e64 = sb("e64", [128, 1], stack=st)
        esink = sb("esink", [128, NQ], stack=st)
        negd = sb("negd", [128, 256], stack=st)
        msk = sb("msk", [128, 256], stack=st)
        first = sb("first", [128, 1], stack=st)
        sdist = sb("sdist", [128, 1], stack=st)
        blk = sb("blk", [128, 128], stack=st)
        c1 = Res()
        P.dma(bcol[:, :], att_b.rearrange("(k p) -> p k", p=128), writes=[c1], allow_slow_non_contiguous=True)
        for half in range(2):
            P.dma(qg[half * 64:(half + 1) * 64, :], att_qn.rearrange("(p o) -> p o", o=1), writes=[c1], allow_slow_non_contiguous=True)
            P.dma(kg[half * 64:(half + 1) * 64, :], att_kn.rearrange("(p o) -> p o", o=1), writes=[c1], allow_slow_non_contiguous=True)
        P.op("act", lambda e: e.activation(out=qg[:, :], in_=qg[:, :], func=AF.Copy, scale=0.125), writes=[c1])
        P.dma(esink[:, :], att_sinks.partition_broadcast(128), writes=[c1])
        P.op("act", lambda e: e.activation(out=esink[:, :], in_=esink[:, :], func=AF.Exp), writes=[c1])
        P.dma(negd[:, :], c_negdist[:, :], writes=[c1])
        P.dma(msk[:, :], c_mask[:, :], writes=[c1])
        P.dma(first[:, :], c_first[:, :], writes=[c1])
        P.dma(sdist[:, :], c_sdist[:, :], writes=[c1])
        P.op("dve", lambda e: e.memset(e64[:, :], RMS_EPS), writes=[c1])
        P.op("dve", lambda e: e.memset(blk[:, :], 0.0), writes=[c1])
        P.op("dve", lambda e: e.memset(blk[0:64, 0:64], 1.0), writes=[c1])
        P.op("dve", lambda e: e.memset(blk[64:128, 64:128], 1.0), writes=[c1])

        knf = sb("knf", [128, KVC, 128 + NS], stack=st)
        vf = sb("vf", [128, KVC, 128 + NS], stack=st)
        knf_res = Res()
        vf_res = Res()
        qsT = sb("qsT", [128, KC, NS], stack=st)
        qsT_res = Res()
        knT = [sb(f"knT{k}", [128, KVC, T], BF16, stack=st) for k in range(2)]
        knT_res = [Res(), Res()]
        vaug = sb("vaug", [128, NT1, NKV, 65], BF16, stack=st)
        vaug_res = Res()
        P.op("dve", lambda e: e.memset(vaug[:, :, :, 64:65], 1.0), writes=[vaug_res])
        xb = [sb(f"xb{k}", [128, T], stack=st2) for k in range(2)]
        xb_res = RL(2)
        xq = [sb(f"xq{k}", [128, T], stack=st2) for k in range(2)]
        xq_res = RL(2)
        qn = [sb(f"qn{k}", [128, T], BF16, stack=st2) for k in range(2)]
        qn_res = RL(2)
        cnt = [0]
        ecnt = [0]
        acnt = [0]
        LAST0 = NTO * 128

        def normed(j, col, banks, gcolv):
            s = cnt[0] % 2
            cnt[0] += 1
            bi = col // 128
            for ci, (a0, a1) in enumerate(cgsT):
                P.op("dve", lambda e, ci=ci, a0=a0, a1=a1, b=banks[ci]: e.tensor_tensor(
                    out=xb[s][:, a0:a1], in0=ps[b][:, 0:a1 - a0], in1=rstd[:, a0:a1], op=ALU.mult),
                    reads=[ps_res[banks[ci]], rstd_res], writes=[xb_res[s]])
            P.op("act", lambda e: e.activation(out=xb[s][:, :], in_=xb[s][:, :], func=AF.Identity, bias=bcol[:, bi:bi + 1], scale=1.0),
                 reads=[c1], writes=[xb_res[s]])
            if gcolv is None:
                return s
            P.op("act", lambda e: e.activation(out=xq[s][:, :], in_=xb[s][:, :], func=AF.Square), reads=[xb_res[s]], writes=[xq_res[s]])
            return s

        def normed_b(s):
            for ci, (a0, a1) in enumerate(cgsT):
                bk = aux_bank()
                P.op("pe", lambda e, a0=a0, a1=a1, bk=bk: e.matmul(ps[bk][:, 0:a1 - a0], lhsT=blk[:, :], rhs=xq[s][:, a0:a1], start=True, stop=True),
                     reads=[xq_res[s], c1], writes=[ps_res[bk]])
                P.op("act", lambda e, a0=a0, a1=a1, bk=bk: e.activation(out=xq[s][:, a0:a1], in_=ps[bk][:, 0:a1 - a0], func=AF.Ln, scale=1.0 / 64.0,
                                                                       bias=e64[:, 0:1]),
                     reads=[ps_res[bk], c1], writes=[xq_res[s]])
            P.op("act", lambda e: e.activation(out=xq[s][:, :], in_=xq[s][:, :], func=AF.Exp, scale=-0.5), writes=[xq_res[s]])

        def nrm_out(s, gcolv, out_ap, c0, c1_, reads_extra, wres):
            P.op("dve", lambda e: e.scalar_tensor_tensor(out=out_ap, in0=xb[s][:, c0:c1_], scalar=gcolv[:, 0:1], in1=xq[s][:, c0:c1_], op0=ALU.mult, op1=ALU.mult),
                 reads=[xb_res[s], xq_res[s], c1] + reads_extra, writes=[wres])

        qlate = []

        def run_qlate():
            pend = list(qlate)
            del qlate[:]
            for f in pend:
                f()

        def epi_qkv(j, col, banks):
            run_qlate()
            if col >= D + KVW:
                vc = (col - D - KVW) // 128
                s = normed(j, col, banks, None)
                P.op("pool", lambda e: e.tensor_copy(out=vf[:, vc, 0:128], in_=xb[s][:, LAST0:LAST0 + 128]), reads=[xb_res[s]], writes=[vf_res])
                P.op("pool", lambda e: e.tensor_copy(out=vf[:, vc, 128:128 + NS], in_=xb[s][:, TM:T]), reads=[xb_res[s]], writes=[vf_res])
                P.op("act", lambda e: e.activation(out=qn[s][:, :], in_=xb[s][:, :], func=AF.Copy), reads=[xb_res[s]], writes=[qn_res[s]])
                tl = [(t * 128, 128, t) for t in range(NTM)] + [(TM, NS, NTM)]
                for g0 in range(0, NTM, 8):
                    subs = [tl[g0:min(NTM, g0 + 8)]]
                    if g0 + 8 >= NTM:
                        subs.append([tl[NTM]])
                    for sub in subs:
                        bk = aux_bank()
                        pb = ps[bk].bitcast(BF16)
                        n = sub[0][1]
                        P.op("pe", lambda e, sub=sub, pb=pb: [e.transpose(out=pb[0:g[1], i * 128:(i + 1) * 128], in_=qn[s][:, g[0]:g[0] + g[1]], identity=ident_b[:, :])
                                                              for i, g in enumerate(sub)][-1],
                             reads=[qn_res[s], cres], writes=[ps_res[bk]])
                        t0 = sub[0][2]
                        for kk in range(2):
                            P.op("act", lambda e, sub=sub, pb=pb, n=n, t0=t0, kk=kk: e.activation(
                                out=vaug[0:n, t0:t0 + len(sub), 2 * vc + kk, 0:64],
                                in_=pb[0:n, 0:len(sub) * 128].rearrange("p (t k d) -> p t k d", k=2, d=64)[:, :, kk, :], func=AF.Copy),
                                reads=[ps_res[bk]], writes=[vaug_res])
            elif col >= D:
                kc_ = (col - D) // 128
                s = normed(j, col, banks, kg)

                def k_stage2():
                    normed_b(s)
                    nrm_out(s, kg, knT[0][:, kc_, :], 0, T, [], knT_res[0])
                    nrm_out(s, kg, knf[:, kc_, 0:128], LAST0, LAST0 + 128, [], knf_res)
                    nrm_out(s, kg, knf[:, kc_, 128:128 + NS], TM, T, [], knf_res)
                    P.defer_dma(knT[1][0:64, kc_, :], knT[0][64:128, kc_, :], reads=[knT_res[0]], writes=[knT_res[1]])
                    P.defer_dma(knT[1][64:128, kc_, :], knT[0][0:64, kc_, :], reads=[knT_res[0]], writes=[knT_res[1]])
                qlate.append(k_stage2)
            else:
                jq = col // 128
                s = normed(j, col, banks, qg)

                def q_stage2():
                    normed_b(s)
                    nrm_out(s, qg, qn[s][:, :], 0, T, [], qn_res[s])
                    nrm_out(s, qg, qsT[:, jq, :], TM, T, [], qsT_res)
                    P.defer_dma(H[jq, :, :], qn[s][:, :], reads=[qn_res[s]], writes=[H_res[jq]])
                qlate.append(q_stage2)

        gemm(A, A_res, cgsT, epi_qkv, expect_W=att_w_qkv)
        run_qlate()
        P.flush()
        P.barrier()
        st2.close()

        st3 = ExitStack()
        qc = [sb(f"qc{k}", [128, T], BF16, stack=st3) for k in range(3)]
        qc_res = RL(3)
        ebias = sb("ebias", [128, 2, 256], stack=st3)
        ebias_res = RL(2)
        GB = 4
        stmp = [sb(f"stmp{k}", [128, GB, 256], stack=st3) for k in range(2)]
        stmp_res = RL(2)
        Et = [sb(f"Et{k}", [128, GB, 256], BF16, stack=st3) for k in range(2)]
        Et_res = RL(2)
        den = [sb(f"den{k}", [128, GB], stack=st3) for k in range(2)]
        den_res = RL(2)
        attc = [sb(f"attc{k}", [128, NTM, 128], BF16, stack=st3) for k in range(2)]
        attc_res = RL(2)
        oTa = [sb(f"oTa{k}", [128, T], BF16, stack=st3) for k in range(2)]
        oTa_res = RL(2)
        for k in range(2):
            P.op("pool", lambda e, k=k: e.memset(oTa[k][:, :], 0.0), writes=[oTa_res[k]])
        items = []
        for jq in range(KC):
            for hh in range(2):
                for t0 in range(1, NTM, GB):
                    items.append((jq, hh, t0, min(GB, NTM - t0)))
        nload = [0]

        def load_q(jq):
            if jq < KC and jq >= nload[0]:
                P.dma(qc[jq % 3][:, :], H[jq, :, :], reads=[H_res[jq]], writes=[qc_res[jq % 3]])
                nload[0] = jq + 1
        load_q(0)
        load_q(1)

        def stage_sc(idx):
            jq, hh, t0, nb_ = items[idx]
            h = 2 * jq + hh
            kvh = h // G
            kc_, kb_ = kvh // 2, kvh % 2
            src = knT[0] if kb_ == hh else knT[1]
            sres = knT_res[0] if kb_ == hh else knT_res[1]
            pb0 = hh * 64
            if t0 == 1:
                if hh == 0:
                    load_q(jq + 2)
                eb = h % 2
                P.op("dve", lambda e: e.scalar_tensor_tensor(out=ebias[:, eb, :], in0=negd[:, :], scalar=slopes[h], in1=msk[:, :], op0=ALU.mult, op1=ALU.add),
                     reads=[c1], writes=[ebias_res[eb]])
            r = idx % 2
            banks = [2 * r, 2 * r + 1]

            def fn(e):
                last = None
                for b_ in range(nb_):
                    t = t0 + b_
                    for i in range(2):
                        k0 = (t - 1 + i) * 128
                        last = e.matmul(ps[banks[b_ // 2]][:, (b_ % 2) * 256 + i * 128:(b_ % 2) * 256 + (i + 1) * 128],
                                        lhsT=src[pb0:pb0 + 64, kc_, k0:k0 + 128], rhs=qc[jq % 3][pb0:pb0 + 64, t * 128:(t + 1) * 128],
                                        start=True, stop=True)
                return last
            P.op("pe", fn, reads=[sres, qc_res[jq % 3]], writes=[ps_res[banks[0]], ps_res[banks[1]]])

        def stage_exp(idx):
            jq, hh, t0, nb_ = items[idx]
            h = 2 * jq + hh
            eb = h % 2
            r = idx % 2
            banks = [2 * r, 2 * r + 1]
            for bi in range((nb_ + 1) // 2):
                n2 = min(2, nb_ - 2 * bi)
                P.op("dve", lambda e, bi=bi, n2=n2: e.tensor_tensor(
                    out=stmp[r][:, 2 * bi:2 * bi + n2, :], in0=ps[banks[bi]][:, 0:n2 * 256].rearrange("p (b c) -> p b c", c=256),
                    in1=ebias[:, eb, :].unsqueeze(1).broadcast_to([128, n2, 256]), op=ALU.add),
                    reads=[ps_res[banks[bi]], ebias_res[eb]], writes=[stmp_res[r]])
            if t0 == 1:
                P.op("dve", lambda e: e.tensor_scalar(out=stmp[r][:, 0, 0:128], in0=stmp[r][:, 0, 0:128], scalar1=first[:, 0:1], scalar2=None, op0=ALU.add),
                     reads=[c1], writes=[stmp_res[r]])
            P.op("act", lambda e: e.activation(out=Et[r][:, 0:nb_, :], in_=stmp[r][:, 0:nb_, :], func=AF.Exp), reads=[stmp_res[r]], writes=[Et_res[r]])

        def stage_pv(idx):
            jq, hh, t0, nb_ = items[idx]
            h = 2 * jq + hh
            kvh = h // G
            r = idx % 2
            ac = jq % 2
            zb = 4 + r

            def fn(e):
                last = None
                for b_ in range(nb_):
                    t = t0 + b_
                    for i in range(2):
                        last = e.matmul(ps[zb][:, b_ * 65:b_ * 65 + 65], lhsT=Et[r][:, b_, i * 128:(i + 1) * 128], rhs=vaug[:, t - 1 + i, kvh, :],
                                        start=(i == 0), stop=(i == 1))
                return last
            P.op("pe", fn, reads=[Et_res[r], vaug_res], writes=[ps_res[zb]])

        def stage_norm(idx):
            jq, hh, t0, nb_ = items[idx]
            h = 2 * jq + hh
            r = idx % 2
            ac = jq % 2
            zb = 4 + r
            zv = ps[zb][:, 0:nb_ * 65].rearrange("p (b c) -> p b c", c=65)
            P.op("dve", lambda e: e.tensor_scalar(out=den[r][:, 0:nb_], in0=zv[:, :, 64], scalar1=esink[:, h:h + 1], scalar2=None, op0=ALU.add),
                 reads=[ps_res[zb], c1], writes=[den_res[r]])
            P.op("dve", lambda e: e.reciprocal(out=den[r][:, 0:nb_], in_=den[r][:, 0:nb_]), writes=[den_res[r]])
            P.op("dve", lambda e: e.tensor_tensor(out=attc[ac][:, t0:t0 + nb_, hh * 64:(hh + 1) * 64], in0=zv[:, :, 0:64],
                                                  in1=den[r][:, 0:nb_].unsqueeze(2).broadcast_to([128, nb_, 64]), op=ALU.mult),
                 reads=[ps_res[zb], den_res[r]], writes=[attc_res[ac]])
            last_of_chunk = (hh == 1 and t0 + nb_ >= NTM)
            if last_of_chunk:
                for g0 in range(1, NTM, 8):
                    grp = list(range(g0, min(NTM, g0 + 8)))
                    bk = aux_bank()
                    pb = ps[bk].bitcast(BF16)
                    P.op("pe", lambda e: [e.transpose(out=pb[:, i * 128:(i + 1) * 128], in_=attc[ac][:, t, :], identity=ident_b[:, :])
                                          for i, t in enumerate(grp)][-1],
                         reads=[attc_res[ac], cres], writes=[ps_res[bk]])
                    P.op("act", lambda e: e.activation(out=oTa[ac][:, grp[0] * 128:(grp[-1] + 1) * 128], in_=pb[:, 0:len(grp) * 128], func=AF.Copy),
                         reads=[ps_res[bk]], writes=[oTa_res[ac]])
                P.dma(O[jq, :, :], oTa[ac][:, :], reads=[oTa_res[ac]], writes=[O_res[jq]])

        n_it = len(items)
        stage_sc(0)
        stage_exp(0)
        if n_it > 1:
            stage_sc(1)
        for idx in range(n_it):
            stage_pv(idx)
            if idx + 1 < n_it:
                stage_exp(idx + 1)
            if idx + 2 < n_it:
                stage_sc(idx + 2)
            stage_norm(idx)
        P.barrier()
        st3.close()

        ktm = sb("ktm", [128, KVW], stack=st)
        vtm2 = sb("vtm2", [128, KVW], stack=st)
        ksm = sb("ksm", [NS, KVW], stack=st)
        vsm = sb("vsm", [NS, KVW], stack=st)
        tm_res = Res()
        for (srcf, sres, dst, dsts) in ((knf, knf_res, ktm, ksm), (vf, vf_res, vtm2, vsm)):
            for kc_ in range(KVC):
                bk = aux_bank()
                P.op("pe", lambda e: e.transpose(out=ps[bk][:, 0:128], in_=srcf[:, kc_, 0:128], identity=ident_f[:, :]), reads=[sres, cres], writes=[ps_res[bk]])
                P.op("act", lambda e: e.activation(out=dst[:, kc_ * 128:(kc_ + 1) * 128], in_=ps[bk][:, 0:128], func=AF.Copy), reads=[ps_res[bk]], writes=[tm_res])
                bk = aux_bank()
                P.op("pe", lambda e: e.transpose(out=ps[bk][0:NS, 0:128], in_=srcf[:, kc_, 128:128 + NS], identity=ident_f[:, :]), reads=[sres, cres], writes=[ps_res[bk]])
                P.op("act", lambda e: e.activation(out=dsts[:, kc_ * 128:(kc_ + 1) * 128], in_=ps[bk][0:NS, 0:128], func=AF.Copy), reads=[ps_res[bk]], writes=[tm_res])
        P.dma(ck_p[:, :], ktm[:, :], reads=[tm_res], writes=[Res()])
        P.dma(cv_p[:, :], vtm2[:, :], reads=[tm_res], writes=[Res()])
        for s_ in range(NS):
            P.dma(ck_s[s_, 0:127, :], cache_k[s_, 1:128, :], writes=[Res()])
            P.dma(cv_s[s_, 0:127, :], cache_v[s_, 1:128, :], writes=[Res()])
        P.dma(ck_s[:, 127, :], ksm[:, :], reads=[tm_res], writes=[Res()])
        P.dma(cv_s[:, 127, :], vsm[:, :], reads=[tm_res], writes=[Res()])

        for kc in range(KC):
            P.dma(A[:, kc, :], O[kc, :, :], reads=[O_res[kc]], writes=[A_res[kc]])

        sel32 = sb("sel32", [NS, 128], stack=st)
        selC = sb("selC", [128, NS, NS], stack=st)
        sel_res = Res()
        P.op("dve", lambda e: e.memset(sel32[:, :], 0.0), writes=[sel_res])
        P.op("dve", lambda e: e.tensor_copy(out=sel32[:, :].rearrange("p (a b) -> p a b", b=32)[:, :, 0], in_=ident_f[0:NS, 0:NS]), reads=[cres], writes=[sel_res])
        P.op("dve", lambda e: e.tensor_copy(out=selC[:, :, :], in_=selrow[:, :, :]), reads=[cres], writes=[sel_res])
        selN = sb("selN", [128, NS], stack=st)
        bk = aux_bank()
        P.op("pe", lambda e: e.transpose(out=ps[bk][:, 0:NS], in_=sel32[0:NS, :], identity=ident_f[0:NS, 0:NS]), reads=[sel_res, cres], writes=[ps_res[bk]])
        P.op("act", lambda e: e.activation(out=selN[:, :], in_=ps[bk][:, 0:NS], func=AF.Copy), reads=[ps_res[bk]], writes=[sel_res])
        k32 = sb("k32", [128, KVW], stack=st)
        v32 = sb("v32", [128, KVW], stack=st)
        Kc = sb("Kc", [128, NS, KVW], stack=st)
        Vc = sb("Vc", [128, NS, KVW], stack=st)
        smp_res = Res()
        for (srct, dstb) in ((ksm, k32), (vsm, v32)):
            for c0 in range(0, KVW, 512):
                bk = aux_bank()
                cw = min(512, KVW - c0)
                P.op("pe", lambda e: e.matmul(ps[bk][:, 0:cw], lhsT=sel32[:, :], rhs=srct[:, c0:c0 + cw], start=True, stop=True),
                     reads=[sel_res, tm_res], writes=[ps_res[bk]])
                P.op("act", lambda e: e.activation(out=dstb[:, c0:c0 + cw], in_=ps[bk][:, 0:cw], func=AF.Copy), reads=[ps_res[bk]], writes=[smp_res])
        for s_ in range(NS):
            P.dma(Kc[:, s_, :], cache_k[s_, :, :], writes=[smp_res])
            P.dma(Vc[:, s_, :], cache_v[s_, :, :], writes=[smp_res])
        sbias = sb("sbias", [128, NQ], stack=st)
        slp = sb("slp", [128, NQ], stack=st)
        for hh in range(NQ):
            P.op("pool", lambda e, hh=hh: e.memset(slp[:, hh:hh + 1], -slopes[hh]), writes=[smp_res])
        P.op("dve", lambda e: e.tensor_scalar(out=sbias[:, :], in0=slp[:, :], scalar1=sdist[:, 0:1], scalar2=None, op0=ALU.mult), reads=[c1], writes=[smp_res])
        E_all = sb("E_all", [128, NS, NQ], stack=st)
        En = sb("En", [128, NQ], stack=st)
        prod = sb("prod", [128, 2, 64], stack=st)
        prodn = sb("prodn", [128, 2, 64], stack=st)
        qbc = sb("qbc", [128, 128], stack=st)
        ev = sb("ev", [128, 512], stack=st)
        evn = sb("evn", [128, 512], stack=st)
        dn_s = sb("dn_s", [NS, NQ], stack=st)
        at_s = sb("at_s", [NS, 512], stack=st)
        at_b = sb("at_b", [NS, 512], BF16, stack=st)
        P.op("dve", lambda e: e.memset(En[:, :], 0.0), writes=[smp_res])
        P.op("dve", lambda e: e.memset(evn[:, :], 0.0), writes=[smp_res])
        for s_ in range(NS):
            p0 = 32 * s_
            for jq in range(KC):
                kvh = (2 * jq) // G
                kvh1 = (2 * jq + 1) // G
                bk = aux_bank()
                P.op("pe", lambda e: e.matmul(ps[bk][:, 0:128], lhsT=qsT[:, jq, s_:s_ + 1].broadcast_to([128, 128]), rhs=ident_f[:, :], start=True, stop=True),
                     reads=[qsT_res, cres], writes=[ps_res[bk]])
                P.op("act", lambda e: e.activation(out=qbc[:, :], in_=ps[bk][:, 0:128], func=AF.Copy), reads=[ps_res[bk]], writes=[smp_res])
                for hh, kv in ((0, kvh), (1, kvh1)):
                    P.op("dve", lambda e: e.tensor_tensor(out=prod[:, hh, :], in0=qbc[:, hh * 64:(hh + 1) * 64], in1=Kc[:, s_, kv * 64:(kv + 1) * 64], op=ALU.mult),
                         writes=[smp_res])
                    P.op("dve", lambda e: e.tensor_tensor(out=prodn[p0:p0 + 1, hh, :], in0=qbc[p0:p0 + 1, hh * 64:(hh + 1) * 64], in1=k32[p0:p0 + 1, kv * 64:(kv + 1) * 64], op=ALU.mult),
                         writes=[smp_res])
                P.op("dve", lambda e: e.tensor_reduce(out=E_all[:, s_, 2 * jq:2 * jq + 2], in_=prod[:, :, :], axis=AX.X, op=ALU.add), writes=[smp_res])
                P.op("dve", lambda e: e.tensor_reduce(out=En[p0:p0 + 1, 2 * jq:2 * jq + 2], in_=prodn[p0:p0 + 1, :, :], axis=AX.X, op=ALU.add), writes=[smp_res])
            P.op("dve", lambda e: e.tensor_tensor(out=E_all[:, s_, :], in0=E_all[:, s_, :], in1=sbias[:, :], op=ALU.add), writes=[smp_res])
            P.op("act", lambda e: e.activation(out=E_all[:, s_, :], in_=E_all[:, s_, :], func=AF.Exp), writes=[smp_res])
            P.op("act", lambda e: e.activation(out=En[p0:p0 + 1, :], in_=En[p0:p0 + 1, :], func=AF.Exp), writes=[smp_res])
        bk = aux_bank()

        def fn_den(e):
            last = None
            for s_ in range(NS):
                e.matmul(ps[bk][0:NS, 0:NQ], lhsT=selC[:, s_, :], rhs=E_all[:, s_, :], start=(s_ == 0), stop=False)
            return e.matmul(ps[bk][0:NS, 0:NQ], lhsT=selN[:, :], rhs=En[:, :], start=False, stop=True)
        P.op("pe", fn_den, reads=[sel_res, smp_res], writes=[ps_res[bk]])
        P.op("dve", lambda e: e.tensor_tensor(out=dn_s[:, :], in0=ps[bk][0:NS, 0:NQ], in1=esink[0:NS, :], op=ALU.add), reads=[ps_res[bk], c1], writes=[smp_res])
        P.op("dve", lambda e: e.reciprocal(out=dn_s[:, :], in_=dn_s[:, :]), writes=[smp_res])
        HP = 512 // 64
        for c0 in range(0, D, 512):
            h0 = c0 // 64
            bkn = aux_bank()
            for s_ in range(NS):
                p0 = 32 * s_
                for hh in range(HP):
                    h = h0 + hh
                    kv = h // G
                    P.op("dve", lambda e: e.tensor_scalar(out=ev[:, hh * 64:(hh + 1) * 64], in0=Vc[:, s_, kv * 64:(kv + 1) * 64], scalar1=E_all[:, s_, h:h + 1], scalar2=None, op0=ALU.mult),
                         writes=[smp_res])
                    P.op("dve", lambda e: e.tensor_scalar(out=evn[p0:p0 + 1, hh * 64:(hh + 1) * 64], in0=v32[p0:p0 + 1, kv * 64:(kv + 1) * 64], scalar1=En[p0:p0 + 1, h:h + 1], scalar2=None, op0=ALU.mult),
                         writes=[smp_res])
                P.op("pe", lambda e: e.matmul(ps[bkn][0:NS, 0:512], lhsT=selC[:, s_, :], rhs=ev[:, :], start=(s_ == 0), stop=False),
                     reads=[sel_res, smp_res], writes=[ps_res[bkn]])
            P.op("pe", lambda e: e.matmul(ps[bkn][0:NS, 0:512], lhsT=selN[:, :], rhs=evn[:, :], start=False, stop=True),
                 reads=[sel_res, smp_res], writes=[ps_res[bkn]])
            P.op("dve", lambda e: e.tensor_tensor(out=at_b[:, :].rearrange("p (h d) -> p h d", d=64),
                                                  in0=ps[bkn][0:NS, 0:512].rearrange("p (h d) -> p h d", d=64),
                                                  in1=dn_s[:, h0:h0 + HP].unsqueeze(2).broadcast_to([NS, HP, 64]), op=ALU.mult),
                 reads=[ps_res[bkn]], writes=[smp_res])
            for i in range(4):
                kc = c0 // 128 + i
                bk2 = aux_bank()
                pb = ps[bk2].bitcast(BF16)
                P.op("pe", lambda e: e.transpose(out=pb[:, 0:NS], in_=at_b[0:NS, i * 128:(i + 1) * 128], identity=ident_b[0:NS, 0:NS]),
                     reads=[smp_res, cres], writes=[ps_res[bk2]])
                P.op("act", lambda e: e.activation(out=A[:, kc, TM:T], in_=pb[:, 0:NS], func=AF.Copy), reads=[ps_res[bk2]], writes=[A_res[kc]])
    P.barrier()
    st.close()
    cgsT = []
    c_ = 128
    while c_ < T:
        cgsT.append((c_, min(T, c_ + 512) if c_ + 512 <= TM else (TM if c_ < TM else T)))
        c_ = cgsT[-1][1]
    with ExitStack() as st:
        epi_r = resid_epi_factory(st, "a")
        gemm(A, A_res, cgsT, epi_r, expect_W=att_w_out)
    P.barrier()
    dump_R("R3")
    mlp_and_ple(1)
    dump_R("R4")

    with ExitStack() as st:
        rb = [sb(f"frb{i}", [128, T], stack=st) for i in range(2)]
        rb_res = RL(2)
        yt = [sb(f"yt{i}", [128, 512], stack=st) for i in range(2)]
        yt_res = RL(2)
        tl = [(t * 128, 128) for t in range(1, NTM)] + [(TM, NS)]
        cnt = 0
        for g0 in range(0, KC, 4):
            gn = min(4, KC - g0)
            rbs = []
            for i in range(gn):
                s = (g0 + i) % 2
            for (c0, n) in tl:
                bk = aux_bank()
                u = cnt % 2
                cnt += 1
                for i in range(gn):
                    s = i % 2
                    P.dma(rb[s][:, 0:n], R[g0 + i, :, c0:c0 + n], reads=[R_res[g0 + i]], writes=[rb_res[s]])
                    P.op("pe", lambda e, i=i, s=s: e.transpose(out=ps[bk][0:n, i * 128:(i + 1) * 128], in_=rb[s][:, 0:n], identity=ident_f[:, :]),
                         reads=[rb_res[s], cres], writes=[ps_res[bk]])
                P.op("act", lambda e: e.activation(out=yt[u][0:n, 0:gn * 128], in_=ps[bk][0:n, 0:gn * 128], func=AF.Copy), reads=[ps_res[bk]], writes=[yt_res[u]])
                if c0 < TM:
                    P.dma(y_main[c0 - 128:c0 - 128 + n, g0 * 128:(g0 + gn) * 128], yt[u][0:n, 0:gn * 128], reads=[yt_res[u]], writes=[Res()])
                else:
                    P.dma(y_smp[:, g0 * 128:(g0 + gn) * 128], yt[u][0:n, 0:gn * 128], reads=[yt_res[u]], writes=[Res()])
    P.barrier()
    assert job_ptr[0] == len(jobs), (job_ptr[0], len(jobs))
    assert ws_state["ptr"] == len(tiles)
    es.close()
    return nc


def host_consts(cfg, hf):
    NH = cfg.NH
    lg = np.log1p(-np.exp2(-5.0 - np.arange(NH, dtype=np.float64)))
    idx = np.arange(128, dtype=np.float64)
    diff = idx[None, :] - idx[:, None]
    decT = np.where(diff[None] >= 0, np.exp(np.maximum(diff, 0)[None] * lg[:, None, None]), 0.0).astype(np.float32)
    qdec = np.exp((idx + 1.0)[None, :] * lg[:, None]).astype(np.float32)
    kdec = np.exp((127.0 - idx)[:, None] * lg[None, :]).astype(np.float32)
    k = np.arange(128)[:, None]
    q = np.arange(128)[None, :]
    dist_prev = 128 + q - k
    dist_cur = q - k
    negdist = np.concatenate([-dist_prev, -np.maximum(dist_cur, 0)], axis=1).astype(np.float32)
    mask = np.concatenate([np.where(dist_prev <= 128, 0.0, NEG), np.where(dist_cur >= 0, 0.0, NEG)], axis=1).astype(np.float32)
    first = np.full((128, 1), NEG if hf == 0 else 0.0, np.float32)
    sdist = (128.0 - np.arange(128, dtype=np.float32)).reshape(128, 1)
    return {"c_ident": np.eye(128, dtype=np.float32), "c_decT": decT, "c_qdec": qdec, "c_kdec": kdec,
            "c_negdist": negdist, "c_mask": mask, "c_first": first, "c_sdist": sdist}


def make_in_maps(cfg, inp):
    f = lambda a: np.ascontiguousarray(np.asarray(a, dtype=np.float32))
    xp = f(inp["x_prompt"])
    xs = f(inp["x_sample"])
    pp = f(inp["p_prompt"])
    psm = f(inp["p_sample"])
    st = f(inp["state_ret"])
    ck = f(inp["cache_k"])
    cv = f(inp["cache_v"])
    OWN, NS, D = cfg.OWN, cfg.NS, cfg.D
    shared = {
        "norm_mix": f(inp["norm_mix"]), "norm_mlp": f(inp["norm_mlp"]), "norm_ple": f(inp["norm_ple"]),
        "ret_w_in": f(inp["ret_w_in"])[0], "ret_gn": f(inp["ret_gn_gain"])[0], "ret_w_out": f(inp["ret_w_out"])[0],
        "att_w_qkv": f(inp["attn_w_qkv"])[0], "att_b": f(inp["attn_b_qkv"])[0], "att_qn": f(inp["attn_q_norm"])[0],
        "att_kn": f(inp["attn_k_norm"])[0], "att_sinks": f(inp["attn_sinks"])[0], "att_w_out": f(inp["attn_w_out"])[0],
        "mlp_up": f(inp["mlp_w_up"]), "mlp_down": f(inp["mlp_w_down"]), "ple_gate": f(inp["ple_w_gate"]), "ple_proj": f(inp["ple_w_proj"]),
    }
    maps = []
    for c in range(8):
        b, hf = c // 2, c % 2
        m = dict(shared)
        if hf == 1:
            m["x_main"] = np.ascontiguousarray(xp[b, OWN - 128:2 * OWN])
            m["x_ctx"] = np.ascontiguousarray(xp[b, 0:max(OWN - 128, 1)])
            m["p_main"] = np.ascontiguousarray(pp[:, b, OWN - 128:2 * OWN])
        else:
            m["x_main"] = np.concatenate([np.zeros((128, D), np.float32), xp[b, 0:OWN]], axis=0)
            m["x_ctx"] = np.zeros((max(OWN - 128, 1), D), np.float32)
            m["p_main"] = np.concatenate([np.zeros((2, 128, 256), np.float32), pp[:, b, 0:OWN]], axis=1)
        sl = slice(c * NS, (c + 1) * NS)
        m["x_smp"] = np.ascontiguousarray(xs[sl, 0])
        m["p_smp"] = np.ascontiguousarray(psm[:, sl, 0])
        m["state_in"] = np.ascontiguousarray(st[0, sl])
        m["cache_k"] = np.ascontiguousarray(ck[0, sl].reshape(NS, 128, cfg.KVW))
        m["cache_v"] = np.ascontiguousarray(cv[0, sl].reshape(NS, 128, cfg.KVW))
        m.update(host_consts(cfg, hf))
        maps.append(m)
    return maps


def assemble(cfg, res):
    B, OWN, NS, D, NH, NKV = cfg.B, cfg.OWN, cfg.NS, cfg.D, cfg.NH, cfg.NKV
    y_p = np.zeros((B, 2 * OWN, D), np.float32)
    y_s = np.zeros((8 * NS, 1, D), np.float32)
    sp = np.zeros((1, B, NH, 256, 512), np.float32)
    ss = np.zeros((1, 8 * NS, NH, 256, 512), np.float32)
    ckp = np.zeros((1, B, 128, NKV, 64), np.float32)
    cvp = np.zeros((1, B, 128, NKV, 64), np.float32)
    cks = np.zeros((1, 8 * NS, 128, NKV, 64), np.float32)
    cvs = np.zeros((1, 8 * NS, 128, NKV, 64), np.float32)
    for c in range(8):
        b, hf = c // 2, c % 2
        r = res[c]
        y_p[b, hf * OWN:(hf + 1) * OWN] = r["y_main"]
        sl = slice(c * NS, (c + 1) * NS)
        y_s[sl, 0] = r["y_smp"]
        ss[0, sl] = r["st_s"]
        cks[0, sl] = r["ck_s"].reshape(NS, 128, NKV, 64)
        cvs[0, sl] = r["cv_s"].reshape(NS, 128, NKV, 64)
        if hf == 1:
            sp[0, b] = r["st_p"]
            ckp[0, b] = r["ck_p"].reshape(128, NKV, 64)
            cvp[0, b] = r["cv_p"].reshape(128, NKV, 64)
    return (y_p, y_s, sp, ss, ckp, cvp, cks, cvs)


_NC_CACHE = {}


def kernel(**inputs):
    cfg = Cfg()
    if "nc" not in _NC_CACHE:
        _NC_CACHE["nc"] = build(cfg)
    nc = _NC_CACHE["nc"]
    maps = make_in_maps(cfg, inputs)
    res = run_bass_kernel_spmd(nc, maps, core_ids=list(range(8)))
    return assemble(cfg, res.results)
```

```python
import math
from contextlib import ExitStack

import numpy as np
import concourse.bass as bass
import concourse.mybir as mybir
from concourse.bass_utils import run_bass_kernel_spmd

F32 = mybir.dt.float32
BF16 = mybir.dt.bfloat16
AF = mybir.ActivationFunctionType
ALU = mybir.AluOpType
AX = mybir.AxisListType

RMS_EPS = 1e-6
GN_EPS = 1e-6
NEG = -30000.0


class Cfg:
    def __init__(self, D=4096, NH=16, NKV=8, DFF=16384, SEQ=2048, B=4, DECB=32, PAST=16384):
        self.D = D
        self.KC = D // 128
        self.NH = NH
        assert NH * 256 == D
        self.NQ = D // 64
        self.NKV = NKV
        self.G = self.NQ // NKV
        self.KVW = NKV * 64
        assert self.KVW % 128 == 0
        self.KVC = self.KVW // 128
        self.DFF = DFF
        self.FC = DFF // 128
        self.PLE = 256
        self.SEQ = SEQ
        self.B = B
        self.DECB = DECB
        self.OWN = SEQ // 2
        self.NTO = self.OWN // 128
        self.TM = 128 + self.OWN
        self.NTM = 1 + self.NTO
        self.NS = DECB // 8
        self.T = self.TM + self.NS
        self.CTX = self.OWN - 128
        self.NTC = self.CTX // 128
        self.KH = min(8, self.KC)
        self.WIN = D * 6

    def cgs(self, T):
        out = []
        c = 0
        while c < T:
            out.append((c, min(T, c + 512)))
            c += 512
        return out


class Res:
    __slots__ = ("w", "rs")

    def __init__(self):
        self.w = None
        self.rs = {}


def RL(n):
    return [Res() for _ in range(n)]


class Prog:
    NDS = 56

    def __init__(self, nc, es):
        self.nc = nc
        self.es = es
        self.E = {"pe": nc.tensor, "act": nc.scalar, "dve": nc.vector, "pool": nc.gpsimd, "sp": nc.sync}
        self.cur = {}
        self.known = {e: {} for e in self.E}
        self.dsems = []
        self.dptr = 0
        self.nsem = 0
        self.allsems = []
        self.last = {}
        self.pending = []

    def _newsem(self):
        self.nsem += 1
        s = self.es.enter_context(self.nc.semaphore(f"s{self.nsem}"))
        self.allsems.append(s)
        return s

    def _ticket(self, eng, instr):
        c = self.cur.get(eng)
        if c is None or c[1] >= 30000:
            c = [self._newsem(), 0]
            self.cur[eng] = c
        c[1] += 1
        instr.then_inc(c[0], 1)
        t = (c[0], c[1], eng)
        self.last[id(c[0])] = t
        return t

    def _wait(self, eng, deps):
        k = self.known[eng]
        best = {}
        for d in deps:
            sem, val, src = d
            if src == "pe" and eng == "pe":
                continue
            sid = id(sem)
            if k.get(sid, 0) >= val:
                continue
            if sid not in best or best[sid][1] < val:
                best[sid] = (sem, val)
        for sid, (sem, val) in best.items():
            self.E[eng].wait_ge(sem, val)
            k[sid] = val

    @staticmethod
    def _deps(reads, writes):
        d = []
        for r in reads:
            if r.w is not None:
                d.append(r.w)
        for w in writes:
            if w.w is not None:
                d.append(w.w)
            d.extend(w.rs.values())
        return d

    @staticmethod
    def _reg(t, reads, writes):
        key = id(t[0])
        for r in reads:
            old = r.rs.get(key)
            if old is None or old[1] < t[1]:
                r.rs[key] = t
        for w in writes:
            w.w = t
            w.rs = {}

    def op(self, eng, fn, reads=(), writes=()):
        self._wait(eng, self._deps(reads, writes))
        instr = fn(self.E[eng])
        t = self._ticket(eng, instr)
        self._reg(t, reads, writes)
        return t

    def dma(self, out, in_, reads=(), writes=(), q="sp", **kw):
        deps = self._deps(reads, writes)
        if len(self.dsems) < self.NDS:
            self.dsems.append([self._newsem(), 0])
            slot = self.dsems[-1]
        else:
            slot = self.dsems[self.dptr % self.NDS]
        self.dptr += 1
        if slot[1] > 0:
            deps.append((slot[0], slot[1], "dma"))
        self._wait(q, deps)
        slot[1] += 16
        self.E[q].dma_start(out=out, in_=in_, **kw).then_inc(slot[0], 16)
        t = (slot[0], slot[1], "dma")
        self.last[id(slot[0])] = t
        self._reg(t, reads, writes)
        return t

    def defer_dma(self, out, in_, reads=(), writes=(), **kw):
        self.pending.append((out, in_, list(reads), list(writes), kw))

    def flush(self):
        pend, self.pending = self.pending, []
        for (out, in_, reads, writes, kw) in pend:
            self.dma(out, in_, reads=reads, writes=writes, **kw)

    def barrier(self, engines=("pe", "act", "dve", "pool", "sp")):
        self.flush()
        deps = list(self.last.values())
        for e in engines:
            self._wait(e, deps)


class Rot:
    def __init__(self, items):
        self.items = items
        self.i = 0

    def next(self):
        it = self.items[self.i % len(self.items)]
        self.i += 1
        return it


def build(cfg, dbg=None):
    nc = bass.Bass("TRN2", target_bir_lowering=False)
    es = ExitStack()
    P = Prog(nc, es)
    D, KC, NH, T, TM, NTM, NS, KH = cfg.D, cfg.KC, cfg.NH, cfg.T, cfg.TM, cfg.NTM, cfg.NS, cfg.KH
    OWN, NTO, CTX, NTC, NQ, NKV, G, KVW, KVC = cfg.OWN, cfg.NTO, cfg.CTX, cfg.NTC, cfg.NQ, cfg.NKV, cfg.G, cfg.KVW, cfg.KVC
    DFF, FC = cfg.DFF, cfg.FC
    cgsT = cfg.cgs(T)
    NCG = len(cgsT)

    def din(name, shape, dt=F32):
        return nc.dram_tensor(name, list(shape), dt, kind="ExternalInput").ap()

    def dout(name, shape, dt=F32):
        return nc.dram_tensor(name, list(shape), dt, kind="ExternalOutput").ap()

    def dscr(name, shape, dt=F32):
        return nc.dram_tensor(name, list(shape), dt, kind="Internal").ap()

    x_main = din("x_main", [TM, D])
    x_ctx = din("x_ctx", [max(CTX, 1), D])
    x_smp = din("x_smp", [NS, D])
    p_main = din("p_main", [2, TM, 256])
    p_smp = din("p_smp", [2, NS, 256])
    state_in = din("state_in", [NS, NH, 256, 512])
    cache_k = din("cache_k", [NS, 128, KVW])
    cache_v = din("cache_v", [NS, 128, KVW])
    norm_mix = din("norm_mix", [2, D])
    norm_mlp = din("norm_mlp", [2, D])
    norm_ple = din("norm_ple", [2, D])
    ret_w_in = din("ret_w_in", [D, 6 * D])
    ret_gn = din("ret_gn", [2 * D])
    ret_w_out = din("ret_w_out", [2 * D, D])
    att_w_qkv = din("att_w_qkv", [D, D + 2 * KVW])
    att_b = din("att_b", [D + 2 * KVW])
    att_qn = din("att_qn", [64])
    att_kn = din("att_kn", [64])
    att_sinks = din("att_sinks", [NQ])
    att_w_out = din("att_w_out", [D, D])
    mlp_up = din("mlp_up", [2, D, DFF])
    mlp_down = din("mlp_down", [2, DFF, D])
    ple_gate = din("ple_gate", [2, D, D])
    ple_proj = din("ple_proj", [2, 256, D])
    c_ident = din("c_ident", [128, 128])
    c_decT = din("c_decT", [NH, 128, 128])
    c_qdec = din("c_qdec", [NH, 128])
    c_kdec = din("c_kdec", [128, NH])
    c_negdist = din("c_negdist", [128, 256])
    c_mask = din("c_mask", [128, 256])
    c_first = din("c_first", [128, 1])
    c_sdist = din("c_sdist", [128, 1])

    y_main = dout("y_main", [OWN, D])
    y_smp = dout("y_smp", [NS, D])
    st_p = dout("st_p", [NH, 256, 512])
    st_s = dout("st_s", [NS, NH, 256, 512])
    ck_p = dout("ck_p", [128, KVW])
    cv_p = dout("cv_p", [128, KVW])
    ck_s = dout("ck_s", [NS, 128, KVW])
    cv_s = dout("cv_s", [NS, 128, KVW])

    R = dscr("R", [KC, 128, T])
    O = dscr("O", [2 * KC, 128, T], BF16)
    H = dscr("H", [FC, 128, T], BF16)
    PP = dscr("PP", [KC, 128, T])
    S7 = dscr("S7", [NH, 256, 512])
    R_res = RL(KC)
    O_res = RL(2 * KC)
    H_res = RL(FC)
    PP_res = RL(KC)
    S7_res = RL(NH)
    dbg_out = {}
    if dbg:
        for name in dbg:
            if name.startswith("R"):
                dbg_out[name] = dout("dbg_" + name, [KC, 128, T])

    sbn = [0]

    def sb(name, shape, dt=F32, stack=es):
        sbn[0] += 1
        return stack.enter_context(nc.sbuf_tensor(f"{name}_{sbn[0]}", list(shape), dt))

    A = sb("A", [128, KC, T], BF16)
    A_res = RL(KC)
    NWS, NWB = 2, 3
    wst = [sb(f"wst{i}", [128, KH, 256]) for i in range(NWS)]
    wst_res = RL(NWS)
    wbf = [sb(f"wbf{i}", [128, KH, 256], BF16) for i in range(NWB)]
    wbf_res = RL(NWB)
    rstd = sb("rstd", [128, T])
    rstd_res = Res()
    ident_f = sb("ident_f", [128, 128])
    ident_b = sb("ident_b", [128, 128], BF16)
    ones_f = sb("ones_f", [128, 128])
    cres = Res()
    ps = [es.enter_context(nc.psum_tensor(f"ps{i}", [128, 512], F32)) for i in range(8)]
    ps_res = RL(8)
    auxrot = [0]

    def aux_bank():
        auxrot[0] += 1
        return 6 + (auxrot[0] % 2)

    P.dma(ident_f[:, :], c_ident[:, :], writes=[cres])
    P.op("dve", lambda e: e.tensor_copy(out=ident_b[:, :], in_=ident_f[:, :]), writes=[cres])
    P.op("dve", lambda e: e.memset(ones_f[:, :], 1.0), writes=[cres])
    ones_b = sb("ones_b", [128, 128], BF16)
    P.op("dve", lambda e: e.memset(ones_b[:, :], 1.0), writes=[cres])

    class Job:
        def __init__(self, W, row0, panels, KCj):
            self.W, self.row0, self.panels, self.KCj = W, row0, panels, KCj

    def full_panels(N):
        return [(c, 2) for c in range(0, N, 256)]

    jobs = []

    def ret_job(h, full):
        pn = []
        if full:
            pn.append((h * 256, 2))
        pn.append((D + h * 256, 2))
        pn += [(2 * D + h * 512, 2), (2 * D + h * 512 + 256, 2)]
        if full:
            pn += [(4 * D + h * 512, 2), (4 * D + h * 512 + 256, 2)]
        return Job(ret_w_in, 0, pn, KC)

    def layer_tail_jobs(i):
        js = []
        js.append(Job(mlp_up[i], 0, full_panels(DFF), KC))
        for p_ in range(DFF // D):
            js.append(Job(mlp_down[i], p_ * D, full_panels(D), KC))
        js.append(Job(ple_proj[i], 0, full_panels(D), 2))
        js.append(Job(ple_gate[i], 0, full_panels(D), KC))
        return js

    if NTC > 0:
        for h in range(NH):
            jobs.append(ret_job(h, False))
    for h in range(NH):
        jobs.append(ret_job(h, True))
    for p_ in range(2):
        jobs.append(Job(ret_w_out, p_ * D, full_panels(D), KC))
    jobs += layer_tail_jobs(0)
    qkv_panels = []
    for c in range(D, D + 2 * KVW, 256):
        qkv_panels.append((c, min(2, (D + 2 * KVW - c) // 128)))
    qkv_panels += full_panels(D)
    jobs.append(Job(att_w_qkv, 0, qkv_panels, KC))
    jobs.append(Job(att_w_out, 0, full_panels(D), KC))
    jobs += layer_tail_jobs(1)

    tiles = []
    for jb in jobs:
        khj = min(KH, jb.KCj)
        for (c0, nch) in jb.panels:
            for hk in range(jb.KCj // khj):
                r0 = jb.row0 + hk * khj * 128
                tiles.append((jb.W[r0:r0 + khj * 128, c0:c0 + nch * 128], khj, nch * 128))

    class WS:
        def __init__(self):
            self.emitted = 0
            self.ptr = 0

        def _load(self, i):
            ap, kh, ncol = tiles[i]
            s = i % NWS
            P.dma(wst[s][:, 0:kh, 0:ncol], ap.rearrange("(k p) n -> p k n", p=128), writes=[wst_res[s]])

        def _conv(self, i):
            ap, kh, ncol = tiles[i]
            s, b = i % NWS, i % NWB
            eng = "pool"
            P.op(eng, lambda e: e.tensor_copy(out=wbf[b][:, 0:kh, 0:ncol], in_=wst[s][:, 0:kh, 0:ncol]),
                 reads=[wst_res[s]], writes=[wbf_res[b]])

        def get(self):
            i = self.ptr
            while self.emitted < min(len(tiles), i + 2):
                j = self.emitted
                self._load(j)
                self.emitted += 1
            self.ptr += 1
            return i

    ws_state = {"ld": 0, "cv": 0, "ptr": 0}

    def ws_emit_load():
        j = ws_state["ld"]
        ap, kh, ncol = tiles[j]
        s = j % NWS
        P.dma(wst[s][:, 0:kh, 0:ncol], ap.rearrange("(k p) n -> p k n", p=128), writes=[wst_res[s]])
        ws_state["ld"] += 1

    def ws_emit_conv():
        jj = ws_state["cv"]
        _, kh2, nc2 = tiles[jj]
        s2, b2 = jj % NWS, jj % NWB
        if jj % 2 == 0:
            P.op("act", lambda e: e.activation(out=wbf[b2][:, 0:kh2, 0:nc2], in_=wst[s2][:, 0:kh2, 0:nc2], func=AF.Copy),
                 reads=[wst_res[s2]], writes=[wbf_res[b2]])
        else:
            P.op("pool", lambda e: e.tensor_copy(out=wbf[b2][:, 0:kh2, 0:nc2], in_=wst[s2][:, 0:kh2, 0:nc2]),
                 reads=[wst_res[s2]], writes=[wbf_res[b2]])
        ws_state["cv"] += 1

    def ws_advance(upto):
        n = len(tiles)
        while ws_state["ld"] < min(n, upto) or ws_state["cv"] < min(n, upto):
            if ws_state["ld"] < min(n, upto) and ws_state["ld"] <= ws_state["cv"] + 1 - 1 + 1 and ws_state["ld"] - ws_state["cv"] < NWS:
                ws_emit_load()
            elif ws_state["cv"] < ws_state["ld"]:
                ws_emit_conv()
            else:
                break

    def ws_get():
        i = ws_state["ptr"]
        ws_advance(i + 2)
        P.flush()
        ws_state["ptr"] += 1
        return wbf[i % NWB], wbf_res[i % NWB]

    def ws_prefetch():
        ws_advance(ws_state["ptr"] + 3)

    job_ptr = [0]

    def gemm(At, Ares, cgs, epi, expect_W=None):
        jb = jobs[job_ptr[0]]
        job_ptr[0] += 1
        if expect_W is not None:
            assert jb.W.tensor.name == expect_W.tensor.name, (jb.W.tensor.name, expect_W.tensor.name)
        KCj = jb.KCj
        khj = min(KH, KCj)
        nh = KCj // khj
        ncg = len(cgs)
        jglob = 0
        for (c0, nch) in jb.panels:
            for hk in range(nh):
                wt, wres = ws_get()
                for ch in range(nch):
                    banks = [ch * 3 + ci for ci in range(ncg)]

                    def fn(e, wt=wt, ch=ch, hk=hk, banks=banks):
                        last = None
                        for k in range(khj):
                            kc = hk * khj + k
                            for ci, (a0, a1) in enumerate(cgs):
                                last = e.matmul(ps[banks[ci]][:, 0:a1 - a0], lhsT=wt[:, k, ch * 128:(ch + 1) * 128],
                                                rhs=At[:, kc, a0:a1], start=(kc == 0), stop=(kc == KCj - 1))
                        return last
                    P.op("pe", fn, reads=[wres] + [Ares[hk * khj + k] for k in range(khj)],
                         writes=[ps_res[b] for b in banks])
                    if hk == nh - 1:
                        epi(jglob + ch, c0 + ch * 128, banks)
            jglob += nch
        P.flush()
        ws_prefetch()

    def load_xT(xap, nrows_total, col0, stack):
        xt = [sb(f"xt{i}", [128, D], stack=stack) for i in range(2)]
        xt_res = RL(2)
        xs = [sb(f"xs{i}", [128, 4, 128], stack=stack) for i in range(2)]
        xs_res = RL(2)
        it = 0
        r = 0
        while r < nrows_total:
            n = min(128, nrows_total - r)
            s = it % 2
            P.dma(xt[s][0:n, :], xap[r:r + n, :], writes=[xt_res[s]])
            for g0 in range(0, KC, 4):
                gn = min(4, KC - g0)
                bk = aux_bank()
                u = (it * ((KC + 3) // 4) + g0 // 4) % 2

                def fn(e, s=s, n=n, g0=g0, gn=gn, bk=bk):
                    last = None
                    for i in range(gn):
                        last = e.transpose(out=ps[bk][:, i * 128:i * 128 + n], in_=xt[s][0:n, (g0 + i) * 128:(g0 + i + 1) * 128],
                                           identity=ident_f[0:n, 0:n])
                    return last
                P.op("pe", fn, reads=[xt_res[s], cres], writes=[ps_res[bk]])
                P.op("act", lambda e, u=u, n=n, gn=gn, bk=bk: e.activation(
                    out=xs[u][:, 0:gn, 0:n], in_=ps[bk][:, 0:gn * 128].rearrange("p (g c) -> p g c", g=gn)[:, :, 0:n], func=AF.Copy),
                    reads=[ps_res[bk]], writes=[xs_res[u]])
                P.dma(R[g0:g0 + gn, :, col0 + r:col0 + r + n].rearrange("k p c -> p k c"), xs[u][:, 0:gn, 0:n],
                      reads=[xs_res[u]], writes=R_res[g0:g0 + gn])
            r += n
            it += 1

    def norm_phase(gain_ap, Tn, cgs, stack, Rsrc=R, Rres=R_res):
        gcol = sb("gcol", [128, KC], stack=stack)
        gres = Res()
        P.dma(gcol[:, :], gain_ap.rearrange("(k p) -> p k", p=128), writes=[gres], allow_slow_non_contiguous=True)
        rb = [sb(f"rb{i}", [128, T], stack=stack) for i in range(2)]
        rb_res = RL(2)
        sq = [sb(f"sq{i}", [128, T], BF16, stack=stack) for i in range(2)]
        sq_res = RL(2)
        for kc in range(KC):
            s = kc % 2
            P.dma(rb[s][:, 0:Tn], Rsrc[kc, :, 0:Tn], reads=[Rres[kc]], writes=[rb_res[s]])
            P.op("act", lambda e, s=s, kc=kc: e.activation(out=A[:, kc, 0:Tn], in_=rb[s][:, 0:Tn], func=AF.Copy,
                                                           scale=gcol[:, kc:kc + 1]),
                 reads=[rb_res[s], gres], writes=[A_res[kc]])
            P.op("dve", lambda e, s=s: e.tensor_tensor(out=sq[s][:, 0:Tn], in0=rb[s][:, 0:Tn], in1=rb[s][:, 0:Tn], op=ALU.mult),
                 reads=[rb_res[s]], writes=[sq_res[s]])

            def fn(e, s=s, kc=kc):
                last = None
                for ci, (a0, a1) in enumerate(cgs):
                    last = e.matmul(ps[ci][:, 0:a1 - a0], lhsT=ones_b[:, :], rhs=sq[s][:, a0:a1], start=(kc == 0), stop=(kc == KC - 1))
                return last
            P.op("pe", fn, reads=[sq_res[s], cres], writes=[ps_res[ci] for ci in range(len(cgs))])
        for ci, (a0, a1) in enumerate(cgs):
            P.op("act", lambda e, ci=ci, a0=a0, a1=a1: e.activation(out=rstd[:, a0:a1], in_=ps[ci][:, 0:a1 - a0], func=AF.Ln,
                                                                    scale=1.0 / D, bias=eps_col[:, 0:1]),
                 reads=[ps_res[ci], cres], writes=[rstd_res])
        P.op("act", lambda e: e.activation(out=rstd[:, 0:Tn], in_=rstd[:, 0:Tn], func=AF.Exp, scale=-0.5), writes=[rstd_res])

    eps_col = sb("eps_col", [128, 1])
    P.op("dve", lambda e: e.memset(eps_col[:, :], RMS_EPS), writes=[cres])

    def resid_epi_factory(stack, tag, scale_rstd=False):
        rr = [sb(f"rr{tag}{i}", [128, T], stack=stack) for i in range(2)]
        rr_res = RL(2)
        cnt = [0]

        def epi(j, col, banks):
            s = cnt[0] % 2
            cnt[0] += 1
            P.dma(rr[s][:, :], R[j, :, :], reads=[R_res[j]], writes=[rr_res[s]])
            for ci, (a0, a1) in enumerate(cgsT):
                eng = "dve"
                P.op(eng, lambda e, s=s, ci=ci, a0=a0, a1=a1, b=banks[ci]: e.tensor_tensor(
                    out=rr[s][:, a0:a1], in0=ps[b][:, 0:a1 - a0], in1=rr[s][:, a0:a1], op=ALU.add),
                    reads=[ps_res[banks[ci]]], writes=[rr_res[s]])
            P.defer_dma(R[j, :, :], rr[s][:, :], reads=[rr_res[s]], writes=[R_res[j]])
        return epi

    def dump_R(name):
        if name in dbg_out:
            with ExitStack() as st:
                tb = sb("dbgt", [128, T], stack=st)
                tr = Res()
                for kc in range(KC):
                    P.dma(tb[:, :], R[kc, :, :], reads=[R_res[kc]], writes=[tr])
                    P.dma(dbg_out[name][kc, :, :], tb[:, :], reads=[tr], writes=[Res()])
                P.barrier()

    loggam = [math.log1p(-2.0 ** (-5 - h)) for h in range(NH)]

    def retention_pass(full, Tn, ntiles, stack):
        cgs = cfg.cgs(Tn)
        gm = lambda h: math.exp(loggam[h])
        g128 = lambda h: math.exp(128.0 * loggam[h])
        nt_all = ntiles + (1 if full else 0)
        kT = sb("kT", [128, 2, Tn], BF16, stack=stack)
        kT_res = Res()
        kd = sb("kd", [128, nt_all, 256], BF16, stack=stack)
        kd_res = Res()
        vtm = sb("vtm", [128, nt_all, 512], BF16, stack=stack)
        vtm_res = Res()
        fT = [sb(f"fT{i}", [128, Tn], BF16, stack=stack) for i in range(2)]
        fT_res = RL(2)
        Sf = sb("Sf", [128, 2, 512], stack=stack)
        Sf_res = Res()
        Sb = sb("Sb", [128, 2, 512], BF16, stack=stack)
        Sb_res = Res()
        kdec = sb("kdec", [128, NH], stack=stack)
        kdec_res = Res()
        P.dma(kdec[:, :], c_kdec[:, :], writes=[kdec_res])
        if full:
            qT = sb("qT", [128, 2, Tn], BF16, stack=stack)
            qT_res = Res()
            qdT = sb("qdT", [128, 2, TM], BF16, stack=stack)
            qdT_res = Res()
            sg = sb("sg", [128, nt_all, 512], BF16, stack=stack)
            sg_res = Res()
            oT = sb("oT", [128, 4, Tn], BF16, stack=stack)
            oT_res = Res()
            decT = [sb(f"decT{i}", [128, 128], stack=stack) for i in range(2)]
            decT_res = RL(2)
            qdec = [sb(f"qdec{i}", [128, 128], stack=stack) for i in range(2)]
            qdec_res = RL(2)
            gncol = sb("gncol", [128, 2 * KC], stack=stack)
            P.dma(gncol[:, :], ret_gn.rearrange("(k p) -> p k", p=128), writes=[kdec_res], allow_slow_non_contiguous=True)
            sT = [sb(f"sT{i}", [128, 128], BF16, stack=stack) for i in range(2)]
            sT_res = RL(2)
            o1 = [sb(f"o1{i}", [128, 512], stack=stack) for i in range(2)]
            o1_res = RL(2)
            og = [sb(f"og{i}", [128, 512], BF16, stack=stack) for i in range(2)]
            og_res = RL(2)
            st6 = [sb(f"st6{i}", [128, 6], stack=stack) for i in range(2)]
            mv = [sb(f"mv{i}", [128, 2], stack=stack) for i in range(2)]
            rs_o = [sb(f"rso{i}", [128, 1], stack=stack) for i in range(2)]
            stat_res = RL(2)
            rs_res = RL(2)
            nbias = [sb(f"nbias{i}", [128, 1], stack=stack) for i in range(2)]
            nb_res = RL(2)
            NSL = NS
            Ss = [sb(f"Ss{i}", [128, 2, 512], stack=stack) for i in range(NSL)]
            Ss_res = RL(NSL)
            Ssb = [sb(f"Ssb{i}", [128, 2, 512], BF16, stack=stack) for i in range(NSL)]
            Ssb_res = RL(NSL)
            Sso = [sb(f"Sso{i}", [128, 2, 512], stack=stack) for i in range(2)]
            Sso_res = RL(2)
            qm = [sb(f"qm{i}", [128, 2, NS], BF16, stack=stack) for i in range(2)]
            qm_res = RL(2)
            km = [sb(f"km{i}", [NS, 256], BF16, stack=stack) for i in range(2)]
            km_res = RL(2)
            geps = sb("geps", [128, 1], stack=stack)
            P.op("dve", lambda e: e.memset(geps[:, :], GN_EPS), writes=[kdec_res])
        fcnt = [0]
        ccnt = [0]

        def tile_rows(t):
            return NS if (full and t == ntiles) else 128

        def tile_cols(t):
            if full and t == ntiles:
                return (TM, T)
            return (t * 128, (t + 1) * 128)

        def transposes_to(srcT, src_res, dst, dst_res, dst_c0, evac):
            t = 0
            while t < nt_all:
                grp = []
                while t < nt_all and len(grp) < 8 and tile_rows(t) == 128:
                    grp.append(t)
                    t += 1
                if not grp:
                    grp = [t]
                    t += 1
                bk = aux_bank()
                pb = ps[bk].bitcast(BF16)

                def fn(e, grp=grp, pb=pb):
                    last = None
                    for i, tt in enumerate(grp):
                        a0, a1 = tile_cols(tt)
                        n = a1 - a0
                        last = e.transpose(out=pb[0:n, i * 128:(i + 1) * 128], in_=srcT(a0, a1), identity=ident_b[:, :])
                    return last
                P.op("pe", fn, reads=[src_res, cres], writes=[ps_res[bk]])
                n = tile_rows(grp[0])
                evac(grp, pb, bk, n)

        for h in range(NH):
            hb = h % 2
            if full:
                P.dma(decT[hb][:, :], c_decT[h, :, :], writes=[decT_res[hb]])
                P.dma(qdec[hb][:, :], c_qdec[h, :].partition_broadcast(128), writes=[qdec_res[hb]])
                P.dma(Sf[:, :, :], S7[h, :, :].rearrange("(c p) e -> p c e", p=128), reads=[S7_res[h]], writes=[Sf_res])
                P.op("act", lambda e: e.activation(out=Sb[:, :, :], in_=Sf[:, :, :], func=AF.Copy), reads=[Sf_res], writes=[Sb_res])
            else:
                P.op("dve", lambda e: e.memset(Sf[:, :, :], 0.0), writes=[Sf_res])

            late = []

            def run_late():
                pend = list(late)
                del late[:]
                for f in pend:
                    f()

            def epi(j, col, banks, h=h, hb=hb):
                run_late()
                kind = col // D
                if kind == 0:
                    dc = (col - h * 256) // 128
                    for ci, (a0, a1) in enumerate(cgs):
                        P.op("dve", lambda e, ci=ci, a0=a0, a1=a1, b=banks[ci]: e.tensor_tensor(
                            out=qT[:, dc, a0:a1], in0=ps[b][:, 0:a1 - a0], in1=rstd[:, a0:a1], op=ALU.mult),
                            reads=[ps_res[banks[ci]], rstd_res], writes=[qT_res])
                    P.op("pool", lambda e: e.tensor_tensor(
                        out=qdT[:, dc, :].rearrange("p (t c) -> p t c", c=128),
                        in0=qT[:, dc, 0:TM].rearrange("p (t c) -> p t c", c=128),
                        in1=qdec[hb][:, :].unsqueeze(1).broadcast_to([128, NTM, 128]), op=ALU.mult),
                        reads=[qT_res, qdec_res[hb]], writes=[qdT_res])
                elif kind == 1:
                    dc = (col - D - h * 256) // 128
                    for ci, (a0, a1) in enumerate(cgs):
                        P.op("dve", lambda e, ci=ci, a0=a0, a1=a1, b=banks[ci]: e.scalar_tensor_tensor(
                            out=kT[:, dc, a0:a1], in0=ps[b][:, 0:a1 - a0], scalar=1.0 / 16.0, in1=rstd[:, a0:a1],
                            op0=ALU.mult, op1=ALU.mult),
                            reads=[ps_res[banks[ci]], rstd_res], writes=[kT_res])

                    def evac(grp, pb, bk, n):
                        t0 = grp[0]
                        if n == 128:
                            P.op("act", lambda e: e.activation(
                                out=kd[:, t0:t0 + len(grp), dc * 128:(dc + 1) * 128],
                                in_=pb[:, 0:len(grp) * 128].rearrange("p (t c) -> p t c", c=128),
                                func=AF.Copy, scale=kdec[:, h:h + 1]),
                                reads=[ps_res[bk], kdec_res], writes=[kd_res])
                        else:
                            P.op("act", lambda e: e.activation(out=kd[0:n, t0, dc * 128:(dc + 1) * 128], in_=pb[0:n, 0:128], func=AF.Copy),
                                 reads=[ps_res[bk]], writes=[kd_res])
                    late.append(lambda: transposes_to(lambda a0, a1: kT[:, dc, a0:a1], kT_res, kd, kd_res, dc * 128, evac))
                else:
                    isv = kind in (2, 3)
                    base = (2 * D if isv else 4 * D) + h * 512
                    ec = (col - base) // 128
                    s = fcnt[0] % 2
                    fcnt[0] += 1
                    for ci, (a0, a1) in enumerate(cgs):
                        P.op("dve", lambda e, ci=ci, a0=a0, a1=a1, b=banks[ci]: e.tensor_tensor(
                            out=fT[s][:, a0:a1], in0=ps[b][:, 0:a1 - a0], in1=rstd[:, a0:a1], op=ALU.mult),
                            reads=[ps_res[banks[ci]], rstd_res], writes=[fT_res[s]])
                    dst, dres = (vtm, vtm_res) if isv else (sg, sg_res)
                    func = AF.Copy if isv else AF.Silu

                    def evac(grp, pb, bk, n):
                        t0 = grp[0]
                        if n == 128:
                            P.op("act", lambda e: e.activation(
                                out=dst[:, t0:t0 + len(grp), ec * 128:(ec + 1) * 128],
                                in_=pb[:, 0:len(grp) * 128].rearrange("p (t c) -> p t c", c=128), func=func),
                                reads=[ps_res[bk]], writes=[dres])
                        else:
                            P.op("act", lambda e: e.activation(out=dst[0:n, t0, ec * 128:(ec + 1) * 128], in_=pb[0:n, 0:128], func=func),
                                 reads=[ps_res[bk]], writes=[dres])
                    late.append(lambda: transposes_to(lambda a0, a1: fT[s][:, a0:a1], fT_res[s], dst, dres, ec * 128, evac))

            gemm(A, A_res, cgs, epi, expect_W=ret_w_in)
            run_late()

            def sample_load(s_):
                u_ = s_ % NSL
                P.dma(Ss[u_][:, :, :], state_in[s_, h, :, :].rearrange("(c p) e -> p c e", p=128), writes=[Ss_res[u_]])
                P.op("act", lambda e: e.activation(out=Ssb[u_][:, :, :], in_=Ss[u_][:, :, :], func=AF.Copy), reads=[Ss_res[u_]], writes=[Ssb_res[u_]])

            def sample_prefetch():
                for s_ in range(NS):
                    sample_load(s_)

            def s2_gn(n, obank, t, u):
                P.op("dve", lambda e: e.bn_stats(out=st6[u][0:n, :], in_=ps[obank][0:n, :]), reads=[ps_res[obank]], writes=[stat_res[u]])
                P.op("dve", lambda e: e.bn_aggr(out=mv[u][0:n, :], in_=st6[u][0:n, :]), writes=[stat_res[u]])
                P.op("act", lambda e: e.activation(out=rs_o[u][0:n, :], in_=mv[u][0:n, 1:2], func=AF.Ln, bias=geps[0:n, :], scale=1.0),
                     reads=[kdec_res, stat_res[u]], writes=[rs_res[u]])
                P.op("act", lambda e: e.activation(out=rs_o[u][0:n, :], in_=rs_o[u][0:n, :], func=AF.Exp, scale=-0.5), writes=[rs_res[u]])
                P.op("dve", lambda e: e.scalar_tensor_tensor(out=nbias[u][0:n, :], in0=mv[u][0:n, 0:1], scalar=-1.0, in1=rs_o[u][0:n, :],
                                                             op0=ALU.mult, op1=ALU.mult),
                     reads=[stat_res[u], rs_res[u]], writes=[nb_res[u]])
                P.op("act", lambda e: e.activation(out=o1[u][0:n, :], in_=ps[obank][0:n, :], func=AF.Identity, scale=rs_o[u][0:n, :], bias=nbias[u][0:n, :]),
                     reads=[ps_res[obank], nb_res[u], rs_res[u]], writes=[o1_res[u]])

            def s2b(n, t, u):
                P.op("dve", lambda e: e.tensor_tensor(out=og[u][0:n, :], in0=o1[u][0:n, :], in1=sg[0:n, t, :], op=ALU.mult),
                     reads=[o1_res[u], sg_res], writes=[og_res[u]])

            def s3_tr(n, u, ocols):
                bk = aux_bank()
                pb = ps[bk].bitcast(BF16)

                def fn(e):
                    last = None
                    for ec in range(4):
                        last = e.transpose(out=pb[:, ec * 128:ec * 128 + n], in_=og[u][0:n, ec * 128:(ec + 1) * 128], identity=ident_b[0:n, 0:n])
                    return last
                P.op("pe", fn, reads=[og_res[u], cres], writes=[ps_res[bk]])
                for ec in range(4):
                    P.op("act", lambda e, ec=ec: e.activation(out=oT[:, ec, ocols[0]:ocols[1]], in_=pb[:, ec * 128:ec * 128 + n], func=AF.Copy,
                                                              scale=gncol[:, 4 * h + ec:4 * h + ec + 1]),
                         reads=[ps_res[bk], kdec_res], writes=[oT_res])

            def s1(c):
                a0, a1 = c * 128, (c + 1) * 128
                if full:
                    u = c % 2
                    ob = 1 if c % 2 == 0 else 5
                    P.op("pe", lambda e: [e.matmul(ps[0][:, 0:128], lhsT=kT[:, dc, a0:a1], rhs=qT[:, dc, a0:a1], start=(dc == 0), stop=(dc == 1))
                                          for dc in range(2)][-1],
                         reads=[kT_res, qT_res], writes=[ps_res[0]])
                    P.op("dve", lambda e: e.tensor_tensor(out=sT[u][:, :], in0=ps[0][:, 0:128], in1=decT[hb][:, :], op=ALU.mult),
                         reads=[ps_res[0], decT_res[hb]], writes=[sT_res[u]])

                P.op("pe", lambda e: [e.matmul(ps[2 + dc][:, :], lhsT=kd[:, c, dc * 128:(dc + 1) * 128], rhs=vtm[:, c, :], start=True, stop=True)
                                      for dc in range(2)][-1],
                     reads=[kd_res, vtm_res], writes=[ps_res[2], ps_res[3]])

            def s1o(c):
                a0, a1 = c * 128, (c + 1) * 128
                u = c % 2
                ob = 1 if c % 2 == 0 else 5

                def fn(e):
                    e.matmul(ps[ob][:, :], lhsT=sT[u][:, :], rhs=vtm[:, c, :], start=True, stop=False)
                    last = None
                    for dc in range(2):
                        last = e.matmul(ps[ob][:, :], lhsT=qdT[:, dc, a0:a1], rhs=Sb[:, dc, :], start=False, stop=(dc == 1))
                    return last
                P.op("pe", fn, reads=[sT_res[u], vtm_res, qdT_res, Sb_res], writes=[ps_res[ob]])

            def s1u(c):
                for dc in range(2):
                    P.op("dve", lambda e, dc=dc: e.scalar_tensor_tensor(out=Sf[:, dc, :], in0=Sf[:, dc, :], scalar=g128(h), in1=ps[2 + dc][:, :],
                                                                        op0=ALU.mult, op1=ALU.add),
                         reads=[ps_res[2 + dc]], writes=[Sf_res])
                if full and c < ntiles - 1:
                    P.op("act", lambda e: e.activation(out=Sb[:, :, :], in_=Sf[:, :, :], func=AF.Copy), reads=[Sf_res], writes=[Sb_res])

            if full:
                sample_prefetch()
            s1(0)
            if full:
                s1o(0)
            s1u(0)
            for c in range(ntiles):
                if c + 1 < ntiles:
                    s1(c + 1)
                if full:
                    if c >= 2:
                        s3_tr(128, (c - 2) % 2, ((c - 2) * 128, (c - 1) * 128))
                    if c + 1 < ntiles:
                        s1o(c + 1)
                    s2_gn(128, 1 if c % 2 == 0 else 5, c, c % 2)
                if c + 1 < ntiles:
                    s1u(c + 1)
                if full and c >= 1:
                    s2b(128, c - 1, (c - 1) % 2)
            if full:
                if ntiles >= 2:
                    s3_tr(128, (ntiles - 2) % 2, ((ntiles - 2) * 128, (ntiles - 1) * 128))
                s2b(128, ntiles - 1, (ntiles - 1) % 2)
                s3_tr(128, (ntiles - 1) % 2, ((ntiles - 1) * 128, ntiles * 128))
            if not full:
                P.dma(S7[h, :, :].rearrange("(c p) e -> p c e", p=128), Sf[:, :, :], reads=[Sf_res], writes=[S7_res[h]])
                continue
            P.dma(st_p[h, :, :].rearrange("(c p) e -> p c e", p=128), Sf[:, :, :], reads=[Sf_res], writes=[Res()])
            ts = ntiles
            P.op("pe", lambda e: [e.matmul(ps[0][0:NS, 0:NS], lhsT=kT[:, dc, TM:T], rhs=qT[:, dc, TM:T], start=(dc == 0), stop=(dc == 1))
                                  for dc in range(2)][-1],
                 reads=[kT_res, qT_res], writes=[ps_res[0]])
            P.op("dve", lambda e: e.tensor_tensor(out=sT[0][0:NS, 0:NS], in0=ps[0][0:NS, 0:NS], in1=ident_f[0:NS, 0:NS], op=ALU.mult),
                 reads=[ps_res[0], cres], writes=[sT_res[0]])
            for s in range(NS):
                u = s % 2
                us = s % NSL
                P.op("dve", lambda e: e.scalar_tensor_tensor(
                    out=qm[u][:, :, :], in0=qT[:, :, TM:T], scalar=gm(h),
                    in1=selrow[:, s, :].unsqueeze(1).broadcast_to([128, 2, NS]),
                    op0=ALU.mult, op1=ALU.mult),
                    reads=[qT_res, cres], writes=[qm_res[u]])

                def fn(e, s=s, u=u):
                    if s == 0:
                        e.matmul(ps[1][0:NS, :], lhsT=sT[0][0:NS, 0:NS], rhs=vtm[0:NS, ts, :], start=True, stop=False)
                    last = None
                    for dc in range(2):
                        last = e.matmul(ps[1][0:NS, :], lhsT=qm[u][:, dc, :], rhs=Ssb[us][:, dc, :], start=False, stop=(s == NS - 1 and dc == 1))
                    return last
                P.op("pe", fn, reads=[sT_res[0], vtm_res, qm_res[u], Ssb_res[us]], writes=[ps_res[1]])
                P.op("dve", lambda e: e.tensor_scalar(out=km[u][:, :], in0=kd[0:NS, ts, :], scalar1=ident_f[0:NS, s:s + 1], scalar2=None, op0=ALU.mult),
                     reads=[kd_res, cres], writes=[km_res[u]])
                P.op("pe", lambda e: [e.matmul(ps[2 + dc][:, :], lhsT=km[u][:, dc * 128:(dc + 1) * 128], rhs=vtm[0:NS, ts, :], start=True, stop=True)
                                      for dc in range(2)][-1],
                     reads=[km_res[u], vtm_res], writes=[ps_res[2], ps_res[3]])
                for dc in range(2):
                    P.op("dve", lambda e, dc=dc: e.scalar_tensor_tensor(out=Sso[u][:, dc, :], in0=Ss[us][:, dc, :], scalar=gm(h), in1=ps[2 + dc][:, :],
                                                                        op0=ALU.mult, op1=ALU.add),
                         reads=[ps_res[2 + dc], Ss_res[us]], writes=[Sso_res[u]])
                P.defer_dma(st_s[s, h, :, :].rearrange("(c p) e -> p c e", p=128), Sso[u][:, :, :], reads=[Sso_res[u]], writes=[Res()])
                if s >= 1:
                    P.flush()
            P.flush()
            s2_gn(NS, 1, ts, 0)
            s2b(NS, ts, 0)
            s3_tr(NS, 0, (TM, T))
            for ec in range(4):
                P.dma(O[4 * h + ec, :, :], oT[:, ec, :], reads=[oT_res], writes=[O_res[4 * h + ec]])

    selrow = sb("selrow", [128, NS, NS])
    P.op("dve", lambda e: e.memset(selrow[:, :, :], 0.0), writes=[cres])
    for s in range(NS):
        P.op("dve", lambda e, s=s: e.memset(selrow[:, s, s:s + 1], 1.0), writes=[cres])

    if NTC > 0:
        with ExitStack() as st:
            load_xT(x_ctx, CTX, 0, st)
        P.barrier()
        with ExitStack() as st:
            norm_phase(norm_mix[0], CTX, cfg.cgs(CTX), st)
        P.barrier()
        with ExitStack() as st:
            retention_pass(False, CTX, NTC, st)
        P.barrier()
    else:
        with ExitStack() as st:
            z = sb("z", [128, 2, 512], stack=st)
            zr = Res()
            P.op("dve", lambda e: e.memset(z[:, :, :], 0.0), writes=[zr])
            for h in range(NH):
                P.dma(S7[h, :, :].rearrange("(c p) e -> p c e", p=128), z[:, :, :], reads=[zr], writes=[S7_res[h]])
        P.barrier()

    with ExitStack() as st:
        load_xT(x_main, TM, 0, st)
        load_xT(x_smp, NS, TM, st)
    P.barrier()
    dump_R("R0")

    def mlp_and_ple(i):
        with ExitStack() as st:
            norm_phase(norm_mlp[i], T, cgsT, st)
        P.barrier()
        with ExitStack() as st:
            hf32 = [sb(f"hf{k}", [128, T], stack=st) for k in range(2)]
            hf_res = RL(2)
            hb16 = [sb(f"hb{k}", [128, T], BF16, stack=st) for k in range(2)]
            hb_res = RL(2)
            cnt = [0]

            def epi_up(j, col, banks):
                s = cnt[0] % 2
                cnt[0] += 1
                for ci, (a0, a1) in enumerate(cgsT):
                    P.op("dve", lambda e, ci=ci, a0=a0, a1=a1, b=banks[ci]: e.scalar_tensor_tensor(
                        out=hf32[s][:, a0:a1], in0=ps[b][:, 0:a1 - a0], scalar=0.0, in1=rstd[:, a0:a1], op0=ALU.max, op1=ALU.mult),
                        reads=[ps_res[banks[ci]], rstd_res], writes=[hf_res[s]])
                P.op("act", lambda e: e.activation(out=hb16[s][:, :], in_=hf32[s][:, :], func=AF.Square), reads=[hf_res[s]], writes=[hb_res[s]])
                P.defer_dma(H[j, :, :], hb16[s][:, :], reads=[hb_res[s]], writes=[H_res[j]])
            gemm(A, A_res, cgsT, epi_up, expect_W=mlp_up)
        P.barrier()
        with ExitStack() as st:
            epi_r = resid_epi_factory(st, "d")
            for p_ in range(DFF // D):
                for kc in range(KC):
                    P.dma(A[:, kc, :], H[p_ * KC + kc, :, :], reads=[H_res[p_ * KC + kc]], writes=[A_res[kc]])
                gemm(A, A_res, cgsT, epi_r, expect_W=mlp_down)
        P.barrier()
        with ExitStack() as st:
            pT = sb("pT", [128, 2, T], BF16, stack=st)
            pT_res = RL(2)
            pt = [sb(f"pt{k}", [128, 256], stack=st) for k in range(2)]
            pt_res = RL(2)
            tl = [(t * 128, 128) for t in range(NTM)] + [(TM, NS)]
            for ti, (c0, n) in enumerate(tl):
                s = ti % 2
                src = p_main[i, c0:c0 + n, :] if c0 < TM else p_smp[i, :, :]
                P.dma(pt[s][0:n, :], src, writes=[pt_res[s]])
                bk = aux_bank()
                P.op("pe", lambda e: [e.transpose(out=ps[bk][:, dc * 128:dc * 128 + n], in_=pt[s][0:n, dc * 128:(dc + 1) * 128], identity=ident_f[0:n, 0:n])
                                      for dc in range(2)][-1],
                     reads=[pt_res[s], cres], writes=[ps_res[bk]])
                P.op("act", lambda e: e.activation(out=pT[:, :, c0:c0 + n], in_=ps[bk][:, 0:256].rearrange("p (g c) -> p g c", g=2)[:, :, 0:n], func=AF.Copy),
                     reads=[ps_res[bk]], writes=pT_res)
            pp = [sb(f"pp{k}", [128, T], stack=st) for k in range(2)]
            pp_res = RL(2)
            cnt = [0]

            def epi_pp(j, col, banks):
                s = cnt[0] % 2
                cnt[0] += 1
                for ci, (a0, a1) in enumerate(cgsT):
                    P.op("act", lambda e, ci=ci, a0=a0, a1=a1, b=banks[ci]: e.activation(out=pp[s][:, a0:a1], in_=ps[b][:, 0:a1 - a0], func=AF.Copy),
                         reads=[ps_res[banks[ci]]], writes=[pp_res[s]])
                P.defer_dma(PP[j, :, :], pp[s][:, :], reads=[pp_res[s]], writes=[PP_res[j]])
            gemm(pT, pT_res, cgsT, epi_pp, expect_W=ple_proj)
        P.barrier()
        with ExitStack() as st:
            norm_phase(norm_ple[i], T, cgsT, st)
        P.barrier()
        with ExitStack() as st:
            rr = [sb(f"rrg{k}", [128, T], stack=st) for k in range(2)]
            rr_res = RL(2)
            pq = [sb(f"pq{k}", [128, T], stack=st) for k in range(2)]
            pq_res = RL(2)
            gt = [sb(f"gt{k}", [128, T], stack=st) for k in range(2)]
            gt_res = RL(2)
            cnt = [0]

            def epi_gate(j, col, banks):
                s = cnt[0] % 2
                cnt[0] += 1
                P.dma(rr[s][:, :], R[j, :, :], reads=[R_res[j]], writes=[rr_res[s]])
                P.dma(pq[s][:, :], PP[j, :, :], reads=[PP_res[j]], writes=[pq_res[s]])
                for ci, (a0, a1) in enumerate(cgsT):
                    P.op("dve", lambda e, ci=ci, a0=a0, a1=a1, b=banks[ci]: e.tensor_tensor(
                        out=gt[s][:, a0:a1], in0=ps[b][:, 0:a1 - a0], in1=rstd[:, a0:a1], op=ALU.mult),
                        reads=[ps_res[banks[ci]], rstd_res], writes=[gt_res[s]])
                P.op("act", lambda e: e.activation(out=gt[s][:, :], in_=gt[s][:, :], func=AF.Sigmoid), writes=[gt_res[s]])
                P.op("dve", lambda e: e.tensor_tensor(out=gt[s][:, :], in0=gt[s][:, :], in1=pq[s][:, :], op=ALU.mult),
                     reads=[pq_res[s]], writes=[gt_res[s]])
                P.op("dve", lambda e: e.tensor_tensor(out=rr[s][:, :], in0=rr[s][:, :], in1=gt[s][:, :], op=ALU.add),
                     reads=[gt_res[s]], writes=[rr_res[s]])
                P.defer_dma(R[j, :, :], rr[s][:, :], reads=[rr_res[s]], writes=[R_res[j]])
            gemm(A, A_res, cgsT, epi_gate, expect_W=ple_gate)
        P.barrier()

    with ExitStack() as st:
        norm_phase(norm_mix[0], T, cgsT, st)
    P.barrier()
    with ExitStack() as st:
        retention_pass(True, T, NTM, st)
    P.barrier()
    with ExitStack() as st:
        epi_r = resid_epi_factory(st, "o")
        for p_ in range(2):
            for kc in range(KC):
                P.dma(A[:, kc, :], O[p_ * KC + kc, :, :], reads=[O_res[p_ * KC + kc]], writes=[A_res[kc]])
            gemm(A, A_res, cgsT, epi_r, expect_W=ret_w_out)
    P.barrier()
    dump_R("R1")
    mlp_and_ple(0)
    dump_R("R2")

    with ExitStack() as st:
        norm_phase(norm_mix[1], T, cgsT, st)
    P.barrier()
    slopes = [2.0 ** (-8.0 * (hh + 1) / NQ) for hh in range(NQ)]
    st = ExitStack()
    st2 = ExitStack()
    if True:
        NT1 = NTM + 1
        bcol = sb("bcol", [128, (D + 2 * KVW) // 128], stack=st)
        qg = sb("qg", [128, 1], stack=st)
        kg = sb("kg", [128, 1], stack=st)
        e64 = sb("e64", [128, 1], stack=st)
        esink = sb("esink", [128, NQ], stack=st)
        negd = sb("negd", [128, 256], stack=st)
        msk = sb("msk", [128, 256], stack=st)
        first = sb("first", [128, 1], stack=st)
        sdist = sb("sdist", [128, 1], stack=st)
        blk = sb("blk", [128, 128], stack=st)
        c1 = Res()
        P.dma(bcol[:, :], att_b.rearrange("(k p) -> p k", p=128), writes=[c1], allow_slow_non_contiguous=True)
        for half in range(2):
            P.dma(qg[half * 64:(half + 1) * 64, :], att_qn.rearrange("(p o) -> p o", o=1), writes=[c1], allow_slow_non_contiguous=True)
            P.dma(kg[half * 64:(half + 1) * 64, :], att_kn.rearrange("(p o) -> p o", o=1), writes=[c1], allow_slow_non_contiguous=True)
        P.op("act", lambda e: e.activation(out=qg[:, :], in_=qg[:, :], func=AF.Copy, scale=0.125), writes=[c1])
        P.dma(esink[:, :], att_sinks.partition_broadcast(128), writes=[c1])
        P.op("act", lambda e: e.activation(out=esink[:, :], in_=esink[:, :], func=AF.Exp), writes=[c1])
        P.dma(negd[:, :], c_negdist[:, :], writes=[c1])
        P.dma(msk[:, :], c_mask[:, :], writes=[c1])
        P.dma(first[:, :], c_first[:, :], writes=[c1])
        P.dma(sdist[:, :], c_sdist[:, :], writes=[c1])
        P.op("dve", lambda e: e.memset(e64[:, :], RMS_EPS), writes=[c1])
        P.op("dve", lambda e: e.memset(blk[:, :], 0.0), writes=[c1])
        P.op("dve", lambda e: e.memset(blk[0:64, 0:64], 1.0), writes=[c1])
        P.op("dve", lambda e: e.memset(blk[64:128, 64:128], 1.0), writes=[c1])

        knf = sb("knf", [128, KVC, 128 + NS], stack=st)
        vf = sb("vf", [128, KVC, 128 + NS], stack=st)
        knf_res = Res()
        vf_res = Res()
        qsT = sb("qsT", [128, KC, NS], stack=st)
        qsT_res = Res()
        knT = [sb(f"knT{k}", [128, KVC, T], BF16, stack=st) for k in range(2)]
        knT_res = [Res(), Res()]
        vaug = sb("vaug", [128, NT1, NKV, 65], BF16, stack=st)
        vaug_res = Res()
        P.op("dve", lambda e: e.memset(vaug[:, :, :, 64:65], 1.0), writes=[vaug_res])
        xb = [sb(f"xb{k}", [128, T], stack=st2) for k in range(2)]
        xb_res = RL(2)
        xq = [sb(f"xq{k}", [128, T], stack=st2) for k in range(2)]
        xq_res = RL(2)
        qn = [sb(f"qn{k}", [128, T], BF16, stack=st2) for k in range(2)]
        qn_res = RL(2)
        cnt = [0]
        ecnt = [0]
        acnt = [0]
        LAST0 = NTO * 128

        def normed(j, col, banks, gcolv):
            s = cnt[0] % 2
            cnt[0] += 1
            bi = col // 128
            for ci, (a0, a1) in enumerate(cgsT):
                P.op("dve", lambda e, ci=ci, a0=a0, a1=a1, b=banks[ci]: e.tensor_tensor(
                    out=xb[s][:, a0:a1], in0=ps[b][:, 0:a1 - a0], in1=rstd[:, a0:a1], op=ALU.mult),
                    reads=[ps_res[banks[ci]], rstd_res], writes=[xb_res[s]])
            P.op("act", lambda e: e.activation(out=xb[s][:, :], in_=xb[s][:, :], func=AF.Identity, bias=bcol[:, bi:bi + 1], scale=1.0),
                 reads=[c1], writes=[xb_res[s]])
            if gcolv is None:
                return s
            P.op("act", lambda e: e.activation(out=xq[s][:, :], in_=xb[s][:, :], func=AF.Square), reads=[xb_res[s]], writes=[xq_res[s]])
            return s

        def normed_b(s):
            for ci, (a0, a1) in enumerate(cgsT):
                bk = aux_bank()
                P.op("pe", lambda e, a0=a0, a1=a1, bk=bk: e.matmul(ps[bk][:, 0:a1 - a0], lhsT=blk[:, :], rhs=xq[s][:, a0:a1], start=True, stop=True),
                     reads=[xq_res[s], c1], writes=[ps_res[bk]])
                P.op("act", lambda e, a0=a0, a1=a1, bk=bk: e.activation(out=xq[s][:, a0:a1], in_=ps[bk][:, 0:a1 - a0], func=AF.Ln, scale=1.0 / 64.0,
                                                                       bias=e64[:, 0:1]),
                     reads=[ps_res[bk], c1], writes=[xq_res[s]])
            P.op("act", lambda e: e.activation(out=xq[s][:, :], in_=xq[s][:, :], func=AF.Exp, scale=-0.5), writes=[xq_res[s]])

        def nrm_out(s, gcolv, out_ap, c0, c1_, reads_extra, wres):
            P.op("dve", lambda e: e.scalar_tensor_tensor(out=out_ap, in0=xb[s][:, c0:c1_], scalar=gcolv[:, 0:1], in1=xq[s][:, c0:c1_], op0=ALU.mult, op1=ALU.mult),
                 reads=[xb_res[s], xq_res[s], c1] + reads_extra, writes=[wres])

        qlate = []

        def run_qlate():
            pend = list(qlate)
            del qlate[:]
            for f in pend:
                f()

        def epi_qkv(j, col, banks):
            run_qlate()
            if col >= D + KVW:
                vc = (col - D - KVW) // 128
                s = normed(j, col, banks, None)
                P.op("pool", lambda e: e.tensor_copy(out=vf[:, vc, 0:128], in_=xb[s][:, LAST0:LAST0 + 128]), reads=[xb_res[s]], writes=[vf_res])
                P.op("pool", lambda e: e.tensor_copy(out=vf[:, vc, 128:128 + NS], in_=xb[s][:, TM:T]), reads=[xb_res[s]], writes=[vf_res])
                P.op("act", lambda e: e.activation(out=qn[s][:, :], in_=xb[s][:, :], func=AF.Copy), reads=[xb_res[s]], writes=[qn_res[s]])
                tl = [(t * 128, 128, t) for t in range(NTM)] + [(TM, NS, NTM)]
                for g0 in range(0, NTM, 8):
                    subs = [tl[g0:min(NTM, g0 + 8)]]
                    if g0 + 8 >= NTM:
                        subs.append([tl[NTM]])
                    for sub in subs:
                        bk = aux_bank()
                        pb = ps[bk].bitcast(BF16)
                        n = sub[0][1]
                        P.op("pe", lambda e, sub=sub, pb=pb: [e.transpose(out=pb[0:g[1], i * 128:(i + 1) * 128], in_=qn[s][:, g[0]:g[0] + g[1]], identity=ident_b[:, :])
                                                              for i, g in enumerate(sub)][-1],
                             reads=[qn_res[s], cres], writes=[ps_res[bk]])
                        t0 = sub[0][2]
                        for kk in range(2):
                            P.op("act", lambda e, sub=sub, pb=pb, n=n, t0=t0, kk=kk: e.activation(
                                out=vaug[0:n, t0:t0 + len(sub), 2 * vc + kk, 0:64],
                                in_=pb[0:n, 0:len(sub) * 128].rearrange("p (t k d) -> p t k d", k=2, d=64)[:, :, kk, :], func=AF.Copy),
                                reads=[ps_res[bk]], writes=[vaug_res])
            elif col >= D:
                kc_ = (col - D) // 128
                s = normed(j, col, banks, kg)

                def k_stage2():
                    normed_b(s)
                    nrm_out(s, kg, knT[0][:, kc_, :], 0, T, [], knT_res[0])
                    nrm_out(s, kg, knf[:, kc_, 0:128], LAST0, LAST0 + 128, [], knf_res)
                    nrm_out(s, kg, knf[:, kc_, 128:128 + NS], TM, T, [], knf_res)
                    P.defer_dma(knT[1][0:64, kc_, :], knT[0][64:128, kc_, :], reads=[knT_res[0]], writes=[knT_res[1]])
                    P.defer_dma(knT[1][64:128, kc_, :], knT[0][0:64, kc_, :], reads=[knT_res[0]], writes=[knT_res[1]])
                qlate.append(k_stage2)
            else:
                jq = col // 128
                s = normed(j, col, banks, qg)

                def q_stage2():
                    normed_b(s)
                    nrm_out(s, qg, qn[s][:, :], 0, T, [], qn_res[s])
                    nrm_out(s, qg, qsT[:, jq, :], TM, T, [], qsT_res)
                    P.defer_dma(H[jq, :, :], qn[s][:, :], reads=[qn_res[s]], writes=[H_res[jq]])
                qlate.append(q_stage2)

        gemm(A, A_res, cgsT, epi_qkv, expect_W=att_w_qkv)
        run_qlate()
        P.flush()
        P.barrier()
        st2.close()

        st3 = ExitStack()
        qc = [sb(f"qc{k}", [128, T], BF16, stack=st3) for k in range(3)]
        qc_res = RL(3)
        ebias = sb("ebias", [128, 2, 256], stack=st3)
        ebias_res = RL(2)
        GB = 4
        stmp = [sb(f"stmp{k}", [128, GB, 256], stack=st3) for k in range(2)]
        stmp_res = RL(2)
        Et = [sb(f"Et{k}", [128, GB, 256], BF16, stack=st3) for k in range(2)]
        Et_res = RL(2)
        den = [sb(f"den{k}", [128, GB], stack=st3) for k in range(2)]
        den_res = RL(2)
        attc = [sb(f"attc{k}", [128, NTM, 128], BF16, stack=st3) for k in range(2)]
        attc_res = RL(2)
        oTa = [sb(f"oTa{k}", [128, T], BF16, stack=st3) for k in range(2)]
        oTa_res = RL(2)
        for k in range(2):
            P.op("pool", lambda e, k=k: e.memset(oTa[k][:, :], 0.0), writes=[oTa_res[k]])
        items = []
        for jq in range(KC):
            for hh in range(2):
                for t0 in range(1, NTM, GB):
                    items.append((jq, hh, t0, min(GB, NTM - t0)))
        nload = [0]

        def load_q(jq):
            if jq < KC and jq >= nload[0]:
                P.dma(qc[jq % 3][:, :], H[jq, :, :], reads=[H_res[jq]], writes=[qc_res[jq % 3]])
                nload[0] = jq + 1
        load_q(0)
        load_q(1)

        def stage_sc(idx):
            jq, hh, t0, nb_ = items[idx]
            h = 2 * jq + hh
            kvh = h // G
            kc_, kb_ = kvh // 2, kvh % 2
            src = knT[0] if kb_ == hh else knT[1]
            sres = knT_res[0] if kb_ == hh else knT_res[1]
            pb0 = hh * 64
            if t0 == 1:
                if hh == 0:
                    load_q(jq + 2)
                eb = h % 2
                P.op("dve", lambda e: e.scalar_tensor_tensor(out=ebias[:, eb, :], in0=negd[:, :], scalar=slopes[h], in1=msk[:, :], op0=ALU.mult, op1=ALU.add),
                     reads=[c1], writes=[ebias_res[eb]])
            r = idx % 2
            banks = [2 * r, 2 * r + 1]

            def fn(e):
                last = None
                for b_ in range(nb_):
                    t = t0 + b_
                    for i in range(2):
                        k0 = (t - 1 + i) * 128
                        last = e.matmul(ps[banks[b_ // 2]][:, (b_ % 2) * 256 + i * 128:(b_ % 2) * 256 + (i + 1) * 128],
                                        lhsT=src[pb0:pb0 + 64, kc_, k0:k0 + 128], rhs=qc[jq % 3][pb0:pb0 + 64, t * 128:(t + 1) * 128],
                                        start=True, stop=True)
                return last
            P.op("pe", fn, reads=[sres, qc_res[jq % 3]], writes=[ps_res[banks[0]], ps_res[banks[1]]])

        def stage_exp(idx):
            jq, hh, t0, nb_ = items[idx]
            h = 2 * jq + hh
            eb = h % 2
            r = idx % 2
            banks = [2 * r, 2 * r + 1]
            for bi in range((nb_ + 1) // 2):
                n2 = min(2, nb_ - 2 * bi)
                P.op("dve", lambda e, bi=bi, n2=n2: e.tensor_tensor(
                    out=stmp[r][:, 2 * bi:2 * bi + n2, :], in0=ps[banks[bi]][:, 0:n2 * 256].rearrange("p (b c) -> p b c", c=256),
                    in1=ebias[:, eb, :].unsqueeze(1).broadcast_to([128, n2, 256]), op=ALU.add),
                    reads=[ps_res[banks[bi]], ebias_res[eb]], writes=[stmp_res[r]])
            if t0 == 1:
                P.op("dve", lambda e: e.tensor_scalar(out=stmp[r][:, 0, 0:128], in0=stmp[r][:, 0, 0:128], scalar1=first[:, 0:1], scalar2=None, op0=ALU.add),
                     reads=[c1], writes=[stmp_res[r]])
            P.op("act", lambda e: e.activation(out=Et[r][:, 0:nb_, :], in_=stmp[r][:, 0:nb_, :], func=AF.Exp), reads=[stmp_res[r]], writes=[Et_res[r]])

        def stage_pv(idx):
            jq, hh, t0, nb_ = items[idx]
            h = 2 * jq + hh
            kvh = h // G
            r = idx % 2
            ac = jq % 2
            zb = 4 + r

            def fn(e):
                last = None
                for b_ in range(nb_):
                    t = t0 + b_
                    for i in range(2):
                        last = e.matmul(ps[zb][:, b_ * 65:b_ * 65 + 65], lhsT=Et[r][:, b_, i * 128:(i + 1) * 128], rhs=vaug[:, t - 1 + i, kvh, :],
                                        start=(i == 0), stop=(i == 1))
                return last
            P.op("pe", fn, reads=[Et_res[r], vaug_res], writes=[ps_res[zb]])

        def stage_norm(idx):
            jq, hh, t0, nb_ = items[idx]
            h = 2 * jq + hh
            r = idx % 2
            ac = jq % 2
            zb = 4 + r
            zv = ps[zb][:, 0:nb_ * 65].rearrange("p (b c) -> p b c", c=65)
            P.op("dve", lambda e: e.tensor_scalar(out=den[r][:, 0:nb_], in0=zv[:, :, 64], scalar1=esink[:, h:h + 1], scalar2=None, op0=ALU.add),
                 reads=[ps_res[zb], c1], writes=[den_res[r]])
            P.op("dve", lambda e: e.reciprocal(out=den[r][:, 0:nb_], in_=den[r][:, 0:nb_]), writes=[den_res[r]])
            P.op("dve", lambda e: e.tensor_tensor(out=attc[ac][:, t0:t0 + nb_, hh * 64:(hh + 1) * 64], in0=zv[:, :, 0:64],
                                                  in1=den[r][:, 0:nb_].unsqueeze(2).broadcast_to([128, nb_, 64]), op=ALU.mult),
                 reads=[ps_res[zb], den_res[r]], writes=[attc_res[ac]])
            last_of_chunk = (hh == 1 and t0 + nb_ >= NTM)
            if last_of_chunk:
                for g0 in range(1, NTM, 8):
                    grp = list(range(g0, min(NTM, g0 + 8)))
                    bk = aux_bank()
                    pb = ps[bk].bitcast(BF16)
                    P.op("pe", lambda e: [e.transpose(out=pb[:, i * 128:(i + 1) * 128], in_=attc[ac][:, t, :], identity=ident_b[:, :])
                                          for i, t in enumerate(grp)][-1],
                         reads=[attc_res[ac], cres], writes=[ps_res[bk]])
                    P.op("act", lambda e: e.activation(out=oTa[ac][:, grp[0] * 128:(grp[-1] + 1) * 128], in_=pb[:, 0:len(grp) * 128], func=AF.Copy),
                         reads=[ps_res[bk]], writes=[oTa_res[ac]])
                P.dma(O[jq, :, :], oTa[ac][:, :], reads=[oTa_res[ac]], writes=[O_res[jq]])

        n_it = len(items)
        stage_sc(0)
        stage_exp(0)
        if n_it > 1:
            stage_sc(1)
        for idx in range(n_it):
            stage_pv(idx)
            if idx + 1 < n_it:
                stage_exp(idx + 1)
            if idx + 2 < n_it:
                stage_sc(idx + 2)
            stage_norm(idx)
        P.barrier()
        st3.close()

        ktm = sb("ktm", [128, KVW], stack=st)
        vtm2 = sb("vtm2", [128, KVW], stack=st)
        ksm = sb("ksm", [NS, KVW], stack=st)
        vsm = sb("vsm", [NS, KVW], stack=st)
        tm_res = Res()
        for (srcf, sres, dst, dsts) in ((knf, knf_res, ktm, ksm), (vf, vf_res, vtm2, vsm)):
            for kc_ in range(KVC):
                bk = aux_bank()
                P.op("pe", lambda e: e.transpose(out=ps[bk][:, 0:128], in_=srcf[:, kc_, 0:128], identity=ident_f[:, :]), reads=[sres, cres], writes=[ps_res[bk]])
                P.op("act", lambda e: e.activation(out=dst[:, kc_ * 128:(kc_ + 1) * 128], in_=ps[bk][:, 0:128], func=AF.Copy), reads=[ps_res[bk]], writes=[tm_res])
                bk = aux_bank()
                P.op("pe", lambda e: e.transpose(out=ps[bk][0:NS, 0:128], in_=srcf[:, kc_, 128:128 + NS], identity=ident_f[:, :]), reads=[sres, cres], writes=[ps_res[bk]])
                P.op("act", lambda e: e.activation(out=dsts[:, kc_ * 128:(kc_ + 1) * 128], in_=ps[bk][0:NS, 0:128], func=AF.Copy), reads=[ps_res[bk]], writes=[tm_res])
        P.dma(ck_p[:, :], ktm[:, :], reads=[tm_res], writes=[Res()])
        P.dma(cv_p[:, :], vtm2[:, :], reads=[tm_res], writes=[Res()])
        for s_ in range(NS):
            P.dma(ck_s[s_, 0:127, :], cache_k[s_, 1:128, :], writes=[Res()])
            P.dma(cv_s[s_, 0:127, :], cache_v[s_, 1:128, :], writes=[Res()])
        P.dma(ck_s[:, 127, :], ksm[:, :], reads=[tm_res], writes=[Res()])
        P.dma(cv_s[:, 127, :], vsm[:, :], reads=[tm_res], writes=[Res()])

        for kc in range(KC):
            P.dma(A[:, kc, :], O[kc, :, :], reads=[O_res[kc]], writes=[A_res[kc]])

        sel32 = sb("sel32", [NS, 128], stack=st)
        selC = sb("selC", [128, NS, NS], stack=st)
        sel_res = Res()
        P.op("dve", lambda e: e.memset(sel32[:, :], 0.0), writes=[sel_res])
        P.op("dve", lambda e: e.tensor_copy(out=sel32[:, :].rearrange("p (a b) -> p a b", b=32)[:, :, 0], in_=ident_f[0:NS, 0:NS]), reads=[cres], writes=[sel_res])
        P.op("dve", lambda e: e.tensor_copy(out=selC[:, :, :], in_=selrow[:, :, :]), reads=[cres], writes=[sel_res])
        selN = sb("selN", [128, NS], stack=st)
        bk = aux_bank()
        P.op("pe", lambda e: e.transpose(out=ps[bk][:, 0:NS], in_=sel32[0:NS, :], identity=ident_f[0:NS, 0:NS]), reads=[sel_res, cres], writes=[ps_res[bk]])
        P.op("act", lambda e: e.activation(out=selN[:, :], in_=ps[bk][:, 0:NS], func=AF.Copy), reads=[ps_res[bk]], writes=[sel_res])
        k32 = sb("k32", [128, KVW], stack=st)
        v32 = sb("v32", [128, KVW], stack=st)
        Kc = sb("Kc", [128, NS, KVW], stack=st)
        Vc = sb("Vc", [128, NS, KVW], stack=st)
        smp_res = Res()
        for (srct, dstb) in ((ksm, k32), (vsm, v32)):
            for c0 in range(0, KVW, 512):
                bk = aux_bank()
                cw = min(512, KVW - c0)
                P.op("pe", lambda e: e.matmul(ps[bk][:, 0:cw], lhsT=sel32[:, :], rhs=srct[:, c0:c0 + cw], start=True, stop=True),
                     reads=[sel_res, tm_res], writes=[ps_res[bk]])
                P.op("act", lambda e: e.activation(out=dstb[:, c0:c0 + cw], in_=ps[bk][:, 0:cw], func=AF.Copy), reads=[ps_res[bk]], writes=[smp_res])
        for s_ in range(NS):
            P.dma(Kc[:, s_, :], cache_k[s_, :, :], writes=[smp_res])
            P.dma(Vc[:, s_, :], cache_v[s_, :, :], writes=[smp_res])
        sbias = sb("sbias", [128, NQ], stack=st)
        slp = sb("slp", [128, NQ], stack=st)
        for hh in range(NQ):
            P.op("pool", lambda e, hh=hh: e.memset(slp[:, hh:hh + 1], -slopes[hh]), writes=[smp_res])
        P.op("dve", lambda e: e.tensor_scalar(out=sbias[:, :], in0=slp[:, :], scalar1=sdist[:, 0:1], scalar2=None, op0=ALU.mult), reads=[c1], writes=[smp_res])
        E_all = sb("E_all", [128, NS, NQ], stack=st)
        En = sb("En", [128, NQ], stack=st)
        prod = sb("prod", [128, 2, 64], stack=st)
        prodn = sb("prodn", [128, 2, 64], stack=st)
        qbc = sb("qbc", [128, 128], stack=st)
        ev = sb("ev", [128, 512], stack=st)
        evn = sb("evn", [128, 512], stack=st)
        dn_s = sb("dn_s", [NS, NQ], stack=st)
        at_s = sb("at_s", [NS, 512], stack=st)
        at_b = sb("at_b", [NS, 512], BF16, stack=st)
        P.op("dve", lambda e: e.memset(En[:, :], 0.0), writes=[smp_res])
        P.op("dve", lambda e: e.memset(evn[:, :], 0.0), writes=[smp_res])
        for s_ in range(NS):
            p0 = 32 * s_
            for jq in range(KC):
                kvh = (2 * jq) // G
                kvh1 = (2 * jq + 1) // G
                bk = aux_bank()
                P.op("pe", lambda e: e.matmul(ps[bk][:, 0:128], lhsT=qsT[:, jq, s_:s_ + 1].broadcast_to([128, 128]), rhs=ident_f[:, :], start=True, stop=True),
                     reads=[qsT_res, cres], writes=[ps_res[bk]])
                P.op("act", lambda e: e.activation(out=qbc[:, :], in_=ps[bk][:, 0:128], func=AF.Copy), reads=[ps_res[bk]], writes=[smp_res])
                for hh, kv in ((0, kvh), (1, kvh1)):
                    P.op("dve", lambda e: e.tensor_tensor(out=prod[:, hh, :], in0=qbc[:, hh * 64:(hh + 1) * 64], in1=Kc[:, s_, kv * 64:(kv + 1) * 64], op=ALU.mult),
                         writes=[smp_res])
                    P.op("dve", lambda e: e.tensor_tensor(out=prodn[p0:p0 + 1, hh, :], in0=qbc[p0:p0 + 1, hh * 64:(hh + 1) * 64], in1=k32[p0:p0 + 1, kv * 64:(kv + 1) * 64], op=ALU.mult),
                         writes=[smp_res])
                P.op("dve", lambda e: e.tensor_reduce(out=E_all[:, s_, 2 * jq:2 * jq + 2], in_=prod[:, :, :], axis=AX.X, op=ALU.add), writes=[smp_res])
                P.op("dve", lambda e: e.tensor_reduce(out=En[p0:p0 + 1, 2 * jq:2 * jq + 2], in_=prodn[p0:p0 + 1, :, :], axis=AX.X, op=ALU.add), writes=[smp_res])
            P.op("dve", lambda e: e.tensor_tensor(out=E_all[:, s_, :], in0=E_all[:, s_, :], in1=sbias[:, :], op=ALU.add), writes=[smp_res])
            P.op("act", lambda e: e.activation(out=E_all[:, s_, :], in_=E_all[:, s_, :], func=AF.Exp), writes=[smp_res])
            P.op("act", lambda e: e.activation(out=En[p0:p0 + 1, :], in_=En[p0:p0 + 1, :], func=AF.Exp), writes=[smp_res])
        bk = aux_bank()

        def fn_den(e):
            last = None
            for s_ in range(NS):
                e.matmul(ps[bk][0:NS, 0:NQ], lhsT=selC[:, s_, :], rhs=E_all[:, s_, :], start=(s_ == 0), stop=False)
            return e.matmul(ps[bk][0:NS, 0:NQ], lhsT=selN[:, :], rhs=En[:, :], start=False, stop=True)
        P.op("pe", fn_den, reads=[sel_res, smp_res], writes=[ps_res[bk]])
        P.op("dve", lambda e: e.tensor_tensor(out=dn_s[:, :], in0=ps[bk][0:NS, 0:NQ], in1=esink[0:NS, :], op=ALU.add), reads=[ps_res[bk], c1], writes=[smp_res])
        P.op("dve", lambda e: e.reciprocal(out=dn_s[:, :], in_=dn_s[:, :]), writes=[smp_res])
        HP = 512 // 64
        for c0 in range(0, D, 512):
            h0 = c0 // 64
            bkn = aux_bank()
            for s_ in range(NS):
                p0 = 32 * s_
                for hh in range(HP):
                    h = h0 + hh
                    kv = h // G
                    P.op("dve", lambda e: e.tensor_scalar(out=ev[:, hh * 64:(hh + 1) * 64], in0=Vc[:, s_, kv * 64:(kv + 1) * 64], scalar1=E_all[:, s_, h:h + 1], scalar2=None, op0=ALU.mult),
                         writes=[smp_res])
                    P.op("dve", lambda e: e.tensor_scalar(out=evn[p0:p0 + 1, hh * 64:(hh + 1) * 64], in0=v32[p0:p0 + 1, kv * 64:(kv + 1) * 64], scalar1=En[p0:p0 + 1, h:h + 1], scalar2=None, op0=ALU.mult),
                         writes=[smp_res])
                P.op("pe", lambda e: e.matmul(ps[bkn][0:NS, 0:512], lhsT=selC[:, s_, :], rhs=ev[:, :], start=(s_ == 0), stop=False),
                     reads=[sel_res, smp_res], writes=[ps_res[bkn]])
            P.op("pe", lambda e: e.matmul(ps[bkn][0:NS, 0:512], lhsT=selN[:, :], rhs=evn[:, :], start=False, stop=True),
                 reads=[sel_res, smp_res], writes=[ps_res[bkn]])
            P.op("dve", lambda e: e.tensor_tensor(out=at_b[:, :].rearrange("p (h d) -> p h d", d=64),
                                                  in0=ps[bkn][0:NS, 0:512].rearrange("p (h d) -> p h d", d=64),
                                                  in1=dn_s[:, h0:h0 + HP].unsqueeze(2).broadcast_to([NS, HP, 64]), op=ALU.mult),
                 reads=[ps_res[bkn]], writes=[smp_res])
            for i in range(4):
                kc = c0 // 128 + i
                bk2 = aux_bank()
                pb = ps[bk2].bitcast(BF16)
                P.op("pe", lambda e: e.transpose(out=pb[:, 0:NS], in_=at_b[0:NS, i * 128:(i + 1) * 128], identity=ident_b[0:NS, 0:NS]),
                     reads=[smp_res, cres], writes=[ps_res[bk2]])
                P.op("act", lambda e: e.activation(out=A[:, kc, TM:T], in_=pb[:, 0:NS], func=AF.Copy), reads=[ps_res[bk2]], writes=[A_res[kc]])
    P.barrier()
    st.close()
    cgsT = []
    c_ = 128
    while c_ < T:
        cgsT.append((c_, min(T, c_ + 512) if c_ + 512 <= TM else (TM if c_ < TM else T)))
        c_ = cgsT[-1][1]
    with ExitStack() as st:
        epi_r = resid_epi_factory(st, "a")
        gemm(A, A_res, cgsT, epi_r, expect_W=att_w_out)
    P.barrier()
    dump_R("R3")
    mlp_and_ple(1)
    dump_R("R4")

    with ExitStack() as st:
        rb = [sb(f"frb{i}", [128, T], stack=st) for i in range(2)]
        rb_res = RL(2)
        yt = [sb(f"yt{i}", [128, 512], stack=st) for i in range(2)]
        yt_res = RL(2)
        tl = [(t * 128, 128) for t in range(1, NTM)] + [(TM, NS)]
        cnt = 0
        for g0 in range(0, KC, 4):
            gn = min(4, KC - g0)
            rbs = []
            for i in range(gn):
                s = (g0 + i) % 2
            for (c0, n) in tl:
                bk = aux_bank()
                u = cnt % 2
                cnt += 1
                for i in range(gn):
                    s = i % 2
                    P.dma(rb[s][:, 0:n], R[g0 + i, :, c0:c0 + n], reads=[R_res[g0 + i]], writes=[rb_res[s]])
                    P.op("pe", lambda e, i=i, s=s: e.transpose(out=ps[bk][0:n, i * 128:(i + 1) * 128], in_=rb[s][:, 0:n], identity=ident_f[:, :]),
                         reads=[rb_res[s], cres], writes=[ps_res[bk]])
                P.op("act", lambda e: e.activation(out=yt[u][0:n, 0:gn * 128], in_=ps[bk][0:n, 0:gn * 128], func=AF.Copy), reads=[ps_res[bk]], writes=[yt_res[u]])
                if c0 < TM:
                    P.dma(y_main[c0 - 128:c0 - 128 + n, g0 * 128:(g0 + gn) * 128], yt[u][0:n, 0:gn * 128], reads=[yt_res[u]], writes=[Res()])
                else:
                    P.dma(y_smp[:, g0 * 128:(g0 + gn) * 128], yt[u][0:n, 0:gn * 128], reads=[yt_res[u]], writes=[Res()])
    P.barrier()
    assert job_ptr[0] == len(jobs), (job_ptr[0], len(jobs))
    assert ws_state["ptr"] == len(tiles)
    es.close()
    return nc


def host_consts(cfg, hf):
    NH = cfg.NH
    lg = np.log1p(-np.exp2(-5.0 - np.arange(NH, dtype=np.float64)))
    idx = np.arange(128, dtype=np.float64)
    diff = idx[None, :] - idx[:, None]
    decT = np.where(diff[None] >= 0, np.exp(np.maximum(diff, 0)[None] * lg[:, None, None]), 0.0).astype(np.float32)
    qdec = np.exp((idx + 1.0)[None, :] * lg[:, None]).astype(np.float32)
    kdec = np.exp((127.0 - idx)[:, None] * lg[None, :]).astype(np.float32)
    k = np.arange(128)[:, None]
    q = np.arange(128)[None, :]
    dist_prev = 128 + q - k
    dist_cur = q - k
    negdist = np.concatenate([-dist_prev, -np.maximum(dist_cur, 0)], axis=1).astype(np.float32)
    mask = np.concatenate([np.where(dist_prev <= 128, 0.0, NEG), np.where(dist_cur >= 0, 0.0, NEG)], axis=1).astype(np.float32)
    first = np.full((128, 1), NEG if hf == 0 else 0.0, np.float32)
    sdist = (128.0 - np.arange(128, dtype=np.float32)).reshape(128, 1)
    return {"c_ident": np.eye(128, dtype=np.float32), "c_decT": decT, "c_qdec": qdec, "c_kdec": kdec,
            "c_negdist": negdist, "c_mask": mask, "c_first": first, "c_sdist": sdist}


def make_in_maps(cfg, inp):
    f = lambda a: np.ascontiguousarray(np.asarray(a, dtype=np.float32))
    xp = f(inp["x_prompt"])
    xs = f(inp["x_sample"])
    pp = f(inp["p_prompt"])
    psm = f(inp["p_sample"])
    st = f(inp["state_ret"])
    ck = f(inp["cache_k"])
    cv = f(inp["cache_v"])
    OWN, NS, D = cfg.OWN, cfg.NS, cfg.D
    shared = {
        "norm_mix": f(inp["norm_mix"]), "norm_mlp": f(inp["norm_mlp"]), "norm_ple": f(inp["norm_ple"]),
        "ret_w_in": f(inp["ret_w_in"])[0], "ret_gn": f(inp["ret_gn_gain"])[0], "ret_w_out": f(inp["ret_w_out"])[0],
        "att_w_qkv": f(inp["attn_w_qkv"])[0], "att_b": f(inp["attn_b_qkv"])[0], "att_qn": f(inp["attn_q_norm"])[0],
        "att_kn": f(inp["attn_k_norm"])[0], "att_sinks": f(inp["attn_sinks"])[0], "att_w_out": f(inp["attn_w_out"])[0],
        "mlp_up": f(inp["mlp_w_up"]), "mlp_down": f(inp["mlp_w_down"]), "ple_gate": f(inp["ple_w_gate"]), "ple_proj": f(inp["ple_w_proj"]),
    }
    maps = []
    for c in range(8):
        b, hf = c // 2, c % 2
        m = dict(shared)
        if hf == 1:
            m["x_main"] = np.ascontiguousarray(xp[b, OWN - 128:2 * OWN])
            m["x_ctx"] = np.ascontiguousarray(xp[b, 0:max(OWN - 128, 1)])
            m["p_main"] = np.ascontiguousarray(pp[:, b, OWN - 128:2 * OWN])
        else:
            m["x_main"] = np.concatenate([np.zeros((128, D), np.float32), xp[b, 0:OWN]], axis=0)
            m["x_ctx"] = np.zeros((max(OWN - 128, 1), D), np.float32)
            m["p_main"] = np.concatenate([np.zeros((2, 128, 256), np.float32), pp[:, b, 0:OWN]], axis=1)
        sl = slice(c * NS, (c + 1) * NS)
        m["x_smp"] = np.ascontiguousarray(xs[sl, 0])
        m["p_smp"] = np.ascontiguousarray(psm[:, sl, 0])
        m["state_in"] = np.ascontiguousarray(st[0, sl])
        m["cache_k"] = np.ascontiguousarray(ck[0, sl].reshape(NS, 128, cfg.KVW))
        m["cache_v"] = np.ascontiguousarray(cv[0, sl].reshape(NS, 128, cfg.KVW))
        m.update(host_consts(cfg, hf))
        maps.append(m)
    return maps


def assemble(cfg, res):
    B, OWN, NS, D, NH, NKV = cfg.B, cfg.OWN, cfg.NS, cfg.D, cfg.NH, cfg.NKV
    y_p = np.zeros((B, 2 * OWN, D), np.float32)
    y_s = np.zeros((8 * NS, 1, D), np.float32)
    sp = np.zeros((1, B, NH, 256, 512), np.float32)
    ss = np.zeros((1, 8 * NS, NH, 256, 512), np.float32)
    ckp = np.zeros((1, B, 128, NKV, 64), np.float32)
    cvp = np.zeros((1, B, 128, NKV, 64), np.float32)
    cks = np.zeros((1, 8 * NS, 128, NKV, 64), np.float32)
    cvs = np.zeros((1, 8 * NS, 128, NKV, 64), np.float32)
    for c in range(8):
        b, hf = c // 2, c % 2
        r = res[c]
        y_p[b, hf * OWN:(hf + 1) * OWN] = r["y_main"]
        sl = slice(c * NS, (c + 1) * NS)
        y_s[sl, 0] = r["y_smp"]
        ss[0, sl] = r["st_s"]
        cks[0, sl] = r["ck_s"].reshape(NS, 128, NKV, 64)
        cvs[0, sl] = r["cv_s"].reshape(NS, 128, NKV, 64)
        if hf == 1:
            sp[0, b] = r["st_p"]
            ckp[0, b] = r["ck_p"].reshape(128, NKV, 64)
            cvp[0, b] = r["cv_p"].reshape(128, NKV, 64)
    return (y_p, y_s, sp, ss, ckp, cvp, cks, cvs)


_NC_CACHE = {}


def kernel(**inputs):
    cfg = Cfg()
    if "nc" not in _NC_CACHE:
        _NC_CACHE["nc"] = build(cfg)
    nc = _NC_CACHE["nc"]
    maps = make_in_maps(cfg, inputs)
    res = run_bass_kernel_spmd(nc, maps, core_ids=list(range(8)))
    return assemble(cfg, res.results)
```
